# Optimizing a Trainium2 kernel written in Bass

```python
import math
import jax
import jax.numpy as jnp
from jax import lax
import numpy as np

D_MODEL = 1024
BATCH = 8
SEQ = 2048
DEPTH = 2
DEC_BATCH = 32
DEC_SEQ = 4
PAST_LEN = 8192
PAGE_SIZE = 128

EPS = 1e-6
NEG = -1e30
LB_FLOOR = 1e-30
D_A = D_MODEL // 2
CONV_W = 31
N_HEADS = 8
N_KV_HEADS = 2
HEAD_DIM = 64
D_B = N_HEADS * HEAD_DIM
KV_DIM = N_KV_HEADS * HEAD_DIM
IDX_HEADS = 4
IDX_DIM = 64
TOPK_MAX = 256
ROPE_THETA = 10000.0
Q_BLOCK = 128
C_HEADS = 4
C_KDIM = 128
C_VDIM = D_MODEL // 2 // C_HEADS
D_C = C_HEADS * C_VDIM
C_FDIM = C_HEADS * C_KDIM
CHUNK = 64

IN_SPLITS = (D_A, D_A, D_A,
             D_B, KV_DIM, KV_DIM, IDX_HEADS * IDX_DIM, IDX_DIM, IDX_HEADS, D_B,
             C_FDIM, C_FDIM, D_C, D_C,
             D_MODEL, D_MODEL, D_MODEL)
N_IN = sum(IN_SPLITS)

kernel_name = 'hybrid_conv_dsa_hgrn2_step'


def _rmsnorm(x, g=None):
    xf = x.astype(jnp.float32)
    y = xf * lax.rsqrt(jnp.mean(xf * xf, axis=-1, keepdims=True) + EPS)
    if g is not None:
        y = y * g.astype(jnp.float32)
    return y.astype(x.dtype)


def _layernorm(x, g, b):
    xf = x.astype(jnp.float32)
    mu = jnp.mean(xf, axis=-1, keepdims=True)
    var = jnp.mean(jnp.square(xf - mu), axis=-1, keepdims=True)
    y = (xf - mu) * lax.rsqrt(var + EPS) * g.astype(jnp.float32) + b.astype(jnp.float32)
    return y.astype(x.dtype)


def _rope(x, pos):
    half = x.shape[-1] // 2
    inv = ROPE_THETA ** (-jnp.arange(half, dtype=jnp.float32) / half)
    ang = pos.astype(jnp.float32)[:, None] * inv[None, :]
    cos = jnp.cos(ang)[None, :, None, :]
    sin = jnp.sin(ang)[None, :, None, :]
    xf = x.astype(jnp.float32)
    x1, x2 = xf[..., :half], xf[..., half:]
    return jnp.concatenate([x1 * cos - x2 * sin, x2 * cos + x1 * sin], axis=-1).astype(x.dtype)


def _split_in(z):
    offs = np.cumsum(IN_SPLITS)[:-1].tolist()
    return jnp.split(z, offs, axis=-1)


def _gather_rows(a, idx):
    return jax.vmap(lambda ab, ib: ab[ib])(a, idx)


def _indexer_scores(qi, ki, wi):
    s = jnp.einsum('bqhd,bsd->bqhs', qi, ki).astype(jnp.float32) * (IDX_DIM ** -0.5)
    return jnp.einsum('bqhs,bqh->bqs', jax.nn.relu(s), wi.astype(jnp.float32)) * (IDX_HEADS ** -0.5)


def _sparse_attend(q, k_sel, v_sel, valid):
    B, T = q.shape[:2]
    qg = q.reshape(B, T, N_KV_HEADS, N_HEADS // N_KV_HEADS, HEAD_DIM)
    s = jnp.einsum('btngd,btsnd->btngs', qg, k_sel).astype(jnp.float32) * (HEAD_DIM ** -0.5)
    s = jnp.where(valid[:, :, None, None, :], s, NEG)
    pr = jax.nn.softmax(s, axis=-1).astype(v_sel.dtype)
    o = jnp.einsum('btngs,btsnd->btngd', pr, v_sel)
    return o.reshape(B, T, D_B)


def _dsa_prompt(q, k, v, qi, ki, wi):
    B, T = q.shape[:2]
    topk = min(TOPK_MAX, T // 4)
    nb = T // Q_BLOCK
    key_pos = jnp.arange(T)

    def blk(xs):
        qb, qib, wib, qpos = xs
        scores = _indexer_scores(qib, ki, wib)
        scores = jnp.where(key_pos[None, None, :] <= qpos[None, :, None], scores, NEG)
        _, idx = lax.top_k(scores, topk)
        valid = idx <= qpos[None, :, None]
        return _sparse_attend(qb, _gather_rows(k, idx), _gather_rows(v, idx), valid)

    def to_blocks(a):
        return a.reshape(B, nb, Q_BLOCK, *a.shape[2:]).swapaxes(0, 1)

    out = lax.map(blk, (to_blocks(q), to_blocks(qi), to_blocks(wi), key_pos.reshape(nb, Q_BLOCK)))
    return out.swapaxes(0, 1).reshape(B, T, D_B)


def _dsa_sample(q, k, v, qi, ki, wi, ck, cv, cki, page_table):
    B, T = q.shape[:2]
    past_len = page_table.shape[1] * PAGE_SIZE
    L = past_len + T
    topk = min(TOPK_MAX, L // 4)
    ki_past = cki[page_table].reshape(B, past_len, IDX_DIM).astype(ki.dtype)
    ki_all = jnp.concatenate([ki_past, ki], axis=1)
    qpos = past_len + jnp.arange(T)
    scores = _indexer_scores(qi, ki_all, wi)
    scores = jnp.where(jnp.arange(L)[None, None, :] <= qpos[None, :, None], scores, NEG)
    _, idx = lax.top_k(scores, topk)
    valid = idx <= qpos[None, :, None]
    is_new = (idx >= past_len)[..., None, None]
    idx_past = jnp.minimum(idx, past_len - 1)
    rows = _gather_rows(page_table, idx_past // PAGE_SIZE) * PAGE_SIZE + idx_past % PAGE_SIZE
    idx_new = jnp.clip(idx - past_len, 0, T - 1)
    ck_flat = ck.reshape(-1, N_KV_HEADS, HEAD_DIM)
    cv_flat = cv.reshape(-1, N_KV_HEADS, HEAD_DIM)
    k_sel = jnp.where(is_new, _gather_rows(k, idx_new), ck_flat[rows].astype(k.dtype))
    v_sel = jnp.where(is_new, _gather_rows(v, idx_new), cv_flat[rows].astype(v.dtype))
    return _sparse_attend(q, k_sel, v_sel, valid)


def _conv_module(u, buf, w_dw, b_dw, ln_g, ln_b):
    full = jnp.concatenate([buf.astype(u.dtype), u], axis=1)
    y = lax.conv_general_dilated(full, w_dw[:, None, :].astype(full.dtype), window_strides=(1,),
                                 padding='VALID', dimension_numbers=('NWC', 'WIO', 'NWC'),
                                 feature_group_count=u.shape[-1])
    y = jax.nn.silu(_layernorm(y + b_dw, ln_g, ln_b))
    return y, full[:, -(CONV_W - 1):]


def _hgrn2(q, log_f, kk, i, S0):
    B, T, H, dk = q.shape
    dv = i.shape[-1]
    C = math.gcd(T, CHUNK)
    N = T // C

    def chunks(a):
        return a.astype(jnp.float32).reshape(B, N, C, H, a.shape[-1]).transpose(1, 0, 3, 2, 4)

    causal = jnp.tril(jnp.ones((C, C), dtype=bool))

    def step(S, xs):
        qc, lfc, kc, ic = xs
        b = jnp.cumsum(lfc, axis=2)
        inter = jnp.einsum('bhtd,bhdv->bhtv', qc * jnp.exp(b), S)
        diff = b[:, :, :, None, :] - b[:, :, None, :, :]
        decay = jnp.exp(jnp.where(causal[None, None, :, :, None], diff, NEG))
        att = jnp.einsum('bhtd,bhtsd,bhsd->bhts', qc, decay, kc)
        o = inter + jnp.einsum('bhts,bhsv->bhtv', att, ic)
        b_end = b[:, :, -1:, :]
        S_new = jnp.exp(b_end[:, :, 0, :])[..., None] * S + jnp.einsum(
            'bhsd,bhsv->bhdv', kc * jnp.exp(b_end - b), ic)
        return S_new, o

    S, o = lax.scan(step, S0.astype(jnp.float32), (chunks(q), chunks(log_f), chunks(kk), chunks(i)))
    o = o.transpose(1, 0, 3, 2, 4).reshape(B, T, H, dv)
    return o, S


def _layer(x, c, pos, p, lb, conv_buf, S0, attend):
    B, T, _ = x.shape
    mod = jax.nn.silu(c) @ p['w_ada'] + p['b_ada']
    shift, scale, gate = jnp.split(mod, 3, axis=-1)
    h = _rmsnorm(x, p['norm_g']) * (1 + scale[:, None, :]) + shift[:, None, :]
    (a_val, a_glu, a_gate, zq, zk, zv, zqi, zki, zwi, b_gate,
     cq, cf, ci, c_gate, g_a, g_b, g_c) = _split_in(h @ p['w_in'])

    u = a_val * jax.nn.sigmoid(a_glu)
    ya, new_buf = _conv_module(u, conv_buf, p['w_dw'], p['b_dw'], p['ln_g'], p['ln_b'])
    ya = ya * jax.nn.silu(a_gate)

    q = _rope(_rmsnorm(zq.reshape(B, T, N_HEADS, HEAD_DIM), p['q_norm_g']), pos)
    k = _rope(_rmsnorm(zk.reshape(B, T, N_KV_HEADS, HEAD_DIM), p['k_norm_g']), pos)
    v = zv.reshape(B, T, N_KV_HEADS, HEAD_DIM)
    qi = _rope(zqi.reshape(B, T, IDX_HEADS, IDX_DIM), pos)
    ki = _rope(_rmsnorm(zki)[:, :, None, :], pos)[:, :, 0, :]
    yb = attend(q, k, v, qi, ki, zwi) * jax.nn.silu(b_gate)

    fx = cf.reshape(B, T, C_HEADS, C_KDIM).astype(jnp.float32)
    log_f = jnp.logaddexp(jnp.log(jnp.maximum(lb, LB_FLOOR)), jnp.log1p(-lb) + jax.nn.log_sigmoid(fx))
    kk = (1.0 - lb) * jax.nn.sigmoid(-fx)
    qc = jax.nn.silu(cq).reshape(B, T, C_HEADS, C_KDIM)
    o, S_new = _hgrn2(qc, log_f, kk, ci.reshape(B, T, C_HEADS, C_VDIM), S0)
    yc = _rmsnorm(o, p['c_norm_g']).astype(x.dtype).reshape(B, T, D_C) * jax.nn.silu(c_gate)

    m = (jax.nn.sigmoid(g_a) * (ya @ p['w_proj_a'])
         + jax.nn.sigmoid(g_b) * (yb @ p['w_proj_b'])
         + jax.nn.sigmoid(g_c) * (yc @ p['w_proj_c']))
    y = x + gate[:, None, :] * (m @ p['w_out'])
    return y, k, v, ki, new_buf, S_new


def setup_inputs(seed: int = 0) -> dict:
    key = jax.random.key(seed)
    ks = jax.random.split(key, 32)
    n_pages = PAST_LEN // PAGE_SIZE
    n_pool = (DEC_BATCH * n_pages * 5) // 4

    def nrm(k, shape, s):
        return jax.random.normal(k, shape, jnp.float32) * s

    page_table = jax.random.permutation(ks[7], n_pool)[: DEC_BATCH * n_pages]
    page_table = page_table.reshape(DEC_BATCH, n_pages).astype(jnp.int32)
    return {
        'x_prompt': nrm(ks[0], (BATCH, SEQ, D_MODEL), 1.0),
        'x_sample': nrm(ks[1], (DEC_BATCH, DEC_SEQ, D_MODEL), 1.0),
        'cache_k': nrm(ks[2], (DEPTH, n_pool, PAGE_SIZE, N_KV_HEADS, HEAD_DIM), 1.0),
        'cache_v': nrm(ks[3], (DEPTH, n_pool, PAGE_SIZE, N_KV_HEADS, HEAD_DIM), 1.0),
        'cache_idx_k': nrm(ks[4], (DEPTH, n_pool, PAGE_SIZE, IDX_DIM), 1.0),
        'state_conv': nrm(ks[5], (DEPTH, DEC_BATCH, CONV_W - 1, D_A), 0.5),
        'state_hgrn': nrm(ks[6], (DEPTH, DEC_BATCH, C_HEADS, C_KDIM, C_VDIM), 0.5),
        'page_table': page_table,
        'c_prompt': nrm(ks[8], (BATCH, D_MODEL), 1.0),
        'c_sample': nrm(ks[9], (DEC_BATCH, D_MODEL), 1.0),
        'w_ada': nrm(ks[10], (DEPTH, D_MODEL, 3 * D_MODEL), 0.5 * D_MODEL ** -0.5),
        'b_ada': nrm(ks[11], (DEPTH, 3 * D_MODEL), 0.02),
        'norm_g': 1.0 + nrm(ks[12], (DEPTH, D_MODEL), 0.02),
        'w_in': nrm(ks[13], (DEPTH, D_MODEL, N_IN), D_MODEL ** -0.5),
        'w_dw': nrm(ks[14], (DEPTH, CONV_W, D_A), CONV_W ** -0.5),
        'b_dw': nrm(ks[15], (DEPTH, D_A), 0.02),
        'ln_g': 1.0 + nrm(ks[16], (DEPTH, D_A), 0.02),
        'ln_b': nrm(ks[17], (DEPTH, D_A), 0.02),
        'q_norm_g': 1.0 + nrm(ks[18], (DEPTH, HEAD_DIM), 0.02),
        'k_norm_g': 1.0 + nrm(ks[19], (DEPTH, HEAD_DIM), 0.02),
        'lb_logits': nrm(ks[20], (DEPTH, C_FDIM), 0.5),
        'c_norm_g': 1.0 + nrm(ks[21], (DEPTH, C_VDIM), 0.02),
        'w_proj_a': nrm(ks[22], (DEPTH, D_A, D_MODEL), D_A ** -0.5),
        'w_proj_b': nrm(ks[23], (DEPTH, D_B, D_MODEL), D_B ** -0.5),
        'w_proj_c': nrm(ks[24], (DEPTH, D_C, D_MODEL), D_C ** -0.5),
        'w_out': nrm(ks[25], (DEPTH, D_MODEL, D_MODEL), D_MODEL ** -0.5),
    }


def reference(x_prompt, x_sample, cache_k, cache_v, cache_idx_k, state_conv, state_hgrn, page_table,
              c_prompt, c_sample, w_ada, b_ada, norm_g, w_in, w_dw, b_dw, ln_g, ln_b, q_norm_g, k_norm_g,
              lb_logits, c_norm_g, w_proj_a, w_proj_b, w_proj_c, w_out):
    lbp = jax.nn.softmax(lb_logits.astype(jnp.float32), axis=0)
    lb_all = jnp.cumsum(lbp, axis=0) - lbp[0:1]
    Bp, Tp = x_prompt.shape[:2]
    pos_p = jnp.arange(Tp)
    pos_s = page_table.shape[1] * PAGE_SIZE + jnp.arange(x_sample.shape[1])
    buf0 = jnp.zeros((Bp, CONV_W - 1, D_A), x_prompt.dtype)
    S00 = jnp.zeros((Bp, C_HEADS, C_KDIM, C_VDIM), jnp.float32)

    xp, xs = x_prompt, x_sample
    kp_l, vp_l, kip_l, bp_l, sp_l = [], [], [], [], []
    ks_l, vs_l, kis_l, bs_l, ss_l = [], [], [], [], []
    for l in range(DEPTH):
        p = {'w_ada': w_ada[l], 'b_ada': b_ada[l], 'norm_g': norm_g[l], 'w_in': w_in[l],
             'w_dw': w_dw[l], 'b_dw': b_dw[l], 'ln_g': ln_g[l], 'ln_b': ln_b[l],
             'q_norm_g': q_norm_g[l], 'k_norm_g': k_norm_g[l], 'c_norm_g': c_norm_g[l],
             'w_proj_a': w_proj_a[l], 'w_proj_b': w_proj_b[l], 'w_proj_c': w_proj_c[l], 'w_out': w_out[l]}
        lb = lb_all[l].reshape(C_HEADS, C_KDIM)

        xp, kp, vp, kip, bp, sp = _layer(xp, c_prompt, pos_p, p, lb, buf0, S00, _dsa_prompt)

        def attend_s(q, k, v, qi, ki, wi, l=l):
            return _dsa_sample(q, k, v, qi, ki, wi, cache_k[l], cache_v[l], cache_idx_k[l], page_table)

        xs, ksm, vsm, kism, bsm, ssm = _layer(xs, c_sample, pos_s, p, lb, state_conv[l], state_hgrn[l], attend_s)

        kp_l.append(kp); vp_l.append(vp); kip_l.append(kip); bp_l.append(bp); sp_l.append(sp.astype(x_prompt.dtype))
        ks_l.append(ksm); vs_l.append(vsm); kis_l.append(kism); bs_l.append(bsm); ss_l.append(ssm.astype(state_hgrn.dtype))

    k_prompt = jnp.stack(kp_l)
    v_prompt = jnp.stack(vp_l)
    idxk_prompt = jnp.stack(kip_l)
    conv_prompt = jnp.stack(bp_l)
    hgrn_prompt = jnp.stack(sp_l)
    k_sample = jnp.stack(ks_l)
    v_sample = jnp.stack(vs_l)
    idxk_sample = jnp.stack(kis_l)
    conv_sample = jnp.stack(bs_l)
    hgrn_sample = jnp.stack(ss_l)
    return (xp, xs, k_prompt, v_prompt, idxk_prompt, conv_prompt, hgrn_prompt,
            k_sample, v_sample, idxk_sample, conv_sample, hgrn_sample)
```

```python
import math
from contextlib import ExitStack

import numpy as np
import ml_dtypes

import concourse.bass as bass
import concourse.mybir as mybir
from concourse.bass_utils import run_bass_kernel_spmd

F32 = mybir.dt.float32
BF16 = mybir.dt.bfloat16
I32 = mybir.dt.int32
AF = mybir.ActivationFunctionType
ALU = mybir.AluOpType
AX = mybir.AxisListType

D = 1024
T = 2048
TS = 1024
NTT = 8
NS = 16
TW = TS + NS
DEPTH = 2
N_IN = 8260
EPS = 1e-6
PAST = 8192
NPG = 64
NPOOL = 2560
C_A_VAL, C_A_GLU, C_A_GATE = 0, 512, 1024
C_ZQ, C_ZK, C_ZV, C_ZQI, C_ZKI, C_ZWI, C_BG = 1536, 2048, 2176, 2304, 2560, 2624, 2628
C_CQ, C_CF, C_CI, C_CG = 3140, 3652, 4164, 4676
C_GA, C_GB, C_GC = 5188, 6212, 7236
NEGM = -1.0e30


class Buf:
    __slots__ = ("name", "lw", "rd", "dsem", "dcnt")

    def __init__(self, name):
        self.name = name
        self.lw = {}
        self.rd = {}
        self.dsem = None
        self.dcnt = 0


class Eng:
    def __init__(self, name, h, sem):
        self.name, self.h, self.sem = name, h, sem
        self.count = 0
        self.waited = {}


def _put(d, tok):
    k = id(tok[0])
    if k not in d or d[k][1] < tok[1]:
        d[k] = tok


class Trk:
    def __init__(self, nc, stack):
        self.nc, self.stack = nc, stack
        self.engs = {}
        for name, h in (("pe", nc.tensor), ("act", nc.scalar), ("dve", nc.vector),
                        ("pool", nc.gpsimd), ("sp", nc.sync)):
            self.engs[name] = Eng(name, h, stack.enter_context(nc.semaphore("s_" + name)))
        self.out_tokens = {}
        self.pend_dma = {}
        self.ninstr = 0
        self.nsem = 5

    def _need(self, e, toks):
        need = {}
        for s, v in toks:
            k = id(s)
            if e.waited.get(k, 0) >= v:
                continue
            if k not in need or need[k][1] < v:
                need[k] = (s, v)
        for k, (s, v) in need.items():
            e.h.wait_ge(s, v)
            e.waited[k] = v
            self.ninstr += 1

    def _collect(self, e, reads, writes, part):
        toks = []
        for b in reads:
            toks.extend(b.lw.values())
        for b in writes:
            toks.extend(b.rd.values())
            toks.extend(b.lw.values())
        for b in part:
            toks.extend(b.rd.values())
            for k, t in b.lw.items():
                if k != id(e.sem):
                    toks.append(t)
        return toks

    def _update(self, tok, reads, writes, part):
        for b in writes:
            b.lw = {id(tok[0]): tok}
            b.rd = {}
        for b in part:
            _put(b.lw, tok)
        for b in reads:
            _put(b.rd, tok)

    def op(self, eng, fn, reads=(), writes=(), part=(), inc=True):
        e = self.engs[eng]
        self._need(e, self._collect(e, reads, writes, part))
        ins = fn(e.h)
        self.ninstr += 1
        if inc:
            e.count += 1
            ins.then_inc(e.sem, 1)
            tok = (e.sem, e.count)
        else:
            tok = (e.sem, e.count + 1)
        self._update(tok, reads, writes, part)
        return ins

    def dma(self, eng, out, in_, reads=(), writes=(), part=(), own=None, is_output=False, fn=None, nobar=False, **kw):
        e = self.engs[eng]
        if own is None:
            own = (list(writes) + list(part) + list(reads))[0]
        if own.dsem is None:
            own.dsem = self.stack.enter_context(self.nc.semaphore(f"d{self.nsem}_" + own.name))
            self.nsem += 1
        self._need(e, self._collect(e, reads, writes, part))
        if fn is not None:
            ins = fn(e.h)
        else:
            ins = e.h.dma_start(out=out, in_=in_, **kw)
        own.dcnt += 1
        ins.then_inc(own.dsem, 16)
        self.ninstr += 1
        tok = (own.dsem, 16 * own.dcnt)
        self._update(tok, reads, writes, part)
        if not nobar:
            _put(self.pend_dma, tok)
        if is_output:
            _put(self.out_tokens, tok)
        return ins

    def barrier(self):
        sp = self.engs["sp"]
        toks = [(e.sem, e.count) for e in self.engs.values() if e is not sp and e.count > 0]
        toks += list(self.pend_dma.values())
        self._need(sp, toks)
        self.pend_dma = {}
        ins = sp.h.nop()
        sp.count += 1
        ins.then_inc(sp.sem, 1)
        for e in self.engs.values():
            if e is not sp:
                self._need(e, [(sp.sem, sp.count)])

    def finish(self):
        self._need(self.engs["sp"], list(self.out_tokens.values()))


def _consts():
    c = {}
    c["ident_f"] = np.eye(128, dtype=np.float32)
    inv = (10000.0 ** (-np.arange(32, dtype=np.float32) / 32)).astype(np.float32)
    pos = np.concatenate([np.arange(T), PAST + np.arange(4)]).astype(np.float32)
    ang = pos[:, None] * inv[None, :]
    cos = np.cos(ang).astype(np.float32)
    sin = np.sin(ang).astype(np.float32)
    c["cos_p"] = np.ascontiguousarray(cos[:T].reshape(16, 128, 32).transpose(1, 0, 2))
    c["sin_p"] = np.ascontiguousarray(sin[:T].reshape(16, 128, 32).transpose(1, 0, 2))
    c["cos_s"] = np.ascontiguousarray(np.tile(cos[T:], (4, 1)))
    c["sin_s"] = np.ascontiguousarray(np.tile(sin[T:], (4, 1)))
    tt = np.arange(128)
    c["caus_neg"] = np.where(tt[None, :] <= tt[:, None], 0.0, NEGM).astype(np.float32)
    c["triT"] = (tt[:, None] <= tt[None, :]).astype(np.float32)
    selp = np.zeros((5, 128), np.float32); selp[0, :] = 1
    sels = np.zeros((5, 128), np.float32)
    for m in range(16):
        sels[1 + m // 4, m] = 1
    c["selp"] = selp
    c["pw2"] = np.tile((2.0 ** -np.arange(32, dtype=np.float32))[None, :], (128, 1)).astype(np.float32)
    c["sels"] = sels
    p = np.arange(128)
    c["roff"] = (p % 16).astype(np.float32)[:, None]
    nm = np.full((128, 4, 16), NEGM, np.float32)
    for b in range(4):
        for t in range(4):
            for s_ in range(t + 1):
                nm[t, b, 4 * b + s_] = 0.0
    c["newmask"] = nm
    c["grp4"] = (p[:, None] % 4 == p[None, :] % 4).astype(np.float32)
    rep = np.zeros((16, 4, 128), np.float32)
    for b in range(4):
        for m in range(128):
            rep[4 * b + m % 4, b, m] = 1.0
    c["rep4"] = rep
    sel16 = np.zeros((128, 4, 16), np.float32)
    for b in range(4):
        sel16[:, b, 4 * b:4 * b + 4] = 1.0
    c["sel16"] = sel16
    bdm = np.zeros((16, 16), np.float32)
    for b in range(4):
        for t in range(4):
            for s_ in range(t + 1):
                bdm[4 * b + s_, 4 * b + t] = 1.0
    c["bdm"] = bdm
    rms = np.ones((128, 16), np.float32); rms[:, 0::4] = 0.0
    c["rmS"] = rms
    rmk = np.zeros((16, 4), np.float32)
    for b in range(4):
        rmk[4 * b:4 * b + 4, b] = 1.0
    c["rowmask"] = rmk
    return c


def build(cfg):
    nc = bass.Bass("TRN2", target_bir_lowering=False)
    NL = cfg.get("layers", DEPTH)
    do_sample = cfg.get("sample", True)
    stage = cfg.get("stage", 99)

    def din(name, shape, dt=F32):
        return nc.dram_tensor(name, list(shape), dt, kind="ExternalInput").ap()

    def dout(name, shape, dt=F32):
        return nc.dram_tensor(name, list(shape), dt, kind="ExternalOutput").ap()

    xp = din("xp", [T, D]); xs = din("xs", [NS, D]); cc = din("cc", [5, D])
    w_ada = din("w_ada", [DEPTH, D, 3 * D]); b_adaT = din("b_adaT", [DEPTH, 128, 24]); b_ada_g = din("b_ada_g", [DEPTH, 1, D])
    norm_gT = din("norm_gT", [DEPTH, 128, 8])
    w_in = din("w_in", [DEPTH, D, N_IN])
    qg_bc = din("qg_bc", [DEPTH, 1, 64]); kg_bc = din("kg_bc", [DEPTH, 1, 64])
    w_dwT = din("w_dwT", [DEPTH, 128, 4, 31]); b_dwT = din("b_dwT", [DEPTH, 128, 4])
    ln_gT = din("ln_gT", [DEPTH, 128, 4]); ln_bT = din("ln_bT", [DEPTH, 128, 4])
    w_pa = din("w_pa", [DEPTH, 512, D]); w_pb = din("w_pb", [DEPTH, 512, D]); w_pc = din("w_pc", [DEPTH, 512, D])
    w_out = din("w_out", [DEPTH, D, D])
    lbT_in = din("lbT", [DEPTH, 128, 4]); cng_in = din("cng", [DEPTH, 128, 1])
    w_c = din("w_c", [DEPTH, D, 4, 512])
    if do_sample:
        ptx_in = din("ptx", [128, 4, 8], I32)
        ck_in = [din(f"ck{i}", [NPOOL * 128, 128]) for i in range(DEPTH)]; cv_in = [din(f"cv{i}", [NPOOL * 128, 128]) for i in range(DEPTH)]
        cik_in = [din(f"cik{i}", [NPOOL * 128, 64]) for i in range(DEPTH)]
        sconv_in = din("sconv", [DEPTH, 4, 30, 512]); shg_in = din("shg", [DEPTH, 4, 4, 128, 128])
    cst = {k: din("c_" + k, v.shape) for k, v in _consts().items()}
    kp = dout("kp", [DEPTH, T, 128]); vp = dout("vp", [DEPTH, T, 128]); ikp = dout("ikp", [DEPTH, T, 64])
    ks = dout("ks", [DEPTH, NS, 128]); vs = dout("vs", [DEPTH, NS, 128]); iks = dout("iks", [DEPTH, NS, 64])
    yp = dout("yp", [T, D]); ys = dout("ys", [NS, D])
    convp = dout("convp", [DEPTH, 30, 512])
    hgp = dout("hgp", [DEPTH, 4, 128, 128])
    convs = dout("convs", [DEPTH, 4, 30, 512]); hgs = dout("hgs", [DEPTH, 4, 4, 128, 128])
    dbg = dout("dbg", [2, 128, 4 * TW]) if cfg.get("dbg") else None

    with ExitStack() as st:
        Tk = Trk(nc, st)
        op, dma = Tk.op, Tk.dma

        def sb(name, shape, dt=F32):
            return st.enter_context(nc.sbuf_tensor(name, list(shape), dt))

        def B(name):
            return Buf(name)

        PS = [st.enter_context(nc.psum_tensor(f"ps{i}", [128, 512], F32)) for i in range(8)]
        bPS = [B(f"ps{i}") for i in range(8)]
        rot = {"i": 0}

        def psum(lo=0, hi=4):
            k = lo + rot["i"] % (hi - lo)
            rot["i"] += 1
            return PS[k], bPS[k]

        ident_f = sb("ident_f", [128, 128]); ident_b = sb("ident_b", [128, 128], BF16)
        ones_b = sb("ones_b", [128, 128], BF16)
        eps_t = sb("eps_t", [128, 1]); bC = B("consts")
        cos_p = sb("cos_p", [128, 16, 32]); sin_p = sb("sin_p", [128, 16, 32])
        cos_s = sb("cos_s", [NS, 32]); sin_s = sb("sin_s", [NS, 32])
        caus_neg = sb("caus_neg", [128, 128]); triT = sb("triT", [128, 128], BF16)
        selp = sb("selp", [5, 128]); sels = sb("sels", [5, 128])
        dma("sp", ident_f[:], cst["ident_f"], part=[bC])
        dma("pool", ident_b[:], cst["ident_f"], part=[bC])
        dma("pool", triT[:], cst["triT"], part=[bC])
        dma("sp", cos_p[:], cst["cos_p"], part=[bC]); dma("sp", sin_p[:], cst["sin_p"], part=[bC])
        dma("sp", cos_s[:], cst["cos_s"], part=[bC]); dma("sp", sin_s[:], cst["sin_s"], part=[bC])
        dma("sp", caus_neg[:], cst["caus_neg"], part=[bC])
        dma("sp", selp[:], cst["selp"], part=[bC]); dma("sp", sels[:], cst["sels"], part=[bC])
        op("pool", lambda e: e.memset(eps_t[:], EPS), part=[bC])
        op("pool", lambda e: e.memset(ones_b[:], 1.0), part=[bC])
        qg = sb("qg", [128, DEPTH, 64]); kg = sb("kg", [128, DEPTH, 64])
        for l in range(DEPTH):
            dma("sp", qg[:, l, :], qg_bc[l].to_broadcast([128, 64]), part=[bC])
            dma("sp", kg[:, l, :], kg_bc[l].to_broadcast([128, 64]), part=[bC])
        ngT = sb("ngT", [128, DEPTH, 8]); badT = sb("badT", [128, DEPTH, 24])
        for l in range(DEPTH):
            dma("sp", ngT[:, l, :], norm_gT[l], part=[bC]); dma("sp", badT[:, l, :], b_adaT[l], part=[bC])

        NWB = 3
        WB = [sb(f"wb{i}", [128, 4096], BF16) for i in range(NWB)]
        bWB = [B(f"wb{i}") for i in range(NWB)]
        wrot = {"i": 0}

        wcache = {}

        def wload(src, kc, ncols, key=None):
            if key is not None and key in wcache:
                return wcache.pop(key)
            k = wrot["i"] % NWB
            wrot["i"] += 1
            view = WB[k][:, 0:kc * ncols].rearrange("p (k n) -> p k n", n=ncols)
            dma("pool", view, src.rearrange("(k p) n -> p k n", p=128), writes=[bWB[k]], nobar=True)
            return view, bWB[k]

        def wprefetch(key, src, kc, ncols):
            wcache[key] = wload(src, kc, ncols)

        X = sb("X", [128, NTT, D]); bX = [B(f"X{i}") for i in range(NTT)]
        XS = sb("XS", [NS, D]); bXS = B("XS")
        hT = sb("hT", [128, 8, TW], BF16); bhT = [B(f"hT{i}") for i in range(NTT + 1)]
        cT = sb("cT", [128, 8, 5], BF16); bcT = B("cT")
        modT = sb("modT", [128, DEPTH, 24, 5]); bmod = B("modT")
        modA = sb("modA", [128, DEPTH, 8, 5]); modB_ = modT
        mT = sb("mT", [128, 8, TW], BF16); bmT_ = [B(f"mT{i}") for i in range(3)]
        wdw = sb("wdw", [128, DEPTH, 4, 31]); bdw = sb("bdw", [128, DEPTH, 4]); lng = sb("lng", [128, DEPTH, 4]); lnb = sb("lnb", [128, DEPTH, 4])
        for l_ in range(DEPTH):
            dma("sp", wdw[:, l_], w_dwT[l_], part=[bC]); dma("sp", bdw[:, l_], b_dwT[l_], part=[bC])
            dma("sp", lng[:, l_], ln_gT[l_], part=[bC]); dma("sp", lnb[:, l_], ln_bT[l_], part=[bC])
        lbl = sb("lbl", [128, DEPTH, 4]); lbm = sb("lbm", [128, DEPTH, 4]); oml = sb("oml", [128, DEPTH, 4]); noml = sb("noml", [128, DEPTH, 4])
        cng = sb("cng_sb", [128, DEPTH]); blb = B("lb")
        for l_ in range(DEPTH):
            dma("sp", lbl[:, l_, :], lbT_in[l_], part=[blb]); dma("sp", cng[:, l_:l_ + 1], cng_in[l_], part=[blb])
        op("dve", lambda e: e.tensor_tensor(out=lbm[:, 1, :], in0=lbl[:, 1, :], in1=lbl[:, 0, :], op=ALU.subtract), reads=[blb], part=[blb])
        op("act", lambda e: e.activation(out=lbm[:, 1, :], in_=lbm[:, 1, :], func=AF.Sigmoid), reads=[blb], part=[blb])
        op("dve", lambda e: e.memset(lbm[:, 0, :], 0.0), part=[blb])
        op("dve", lambda e: e.tensor_scalar(out=oml[:, :, :], in0=lbm[:, :, :], scalar1=-1.0, scalar2=1.0, op0=ALU.mult, op1=ALU.add), reads=[blb], part=[blb])
        op("dve", lambda e: e.tensor_scalar(out=noml[:, :, :], in0=oml[:, :, :], scalar1=-1.0, scalar2=None, op0=ALU.mult), reads=[blb], part=[blb])
        op("dve", lambda e: e.tensor_scalar(out=lbm[:, :, :], in0=lbm[:, :, :], scalar1=1e-30, scalar2=None, op0=ALU.max), reads=[blb], part=[blb])
        Sst = sb("Sst", [128, DEPTH, 4, 128]); bS = [[B(f"S{l_}{h}") for h in range(4)] for l_ in range(DEPTH)]
        sQ = sb("sQ", [NS, 512], BF16); sQI = sb("sQI", [NS, 256], BF16); sKV = sb("sKV", [NS, 320]); sWI = sb("sWI", [NS, 4]); bsQ = B("sQ")
        bgS = sb("bgS", [128, 4, NS], BF16); bbgS = B("bgS"); uS = sb("uS", [128, 4, NS]); agS = sb("agS", [128, 4, NS], BF16); bsA = B("sA")
        qcS = sb("qcS", [128, 4, NS], BF16); sigS = sb("sigS", [128, 4, NS]); cgS = sb("cgS", [128, 4, NS], BF16); ciS = sb("ciS", [NS, 4, 128], BF16); bsC = B("sC")
        yS = sb("yS", [128, 3, 4, NS], BF16); byS = [B(f"yS{i}") for i in range(3)]
        uhalo = sb("uhalo", [128, DEPTH, 4, 30], BF16); buh = B("uhalo")
        ones_f = sb("ones_f", [128, 128])
        op("pool", lambda e: e.memset(ones_f[:], 1.0), part=[bC])
        onesm = sb("onesm", [128, 128], BF16)
        op("pool", lambda e: e.memset(onesm[:], 1.0 / 512), part=[bC])
        kdA = sb("kdA", [128, DEPTH, 2, TS], BF16); kdB = sb("kdB", [128, 2, TS], BF16)
        VpA = sb("VpA", [128, DEPTH, NTT, 2, 128], BF16); VpB = sb("VpB", [128, NTT, 2, 128], BF16)
        kidA = sb("kidA", [128, DEPTH, TS], BF16); kidB = sb("kidB", [128, TS], BF16)
        bKA = [[B(f"KA{l}_{i}") for i in range(NTT)] for l in range(DEPTH)]
        bKB = [B(f"KB{i}") for i in range(NTT)]

        def kd_ap(l, kb, g, nblk=1):
            if kb < NTT:
                return kdA[:, l, g, kb * 128:(kb + nblk) * 128]
            return kdB[:, g, (kb - NTT) * 128:(kb - NTT + nblk) * 128]

        def kid_ap(l, kb, nblk=1):
            if kb < NTT:
                return kidA[:, l, kb * 128:(kb + nblk) * 128]
            return kidB[:, (kb - NTT) * 128:(kb - NTT + nblk) * 128]

        def vp_ap(l, kb, g):
            return VpA[:, l, kb, g, :] if kb < NTT else VpB[:, kb - NTT, g, :]

        def bK(l, kb):
            return bKA[l][kb] if kb < NTT else bKB[kb - NTT]

        bis = sb("bis", [128, 64]); bbis = B("bis")
        pw2 = sb("pw2", [128, 32]);
        dma("sp", pw2[:], cst["pw2"], part=[bC])
        scr = sb("scr", [128, D]); bscr = B("scr")
        scr_b = scr[:, :].bitcast(BF16)
        mgs = [scr_b[:, i * 512:(i + 1) * 512] for i in range(2)]; bmgs = [B(f"mgs{i}") for i in range(2)]
        mtmp = [scr_b[:, 1024 + i * 512:1024 + (i + 1) * 512] for i in range(2)]; bmtmp = [B(f"mtmp{i}") for i in range(2)]
        xn = sb("xn", [128, D], BF16); bxn = B("xn")
        st8 = sb("st8", [128, 64]); bst8 = B("st8")

        ada_st = ExitStack()
        cin = ada_st.enter_context(nc.sbuf_tensor("cin", [5, D], F32)); bcin = B("cin")
        dma("sp", cin[:], cc, writes=[bcin])
        csil = ada_st.enter_context(nc.sbuf_tensor("csil", [5, D], F32)); bcs = B("csil")
        op("act", lambda e: e.activation(out=csil[:], in_=cin[:], func=AF.Silu), reads=[bcin], writes=[bcs])
        for k in range(8):
            ps, bps = psum()
            op("pe", lambda e: e.transpose(out=ps[:, 0:5], in_=csil[:, k * 128:(k + 1) * 128], identity=ident_f[0:5, 0:5]),
               reads=[bcs, bC], writes=[bps])
            op("act", lambda e: e.copy(out=cT[:, k, :], in_=ps[:, 0:5]), reads=[bps], part=[bcT])
        for l in range(NL):
            for gi in range(6):
                w, bw = wload(w_ada[l][:, gi * 512:(gi + 1) * 512], 8, 512)
                for c4 in range(4):
                    ch = gi * 4 + c4
                    ps, bps = psum()
                    for k in range(8):
                        op("pe", lambda e: e.matmul(ps[:, 0:5], lhsT=w[:, k, c4 * 128:(c4 + 1) * 128], rhs=cT[:, k, :],
                                                    start=(k == 0), stop=(k == 7)),
                           reads=[bw, bcT], writes=[bps] if k == 0 else (), part=() if k == 0 else [bps], inc=(k == 7))
                    op("act", lambda e: e.activation(out=modT[:, l, ch, :], in_=ps[:, 0:5], func=AF.Identity,
                                                     bias=badT[:, l, ch:ch + 1], scale=1.0),
                       reads=[bps, bC], part=[bmod])
            op("dve", lambda e: e.tensor_scalar(out=modA[:, l, :, :], in0=modT[:, l, 8:16, :], scalar1=1.0, scalar2=None, op0=ALU.add),
               reads=[bmod], part=[bmod])
            op("dve", lambda e: e.tensor_tensor(out=modA[:, l, :, :], in0=modA[:, l, :, :],
                                                in1=ngT[:, l, :].unsqueeze(2).to_broadcast([128, 8, 5]), op=ALU.mult),
               reads=[bmod, bC], part=[bmod])

        Tk.barrier()
        ada_st.close()

        def make_hT(l, src, npart, bsrc, col0, bh, rows):
            op("act", lambda e: e.activation(out=scr[0:npart, :], in_=src, func=AF.Square, accum_out=st8[0:npart, 0:1]),
               reads=[bsrc], writes=[bscr, bst8])
            op("act", lambda e: e.activation(out=st8[0:npart, 1:2], in_=st8[0:npart, 0:1], func=AF.Sqrt,
                                             bias=eps_t[0:npart, :], scale=1.0 / D), reads=[bst8, bC], part=[bst8])
            op("dve", lambda e: e.reciprocal(out=st8[0:npart, 2:3], in_=st8[0:npart, 1:2]), reads=[bst8], part=[bst8])
            op("act", lambda e: e.activation(out=xn[0:npart, :], in_=src, func=AF.Copy, scale=st8[0:npart, 2:3]),
               reads=[bsrc, bst8], writes=[bxn])
            for half in range(2):
                ps, bps = psum()
                psb = ps[:].bitcast(BF16)
                for k4 in range(4):
                    k = half * 4 + k4
                    op("pe", lambda e: e.transpose(out=psb[:, k4 * 128:k4 * 128 + npart], in_=xn[0:npart, k * 128:(k + 1) * 128],
                                                   identity=ident_b[0:npart, 0:npart]),
                       reads=[bxn, bC], writes=[bps] if k4 == 0 else (), part=() if k4 == 0 else [bps], inc=(k4 == 3))
                for k4 in range(4):
                    k = half * 4 + k4
                    for (lo, hi, r) in rows:
                        op("act", lambda e: e.activation(out=hT[:, k, col0 + lo:col0 + hi], in_=psb[:, k4 * 128 + lo:k4 * 128 + hi],
                                                         func=AF.Identity, scale=modA[:, l, k, r:r + 1], bias=modT[:, l, k, r:r + 1]),
                           reads=[bps, bmod], part=[bh])

        rtmp = scr[:, :].rearrange("p (a b) -> p a b", a=4); brt = bscr

        def rms_heads(ps_ap, nh, npart, gsel, l):
            n = nh * 64
            op("act", lambda e: e.activation(out=scr[0:npart, 0:n], in_=ps_ap, func=AF.Square), reads=[], writes=[bscr])
            op("dve", lambda e: e.tensor_reduce(out=st8[0:npart, 8:8 + nh], in_=scr[0:npart, 0:n].rearrange("p (h d) -> p h d", d=64),
                                                axis=AX.X, op=ALU.add), reads=[bscr], part=[bst8])
            op("act", lambda e: e.activation(out=st8[0:npart, 16:16 + nh], in_=st8[0:npart, 8:8 + nh], func=AF.Sqrt,
                                             bias=eps_t[0:npart, :], scale=1.0 / 64), reads=[bst8, bC], part=[bst8])
            op("dve", lambda e: e.reciprocal(out=st8[0:npart, 24:24 + nh], in_=st8[0:npart, 16:16 + nh]), reads=[bst8], part=[bst8])
            op("dve", lambda e: e.tensor_tensor(out=qtok[0:npart, 0:n].rearrange("p (h d) -> p h d", d=64),
                                                in0=ps_ap.rearrange("p (h d) -> p h d", d=64),
                                                in1=st8[0:npart, 24:24 + nh].unsqueeze(2).to_broadcast([npart, nh, 64]), op=ALU.mult),
               reads=[bst8], writes=[bqtok])
            if gsel is not None:
                op("pool", lambda e: e.tensor_tensor(out=qtok[0:npart, 0:n].rearrange("p (h d) -> p h d", d=64),
                                                     in0=qtok[0:npart, 0:n].rearrange("p (h d) -> p h d", d=64),
                                                     in1=gsel[0:npart, l, :].unsqueeze(1).to_broadcast([npart, nh, 64]), op=ALU.mult),
                   reads=[bqtok, bC], writes=[bqtok])

        def rope(src_ap, bsrc, nh, npart, cos_ap, sin_ap, out_ap, bout, eng="pool", part_out=False):
            s3 = src_ap.rearrange("p (h two d) -> p h two d", two=2, d=32)
            o3 = out_ap.rearrange("p (h two d) -> p h two d", two=2, d=32)
            x1, x2 = s3[:, :, 0, :], s3[:, :, 1, :]
            cb = cos_ap.unsqueeze(1).to_broadcast([npart, nh, 32])
            sbb = sin_ap.unsqueeze(1).to_broadcast([npart, nh, 32])
            t = [rtmp[0:npart, i, 0:nh * 32].rearrange("p (h d) -> p h d", d=32) for i in range(4)]
            op(eng, lambda e: e.tensor_tensor(out=t[0], in0=x1, in1=cb, op=ALU.mult), reads=[bsrc, bC], writes=[brt])
            op(eng, lambda e: e.tensor_tensor(out=t[1], in0=x2, in1=sbb, op=ALU.mult), reads=[bsrc, bC], part=[brt])
            op(eng, lambda e: e.tensor_tensor(out=t[2], in0=x2, in1=cb, op=ALU.mult), reads=[bsrc, bC], part=[brt])
            op(eng, lambda e: e.tensor_tensor(out=t[3], in0=x1, in1=sbb, op=ALU.mult), reads=[bsrc, bC], part=[brt])
            w_ = dict(part=[bout]) if part_out else dict(writes=[bout])
            op(eng, lambda e: e.tensor_tensor(out=o3[:, :, 0, :], in0=t[0], in1=t[1], op=ALU.subtract), reads=[brt], **w_)
            op(eng, lambda e: e.tensor_tensor(out=o3[:, :, 1, :], in0=t[2], in1=t[3], op=ALU.add), reads=[brt], part=[bout])

        def sample_dsa(l):
            NITS = 20
            U16 = mybir.dt.uint16
            ph0 = ExitStack()
            a0 = lambda n_, shp, dt=F32: ph0.enter_context(nc.sbuf_tensor(f"{n_}_s{l}", list(shp), dt))
            ptx = a0("ptx", [128, 4, 8], I32); idx2 = a0("idx2", [128, 4, 8], I32); roff = a0("roff", [128, 1]); bidx = B("idx")
            newmask = a0("newmask", [128, 4, 16]); grp4 = a0("grp4", [128, 128]); rep4 = a0("rep4", [16, 4, 128]); bsc = B("sconst")
            dma("sp", ptx[:], ptx_in, part=[bidx]); dma("sp", roff[:], cst["roff"], part=[bidx])
            dma("sp", newmask[:], cst["newmask"], part=[bsc]); dma("sp", grp4[:], cst["grp4"], part=[bsc]); dma("sp", rep4[:], cst["rep4"], part=[bsc])
            op("dve", lambda e: e.tensor_scalar(out=idx2[:], in0=ptx[:], scalar1=16.0, scalar2=roff[:, 0:1], op0=ALU.mult, op1=ALU.add), reads=[bidx], writes=[bidx])
            qsTz = a0("qsTz", [128, 2, 4, NS], BF16); sQg = a0("sQg", [NS, 4, 128], BF16); qiT4 = a0("qiT4", [64, 4, NS], BF16); kinT = a0("kinT", [64, NS], BF16); knT = a0("knT", [128, NS], BF16)
            vnew = a0("vnew", [NS, 2, 65], BF16); kvb = a0("kvb", [NS, 192], BF16); wrepS = a0("wrepS", [128, 4, 4]); bq = B("sq")
            Wh = a0("Wh", [64, 4, 252], BF16); bWh = B("Wh")
            op("pool", lambda e: e.memset(qsTz[:], 0.0), writes=[bq])
            op("pool", lambda e: e.memset(Wh[:], 0.0), writes=[bWh])
            op("pool", lambda e: e.memset(vnew[:], 1.0), part=[bq])
            op("act", lambda e: e.copy(out=kvb[:, 0:128], in_=sKV[:, 0:128]), reads=[bsQ], part=[bq])
            op("act", lambda e: e.copy(out=kvb[:, 128:192], in_=sKV[:, 256:320]), reads=[bsQ], part=[bq])
            op("act", lambda e: e.copy(out=vnew[:, :, 0:64], in_=sKV[:, 128:256].rearrange("p (g d) -> p g d", d=64)), reads=[bsQ], part=[bq])
            ps, bps = psum(0, 3)
            pb = ps[:].bitcast(BF16)
            op("act", lambda e: e.copy(out=sQg[:, :, :].rearrange("p h (g d) -> p h g d", g=2), in_=sQ[:, :].rearrange("p (g h d) -> p h g d", g=2, h=4)),
               reads=[bsQ], part=[bq])
            for hh in range(4):
                op("pe", lambda e: e.transpose(out=pb[:, hh * 16:(hh + 1) * 16], in_=sQg[:, hh, :], identity=ident_b[0:NS, 0:NS]),
                   reads=[bq, bC], writes=[bps] if hh == 0 else (), part=() if hh == 0 else [bps], inc=False)
            for h in range(4):
                op("pe", lambda e: e.transpose(out=pb[0:64, 64 + h * 16:64 + (h + 1) * 16], in_=sQI[:, h * 64:(h + 1) * 64], identity=ident_b[0:NS, 0:NS]),
                   reads=[bsQ, bC], part=[bps], inc=False)
            op("pe", lambda e: e.transpose(out=pb[0:64, 128:144], in_=kvb[:, 128:192], identity=ident_b[0:NS, 0:NS]), reads=[bq, bC], part=[bps], inc=False)
            op("pe", lambda e: e.transpose(out=pb[:, 144:160], in_=kvb[:, 0:128], identity=ident_b[0:NS, 0:NS]), reads=[bq, bC], part=[bps])
            for g in range(2):
                rows = slice(g * 64, (g + 1) * 64)
                op("act", lambda e: e.copy(out=qsTz[rows, g, :, :].rearrange("p b (h t) -> p b h t", h=4),
                                           in_=pb[rows, 0:64].rearrange("p (h b t) -> p b h t", h=4, b=4)), reads=[bps], part=[bq])
            op("act", lambda e: e.copy(out=qiT4[:, :, :], in_=pb[0:64, 64:128].rearrange("p (h t) -> p h t", t=NS)), reads=[bps], part=[bq])
            op("act", lambda e: e.copy(out=kinT[:, :], in_=pb[0:64, 128:144]), reads=[bps], part=[bq])
            op("act", lambda e: e.copy(out=knT[:, :], in_=pb[:, 144:160]), reads=[bps], part=[bq])
            for b in range(4):
                ps, bps = psum(0, 3)
                op("pe", lambda e: e.matmul(ps[:, 0:4], lhsT=rep4[:, b, :], rhs=sWI[:, :], start=True, stop=True), reads=[bsc, bsQ], writes=[bps])
                op("act", lambda e: e.copy(out=wrepS[:, b, :], in_=ps[:, 0:4]), reads=[bps], part=[bq])

            bgI = [B(f"gI{j}") for j in range(8)]; bgK = [B(f"gK{j}") for j in range(4)]
            mT2 = a0("mT2", [128, 2, 128], BF16); mTn = a0("mTn", [NS, 128], BF16); bmT2 = B("mT2")
            for b in range(4):
                p1 = ExitStack()
                a1 = lambda n_, shp, dt=F32: p1.enter_context(nc.sbuf_tensor(f"{n_}_s{l}{b}", list(shp), dt))
                gI = a1("gI", [128, 8, 512])
                kiTs = a1("kiTs", [64, PAST], BF16); bkiT = [B(f"kiT{j}") for j in range(8)]
                Ib = a1("Ib", [128, 272]); bIb = B("Ib"); rlS = a1("rlS", [128, 272]); brlS = B("rlS")
                mkS = a1("mkS", [128, 272], BF16); bmkS = B("mkS"); bs = a1("bs", [128, 64]); bbs = B("bs")
                op("pool", lambda e: e.tensor_copy(out=Wh[:, :, 124:128], in_=qiT4[:, :, 4 * b:4 * b + 4]), reads=[bq], writes=[bWh])
                for jhi in range(8):
                    dma("pool", None, None, reads=[bidx], writes=[bgI[jhi]],
                        fn=lambda e: e.indirect_dma_start(out=gI[:, jhi, :], out_offset=None, in_=cik_in[l].rearrange("(r e) d -> r (e d)", e=8),
                                                          in_offset=bass.IndirectOffsetOnAxis(ap=idx2[:, b, jhi:jhi + 1], axis=0)))
                for jhi in range(8):
                    for r4 in range(2):
                        ps, bps = psum(0, 3)
                        for x in range(4):
                            rr = r4 * 4 + x
                            op("pe", lambda e: e.transpose(out=ps[0:64, x * 128:(x + 1) * 128], in_=gI[:, jhi, rr * 64:(rr + 1) * 64], identity=ident_f[:, :]),
                               reads=[bgI[jhi], bC], writes=[bps] if x == 0 else (), part=() if x == 0 else [bps], inc=(x == 3))
                        k0 = (jhi * 8 + r4 * 4) * 128
                        op("act" if r4 == 0 else "dve", (lambda e: e.copy(out=kiTs[:, k0:k0 + 512], in_=ps[0:64, :])) if r4 == 0 else
                           (lambda e: e.tensor_copy(out=kiTs[:, k0:k0 + 512], in_=ps[0:64, :])), reads=[bps], part=[bkiT[jhi]])
                for h in range(4):
                    ps, bps = psum(0, 3)
                    for k in range(32):
                        op("pe", lambda e: e.matmul(ps[:, 0:256], lhsT=Wh[:, h, 124 - 4 * k:252 - 4 * k], rhs=kiTs[:, k * 256:(k + 1) * 256], start=(k == 0), stop=(k == 31)),
                           reads=[bWh, bkiT[k // 4]], writes=[bps] if k == 0 else (), part=() if k == 0 else [bps], inc=(k == 31))
                    op("pe", lambda e: e.matmul(ps[:, 256:272], lhsT=Wh[:, h, 124:252], rhs=kinT[:, :], start=True, stop=True), reads=[bWh, bq], part=[bps])
                    op("act", lambda e: e.activation(out=rlS[:, :], in_=ps[:, 0:272], func=AF.Relu), reads=[bps], writes=[brlS])
                    if h == 0:
                        op("dve", lambda e: e.tensor_scalar(out=Ib[:, :], in0=rlS[:, :], scalar1=wrepS[:, b, 0:1], scalar2=None, op0=ALU.mult), reads=[brlS, bq], writes=[bIb])
                    else:
                        op("dve", lambda e: e.scalar_tensor_tensor(out=Ib[:, :], in0=rlS[:, :], scalar=wrepS[:, b, h:h + 1], in1=Ib[:, :], op0=ALU.mult, op1=ALU.add),
                           reads=[brlS, bq], writes=[bIb])
                op("dve", lambda e: e.tensor_reduce(out=bs[:, 0:1], in_=Ib[:, :], axis=AX.X, op=ALU.max, apply_absolute_value=True), reads=[bIb], writes=[bbs])
                ps, bps = psum(0, 3)
                op("pe", lambda e: e.matmul(ps[:, 0:1], lhsT=ones_f[:, :], rhs=bs[:, 0:1], start=True, stop=True), reads=[bbs, bC], writes=[bps])
                op("dve", lambda e: e.tensor_scalar(out=bs[:, 1:2], in0=ps[:, 0:1], scalar1=1.0, scalar2=None, op0=ALU.add), reads=[bps], part=[bbs])
                op("dve", lambda e: e.tensor_scalar(out=bs[:, 16:16 + NITS + 2], in0=pw2[:, 0:NITS + 2], scalar1=bs[:, 1:2], scalar2=None, op0=ALU.mult), reads=[bC, bbs], part=[bbs])
                op("dve", lambda e: e.memset(bs[:, 2:3], 0.0), part=[bbs])
                op("dve", lambda e: e.tensor_tensor(out=Ib[:, 256:272], in0=Ib[:, 256:272], in1=newmask[:, b, :], op=ALU.add), reads=[bsc], writes=[bIb])
                for it in range(NITS + 1):
                    if it < NITS:
                        op("dve", lambda e: e.tensor_scalar(out=mkS[:, :], in0=Ib[:, :], scalar1=bs[:, 2:3], scalar2=None, op0=ALU.is_ge, op1=ALU.add, accum_out=bs[:, 3:4]),
                           reads=[bIb, bbs], writes=[bmkS], part=[bbs])
                        ps, bps = psum(0, 3)
                        op("pe", lambda e: e.matmul(ps[:, 0:1], lhsT=grp4[:, :], rhs=bs[:, 3:4], start=True, stop=True), reads=[bbs, bsc], writes=[bps])
                        op("dve", lambda e: e.scalar_tensor_tensor(out=bs[:, 4:5], in0=ps[:, 0:1], scalar=256.0, in1=bs[:, 16 + it:17 + it], op0=ALU.is_ge, op1=ALU.mult),
                           reads=[bps, bbs], part=[bbs])
                        op("dve", lambda e: e.scalar_tensor_tensor(out=bs[:, 2:3], in0=bs[:, 4:5], scalar=bs[:, 17 + it:18 + it], in1=bs[:, 2:3], op0=ALU.subtract, op1=ALU.add),
                           reads=[bbs], part=[bbs])
                    else:
                        op("dve", lambda e: e.tensor_tensor(out=bs[:, 5:6], in0=bs[:, 2:3], in1=bs[:, 16 + it:17 + it], op=ALU.subtract), reads=[bbs], part=[bbs])
                op("dve", lambda e: e.tensor_scalar(out=mkS[:, :], in0=Ib[:, :], scalar1=bs[:, 5:6], scalar2=None, op0=ALU.is_ge), reads=[bIb, bbs], writes=[bmkS])
                ps, bps = psum(0, 3)
                pb = ps[:].bitcast(BF16)
                for cb in range(2):
                    op("pe", lambda e: e.transpose(out=pb[:, cb * 128:(cb + 1) * 128], in_=mkS[:, cb * 128:(cb + 1) * 128], identity=ident_b[:, :]),
                       reads=[bmkS, bC], writes=[bps] if cb == 0 else (), part=() if cb == 0 else [bps], inc=False)
                op("pe", lambda e: e.transpose(out=pb[0:NS, 256:384], in_=mkS[:, 256:272], identity=ident_b[:, :]), reads=[bmkS, bC], part=[bps])
                op("act", lambda e: e.copy(out=mT2[:, :, :], in_=pb[:, 0:256].rearrange("p (c t) -> p c t", c=2)), reads=[bps], writes=[bmT2])
                op("act", lambda e: e.copy(out=mTn[:, :], in_=pb[0:NS, 256:384]), reads=[bps], part=[bmT2])
                if dbg is not None and cfg.get("dbgkey") == "s_dbg" and l == 0 and b == 0:
                    dma("sp", dbg[1][:, 0:272], Ib[:, :], reads=[bIb], own=bIb, is_output=True)
                    dma("pool", dbg[1][:, 272:544], mkS[:, :], reads=[bmkS], own=bmkS, is_output=True)
                    dma("sp", dbg[1][:, 544:552], bs[:, 0:8], reads=[bbs], own=bbs, is_output=True)
                    dma("pool", dbg[1][0:64, 600:600 + 2048], kiTs[:, 0:2048], reads=bkiT, own=bkiT[0], is_output=True)
                Tk.barrier(); p1.close()
                p2 = ExitStack()
                a2 = lambda n_, shp, dt=F32: p2.enter_context(nc.sbuf_tensor(f"{n_}_s{l}{b}", list(shp), dt))
                gK = a2("gK", [128, 4, 1024])
                KTs = a2("KTs", [128, PAST], BF16); bKT = [B(f"KT{j}") for j in range(8)]
                Vs = a2("Vs", [128, 32, 2, 65], BF16); bVs = B("Vs")
                Ee = a2("Ee", [128, 64, 32], BF16); bEe = [B(f"Ee{j}") for j in range(4)]; En = a2("En", [NS, 32], BF16); bEn = B("En")
                osb = a2("osb", [NS, 2, 2, 64], BF16); bosb = B("osb"); ork = a2("ork", [NS, 4]); bork = B("ork")
                op("pool", lambda e: e.memset(Vs[:, :, :, 64:65], 1.0), writes=[bVs])
                ck_r = ck_in[l].rearrange("(r e) d -> r (e d)", e=8); cv_r = cv_in[l].rearrange("(r e) d -> r (e d)", e=8)
                for half in range(2):
                    for j4 in range(4):
                        jhi = half * 4 + j4
                        dma("pool", None, None, reads=[bidx], writes=[bgK[j4]],
                            fn=lambda e: e.indirect_dma_start(out=gK[:, j4, :], out_offset=None, in_=ck_r,
                                                              in_offset=bass.IndirectOffsetOnAxis(ap=idx2[:, b, jhi:jhi + 1], axis=0)))
                    for j4 in range(4):
                        jhi = half * 4 + j4
                        for r4 in range(2):
                            ps, bps = psum(0, 3)
                            for x in range(4):
                                rr = r4 * 4 + x
                                op("pe", lambda e: e.transpose(out=ps[:, x * 128:(x + 1) * 128], in_=gK[:, j4, rr * 128:(rr + 1) * 128], identity=ident_f[:, :]),
                                   reads=[bgK[j4], bC], writes=[bps] if x == 0 else (), part=() if x == 0 else [bps], inc=(x == 3))
                            k0 = (jhi * 8 + r4 * 4) * 128
                            if r4 == 0:
                                op("act", lambda e: e.copy(out=KTs[:, k0:k0 + 512], in_=ps[:, :]), reads=[bps], part=[bKT[jhi]])
                            else:
                                op("dve", lambda e: e.tensor_copy(out=KTs[:, k0:k0 + 512], in_=ps[:, :]), reads=[bps], part=[bKT[jhi]])
                for bk in range(4):
                    psS, bpsS = PS[3 + bk], bPS[3 + bk]
                    for k16 in range(16):
                        kt = bk * 16 + k16
                        for g in range(2):
                            op("pe", lambda e: e.matmul(psS[:, k16 * 32 + g * 16:k16 * 32 + (g + 1) * 16], lhsT=KTs[:, kt * 128:(kt + 1) * 128],
                                                        rhs=qsTz[:, g, b, :], start=True, stop=True),
                               reads=[bKT[kt // 8], bq], writes=[bpsS] if (k16 == 0 and g == 0) else (), part=() if (k16 == 0 and g == 0) else [bpsS],
                               inc=(k16 == 15 and g == 1))
                    op("act", lambda e: e.activation(out=Ee[:, bk * 16:(bk + 1) * 16, :], in_=psS[:, :].rearrange("p (k c) -> p k c", c=32), func=AF.Exp, scale=0.125),
                       reads=[bpsS], writes=[bEe[bk]])
                    for cb in range(2):
                        e4 = Ee[:, bk * 16:(bk + 1) * 16, :].rearrange("p (s c) (gh t) -> p s c gh t", c=2, t=4)[:, :, cb, :, :]
                        m4 = mT2[:, cb, bk * 32:(bk + 1) * 32].rearrange("p (s t) -> p s t", t=4).unsqueeze(2).to_broadcast([128, 8, 8, 4])
                        op("pool", lambda e: e.tensor_tensor(out=e4, in0=e4, in1=m4, op=ALU.mult), reads=[bmT2], part=[bEe[bk]])
                psN, bpsN = PS[7], bPS[7]
                for g in range(2):
                    op("pe", lambda e: e.matmul(psN[0:NS, g * 16:(g + 1) * 16], lhsT=knT[:, :], rhs=qsTz[:, g, b, :], start=True, stop=True),
                       reads=[bq], writes=[bpsN] if g == 0 else (), part=() if g == 0 else [bpsN], inc=(g == 1))
                op("act", lambda e: e.activation(out=En[:, :], in_=psN[0:NS, 0:32], func=AF.Exp, scale=0.125), reads=[bpsN], writes=[bEn])
                op("pool", lambda e: e.tensor_tensor(out=En[:, :].rearrange("p (gh t) -> p gh t", t=4), in0=En[:, :].rearrange("p (gh t) -> p gh t", t=4),
                                                     in1=mTn[:, 0:4].unsqueeze(1).to_broadcast([NS, 8, 4]), op=ALU.mult), reads=[bmT2], writes=[bEn])
                psOg = [PS[3], PS[4]]; bpsOg = [bPS[3], bPS[4]]
                for half in range(2):
                    for j4 in range(4):
                        jhi = half * 4 + j4
                        dma("pool", None, None, reads=[bidx], writes=[bgK[j4]],
                            fn=lambda e: e.indirect_dma_start(out=gK[:, j4, :], out_offset=None, in_=cv_r,
                                                              in_offset=bass.IndirectOffsetOnAxis(ap=idx2[:, b, jhi:jhi + 1], axis=0)))
                    for j4 in range(4):
                        vsrc = gK[:, j4, :].rearrange("p (r g d) -> p r g d", g=2, d=64)
                        vdst = Vs[:, j4 * 8:(j4 + 1) * 8, :, 0:64]
                        if j4 % 2 == 0:
                            op("act", lambda e: e.copy(out=vdst, in_=vsrc), reads=[bgK[j4]], part=[bVs])
                        else:
                            op("pool", lambda e: e.tensor_copy(out=vdst, in_=vsrc), reads=[bgK[j4]], part=[bVs])
                    for g in range(2):
                        for k32 in range(32):
                            kt = half * 32 + k32
                            first = (half == 0 and k32 == 0)
                            op("pe", lambda e: e.matmul(psOg[g][0:NS, 0:65], lhsT=Ee[:, kt, g * 16:(g + 1) * 16], rhs=Vs[:, k32, g, :], start=first, stop=False),
                               reads=[bEe[kt // 16], bVs], writes=[bpsOg[g]] if first else (), part=() if first else [bpsOg[g]], inc=(k32 == 31))
                for g in range(2):
                    op("pe", lambda e: e.matmul(psOg[g][0:NS, 0:65], lhsT=En[:, g * 16:(g + 1) * 16], rhs=vnew[:, g, :], start=False, stop=True),
                       reads=[bEn, bq], part=[bpsOg[g]])
                for g in range(2):
                    op("dve", lambda e: e.reciprocal(out=ork[:, g:g + 1], in_=psOg[g][0:NS, 64:65]), reads=[bpsOg[g]], part=[bork])
                    op("dve", lambda e: e.tensor_scalar(out=osb[:, g, :, :], in0=psOg[g][0:NS, 0:64].unsqueeze(1).to_broadcast([NS, 2, 64]),
                                                        scalar1=ork[:, g:g + 1], scalar2=None, op0=ALU.mult), reads=[bpsOg[g], bork], part=[bosb])
                ps, bps = psum(0, 3)
                pb = ps[:].bitcast(BF16)
                for g in range(2):
                    op("pe", lambda e: e.transpose(out=pb[:, g * 16:(g + 1) * 16], in_=osb[:, g, :, :].rearrange("p r d -> p (r d)"), identity=ident_b[0:NS, 0:NS]),
                       reads=[bosb, bC], writes=[bps] if g == 0 else (), part=() if g == 0 else [bps], inc=(g == 1))
                for g in range(2):
                    for hh in range(4):
                        j, c = hh % 2, 2 * g + hh // 2
                        rows = slice(j * 64, (j + 1) * 64)
                        op("dve", lambda e: e.tensor_tensor(out=yS[rows, 1, c, 4 * b:4 * b + 4], in0=pb[rows, g * 16 + hh * 4:g * 16 + hh * 4 + 4],
                                                            in1=bgS[rows, c, 4 * b:4 * b + 4], op=ALU.mult), reads=[bps, bbgS], part=[byS[1]])
                if dbg is not None and cfg.get("dbgkey") == "s_dbg" and l == 0 and b == 0:
                    dma("pool", dbg[0][:, 0:4096], KTs[:, 0:4096], reads=bKT, own=bKT[0], is_output=True)
                Tk.barrier(); p2.close()
            Tk.barrier(); ph0.close()

        def sample_conv(l):
            ph = ExitStack()
            a = lambda n_, shp, dt=F32: ph.enter_context(nc.sbuf_tensor(f"{n_}_c{l}", list(shp), dt))
            sct = a("sct", [30, 4, 512]); bsct = B("sct")
            ufS = a("ufS", [128, 4, 4, 34]); bufS = B("ufS")
            prod = a("prod", [128, 4, 4, 31]); bprod = B("prod")
            ycv = a("ycv", [128, 4, NS]); bycv = B("ycv"); ycb = a("ycb", [128, 4, NS], BF16); ysq = a("ysq", [128, 4, NS], BF16)
            lnS = a("lnS", [128, 3, NS]); blnS = B("lnS"); t1 = a("t1", [128, 4, NS]); bt1 = B("t1"); t2 = a("t2", [128, 4, NS], BF16)
            utok = a("utok", [NS, 512]); butok = B("utok")
            dma("sp", sct[:, :, :], sconv_in[l].rearrange("b r c -> r b c"), writes=[bsct])
            for b in range(4):
                dma("sp", convs[l, b, 0:26, :], sct[4:30, b, :], reads=[bsct], own=bsct, is_output=True)
            for b in range(4):
                ps, bps = psum()
                for c in range(4):
                    op("pe", lambda e: e.transpose(out=ps[:, c * 32:c * 32 + 30], in_=sct[:, b, c * 128:(c + 1) * 128], identity=ident_f[0:30, 0:30]),
                       reads=[bsct, bC], writes=[bps] if c == 0 else (), part=() if c == 0 else [bps], inc=(c == 3))
                op("act", lambda e: e.copy(out=ufS[:, b, :, 0:30], in_=ps[:, 0:128].rearrange("p (c r) -> p c r", r=32)[:, :, 0:30]), reads=[bps], part=[bufS])
            op("act", lambda e: e.copy(out=ufS[:, :, :, 30:34], in_=uS[:, :, :].rearrange("p c (b t) -> p b c t", t=4)), reads=[bsA], part=[bufS])
            for c in range(4):
                win = ufS[:, :, c, 0:31].unsqueeze(2).to_broadcast([128, 4, 4, 31])
                import copy as _cp
                base = ufS[:, :, c, :]
                for t in range(4):
                    op("dve", lambda e: e.tensor_tensor(out=prod[:, :, t, :], in0=ufS[:, :, c, t:t + 31],
                                                        in1=wdw[:, l, c, :].unsqueeze(1).to_broadcast([128, 4, 31]), op=ALU.mult),
                       reads=[bufS, bC], **(dict(writes=[bprod]) if t == 0 else dict(part=[bprod])))
                op("dve", lambda e: e.tensor_reduce(out=ycv[:, c, :].rearrange("p (b t) -> p b t", t=4), in_=prod[:, :, :, :], axis=AX.X, op=ALU.add),
                   reads=[bprod], part=[bycv])
                op("dve", lambda e: e.tensor_scalar(out=ycv[:, c, :], in0=ycv[:, c, :], scalar1=bdw[:, l, c:c + 1], scalar2=None, op0=ALU.add),
                   reads=[bycv, bC], part=[bycv])
            op("act", lambda e: e.copy(out=ycb[:, :, :], in_=ycv[:, :, :]), reads=[bycv], part=[bycv])
            op("act", lambda e: e.activation(out=ysq[:, :, :], in_=ycv[:, :, :], func=AF.Square), reads=[bycv], part=[bycv])
            psM, bpsM = psum()
            for c in range(4):
                op("pe", lambda e: e.matmul(psM[:, 0:NS], lhsT=onesm[:, :], rhs=ycb[:, c, :], start=(c == 0), stop=(c == 3)),
                   reads=[bycv, bC], writes=[bpsM] if c == 0 else (), part=() if c == 0 else [bpsM], inc=(c == 3))
            psQ, bpsQ = psum()
            for c in range(4):
                op("pe", lambda e: e.matmul(psQ[:, 0:NS], lhsT=onesm[:, :], rhs=ysq[:, c, :], start=(c == 0), stop=(c == 3)),
                   reads=[bycv, bC], writes=[bpsQ] if c == 0 else (), part=() if c == 0 else [bpsQ], inc=(c == 3))
            op("act", lambda e: e.copy(out=lnS[:, 0, :], in_=psM[:, 0:NS]), reads=[bpsM], writes=[blnS])
            op("act", lambda e: e.activation(out=lnS[:, 1, :], in_=psM[:, 0:NS], func=AF.Square), reads=[bpsM], part=[blnS])
            op("dve", lambda e: e.tensor_tensor(out=lnS[:, 2, :], in0=psQ[:, 0:NS], in1=lnS[:, 1, :], op=ALU.subtract), reads=[bpsQ, blnS], part=[blnS])
            op("act", lambda e: e.activation(out=lnS[:, 2, :], in_=lnS[:, 2, :], func=AF.Sqrt, bias=eps_t[:, :], scale=1.0), reads=[blnS, bC], part=[blnS])
            op("dve", lambda e: e.reciprocal(out=lnS[:, 2, :], in_=lnS[:, 2, :]), reads=[blnS], part=[blnS])
            op("dve", lambda e: e.tensor_tensor(out=t1[:, :, :], in0=ycv[:, :, :], in1=lnS[:, 0, :].unsqueeze(1).to_broadcast([128, 4, NS]), op=ALU.subtract),
               reads=[bycv, blnS], writes=[bt1])
            op("dve", lambda e: e.tensor_tensor(out=t1[:, :, :], in0=t1[:, :, :], in1=lnS[:, 2, :].unsqueeze(1).to_broadcast([128, 4, NS]), op=ALU.mult),
               reads=[bt1, blnS], writes=[bt1])
            for c in range(4):
                op("act", lambda e: e.activation(out=t2[:, c, :], in_=t1[:, c, :], func=AF.Silu, scale=lng[:, l, c:c + 1], bias=lnb[:, l, c:c + 1]),
                   reads=[bt1, bC], part=[bt1])
            op("dve", lambda e: e.tensor_tensor(out=yS[:, 0, :, :], in0=t2[:, :, :], in1=agS[:, :, :], op=ALU.mult), reads=[bt1, bsA], writes=[byS[0]])
            ps, bps = psum()
            for c in range(4):
                op("pe", lambda e: e.transpose(out=ps[0:NS, c * 128:(c + 1) * 128], in_=uS[:, c, :], identity=ident_f[:, :]),
                   reads=[bsA, bC], writes=[bps] if c == 0 else (), part=() if c == 0 else [bps], inc=(c == 3))
            op("act", lambda e: e.copy(out=utok[:, :], in_=ps[0:NS, :]), reads=[bps], writes=[butok])
            for b in range(4):
                dma("sp", convs[l, b, 26:30, :], utok[4 * b:4 * b + 4, :], reads=[butok], own=butok, is_output=True)
            Tk.barrier(); ph.close()

        def sample_hgrn(l):
            U16 = mybir.dt.uint16
            ph = ExitStack()
            a = lambda n_, shp, dt=F32: ph.enter_context(nc.sbuf_tensor(f"{n_}_h{l}", list(shp), dt))
            S0 = a("S0", [128, 4, 4, 128]); bS0 = [B(f"S0{b}") for b in range(4)]
            S0b = [a(f"S0b{i}", [128, 128], BF16) for i in range(2)]; bS0b = [B(f"S0b{i}") for i in range(2)]
            rmS = a("rmS", [128, NS]); bdmk = a("bdmk", [NS, NS], BF16); rowm = a("rowm", [NS, 4]); bhc = B("hconst")
            w = a("w", [128, 8, NS]); bw_ = B("w")
            qh_ = a("qh_", [128, NS], BF16); kt_ = a("kt_", [128, NS], BF16); kh_ = a("kh_", [128, NS], BF16); bqk = B("qk")
            attS_ = a("attS_", [NS, NS], BF16); battS_ = B("attS_")
            khT_ = a("khT_", [NS, 128], BF16); khm = a("khm", [NS, 4, 128], BF16); bkhm = B("khm")
            oS = a("oS", [128, NS]); boS = B("oS"); osq_ = a("osq_", [128, NS], BF16); ors = a("ors", [128, NS]); bors_ = B("ors")
            ebe_ = a("ebe_", [128, 4]); bebe_ = B("ebe_")
            for b in range(4):
                dma("sp", S0[:, b, :, :], shg_in[l, b].rearrange("h d v -> d h v"), writes=[bS0[b]])
            dma("sp", rmS[:, :], cst["rmS"], part=[bhc]); dma("pool", bdmk[:, :], cst["bdm"], part=[bhc]); dma("sp", rowm[:, :], cst["rowmask"], part=[bhc])
            op("pool", lambda e: e.memset(attS_[:, :], 0.0), writes=[battS_])
            srot = 0
            for h in range(4):
                op("dve", lambda e: e.tensor_scalar(out=w[:, 0, :], in0=sigS[:, h, :], scalar1=oml[:, l, h:h + 1], scalar2=lbm[:, l, h:h + 1], op0=ALU.mult, op1=ALU.add),
                   reads=[bsC, blb, bw_], writes=[bw_])
                op("act", lambda e: e.activation(out=w[:, 0, :], in_=w[:, 0, :], func=AF.Ln), reads=[bw_], part=[bw_])
                op("dve", lambda e: e.tensor_scalar(out=w[:, 1, :], in0=sigS[:, h, :], scalar1=noml[:, l, h:h + 1], scalar2=oml[:, l, h:h + 1], op0=ALU.mult, op1=ALU.add),
                   reads=[bsC, blb, bw_], part=[bw_])
                op("dve", lambda e: e.tensor_tensor_scan(out=w[:, 2, :], data0=rmS[:, :], data1=w[:, 0, :], initial=0.0, op0=ALU.mult, op1=ALU.add),
                   reads=[bw_, bhc], part=[bw_])
                op("act", lambda e: e.activation(out=w[:, 3, :], in_=w[:, 2, :], func=AF.Exp), reads=[bw_], part=[bw_])
                op("act", lambda e: e.activation(out=w[:, 4, :], in_=w[:, 2, :], func=AF.Exp, scale=-1.0), reads=[bw_], part=[bw_])
                B3 = w[:, 2, :].rearrange("p (b t) -> p b t", t=4)
                op("dve", lambda e: e.tensor_tensor(out=w[:, 5, :].rearrange("p (b t) -> p b t", t=4), in0=B3[:, :, 3:4].to_broadcast([128, 4, 4]), in1=B3, op=ALU.subtract),
                   reads=[bw_], part=[bw_])
                op("act", lambda e: e.activation(out=w[:, 5, :], in_=w[:, 5, :], func=AF.Exp), reads=[bw_], part=[bw_])
                op("dve", lambda e: e.tensor_copy(out=ebe_[:, :], in_=w[:, 3, :].rearrange("p (b t) -> p b t", t=4)[:, :, 3]), reads=[bw_], writes=[bebe_])
                op("dve", lambda e: e.tensor_tensor(out=qh_[:, :], in0=qcS[:, h, :], in1=w[:, 3, :], op=ALU.mult), reads=[bsC, bw_], writes=[bqk])
                op("dve", lambda e: e.tensor_tensor(out=kt_[:, :], in0=w[:, 1, :], in1=w[:, 4, :], op=ALU.mult), reads=[bw_], part=[bqk])
                op("dve", lambda e: e.tensor_tensor(out=kh_[:, :], in0=w[:, 1, :], in1=w[:, 5, :], op=ALU.mult), reads=[bw_], part=[bqk])
                ps, bps = psum()
                op("pe", lambda e: e.matmul(ps[0:NS, 0:NS], lhsT=kt_[:, :], rhs=qh_[:, :], start=True, stop=True), reads=[bqk], writes=[bps])
                op("dve", lambda e: e.copy_predicated(out=attS_[:, :], mask=bdmk[:, :].bitcast(U16), data=ps[0:NS, 0:NS]), reads=[bps, bhc], part=[battS_])
                ps2, bps2 = psum()
                op("pe", lambda e: e.matmul(ps2[:, 0:NS], lhsT=ciS[:, h, :], rhs=attS_[:, :], start=True, stop=False), reads=[bsC, battS_], writes=[bps2], inc=False)
                for b in range(4):
                    x = srot % 2
                    srot += 1
                    op("act", lambda e: e.copy(out=S0b[x][:, :], in_=S0[:, b, h, :]), reads=[bS0[b]], writes=[bS0b[x]])
                    op("pe", lambda e: e.matmul(ps2[:, 4 * b:4 * b + 4], lhsT=S0b[x][:, :], rhs=qh_[:, 4 * b:4 * b + 4], start=False, stop=(b == 3)),
                       reads=[bS0b[x], bqk], part=[bps2], inc=True)
                op("act", lambda e: e.copy(out=oS[:, :], in_=ps2[:, 0:NS]), reads=[bps2], writes=[boS])
                psT, bpsT = psum()
                pbT = psT[:].bitcast(BF16)
                op("pe", lambda e: e.transpose(out=pbT[0:NS, 0:128], in_=kh_[:, :], identity=ident_b[:, :]), reads=[bqk, bC], writes=[bpsT])
                op("act", lambda e: e.copy(out=khT_[:, :], in_=pbT[0:NS, 0:128]), reads=[bpsT], writes=[bkhm])
                for b in range(4):
                    op("dve", lambda e: e.tensor_scalar(out=khm[:, b, :], in0=khT_[:, :], scalar1=rowm[:, b:b + 1], scalar2=None, op0=ALU.mult), reads=[bkhm, bhc], part=[bkhm])
                for b in range(4):
                    ps3, bps3 = psum()
                    op("pe", lambda e: e.matmul(ps3[:, 0:128], lhsT=khm[:, b, :], rhs=ciS[:, h, :], start=True, stop=True), reads=[bkhm, bsC], writes=[bps3])
                    op("dve", lambda e: e.scalar_tensor_tensor(out=S0[:, b, h, :], in0=S0[:, b, h, :], scalar=ebe_[:, b:b + 1], in1=ps3[:, 0:128], op0=ALU.mult, op1=ALU.add),
                       reads=[bps3, bebe_], part=[bS0[b]])
                op("act", lambda e: e.activation(out=osq_[:, :], in_=oS[:, :], func=AF.Square), reads=[boS], writes=[bors_])
                ps, bps = psum()
                op("pe", lambda e: e.matmul(ps[:, 0:NS], lhsT=ones_b[:, :], rhs=osq_[:, :], start=True, stop=True), reads=[bors_, bC], writes=[bps])
                op("act", lambda e: e.activation(out=ors[:, :], in_=ps[:, 0:NS], func=AF.Sqrt, bias=eps_t[:, :], scale=1.0 / 128), reads=[bps, bC], part=[bors_])
                op("dve", lambda e: e.reciprocal(out=ors[:, :], in_=ors[:, :]), reads=[bors_], part=[bors_])
                op("dve", lambda e: e.tensor_tensor(out=ors[:, :], in0=oS[:, :], in1=ors[:, :], op=ALU.mult), reads=[boS, bors_], part=[bors_])
                op("dve", lambda e: e.tensor_tensor(out=yS[:, 2, h, :], in0=ors[:, :], in1=cgS[:, h, :], op=ALU.mult), reads=[bors_, bsC], part=[byS[2]])
            for b in range(4):
                dma("sp", hgs[l, b].rearrange("h d v -> d h v"), S0[:, b, :, :], reads=[bS0[b]], own=bS0[b], is_output=True)
            Tk.barrier(); ph.close()

        for hf in cfg.get('halves', [0, 1]):
            for l in range(NL):
                if l == 0:
                    for tt in range(NTT):
                        r0 = hf * TS + tt * 128
                        dma("sp", X[:, tt, :], xp[r0:r0 + 128, :], writes=[bX[tt]])
                    if hf == 0 and do_sample:
                        dma("sp", XS[:], xs, writes=[bXS])
                for tt in range(NTT):
                    make_hT(l, X[:, tt, :], 128, bX[tt], tt * 128, bhT[tt], [(0, 128, 0)])
                samp = (hf == 0 and do_sample)
                if samp:
                    make_hT(l, XS[:], NS, bXS, TS, bhT[NTT], [(4 * b, 4 * b + 4, 1 + b) for b in range(4)])
                tiles = [(tt, 128, tt * 128) for tt in range(NTT)] + ([(NTT, NS, TS)] if samp else [])

                phB = ExitStack()

                def sp_(name, shape, dt=F32, _ph=phB):
                    return _ph.enter_context(nc.sbuf_tensor(f"{name}_{hf}{l}", list(shape), dt))

                qtok = sp_("qtok", [128, 512]); bqtok = B("qtok")
                qrope = sp_("qrope", [128, 512], BF16); bqr = B("qrope")
                ostage = sp_("ostage", [128, 320]); bos = B("ostage")
                ktok = sp_("ktok", [128, 448]); bktok = B("ktok")
                kdup = sp_("kdup", [128, 2, 2, 64], BF16); kidup = sp_("kidup", [128, 2, 64], BF16); bkdup = B("kdup")
                qirope = sp_("qirope", [128, 256], BF16); bqir = B("qirope")
                qT = sp_("qT", [128, 4, TS], BF16); bqT = [B(f"qT{i}") for i in range(NTT)]
                qiT = sp_("qiT", [128, 2, TS], BF16)
                wiS = sp_("wiS", [128, NTT, 4])
                bgT = sp_("bgT", [128, 4, TW], BF16); bbg = [B(f"bg{i}") for i in range(3)]
                ybT = sp_("ybT", [128, 4, TW], BF16); byb = [B(f"yb{i}") for i in range(NTT + 1)]
                Isc = sp_("Isc", [128, T]); bI = B("Isc")
                maskb = sp_("maskb", [128, T], BF16); bmk = B("maskb")
                maskT = sp_("maskT", [128, 16, 128], BF16); bmT = B("maskT")
                rl = [qtok] * 2; brl = [bqtok] * 2
                Eb = [sp_(f"Eb{i}", [128, 512], BF16) for i in range(2)]; bEb = [B(f"Eb{i}") for i in range(2)]
                Pb = [sp_(f"Pb{i}", [128, 512], BF16) for i in range(2)]; bPb = [B(f"Pb{i}") for i in range(2)]
                rden = scr[:, 0:512]; brden = bscr
                otmp, botmp = rden, brden
                w2, bw2 = wload(w_in[l][:, C_ZK:C_ZK + 512], 8, 512, key=("b2", hf, l))
                w3, bw3 = wload(w_in[l][:, C_ZKI:C_ZKI + 68], 8, 68, key=("b3", hf, l))
                w1, bw1 = wload(w_in[l][:, C_ZQ:C_ZQ + 512], 8, 512, key=("b1", hf, l))
                for (tt, npt, c0) in tiles:
                    is_s = (tt == NTT)
                    gkb = hf * NTT + tt
                    cosT = cos_s[:, :] if is_s else cos_p[:, gkb, :]
                    sinT = sin_s[:, :] if is_s else sin_p[:, gkb, :]

                    def proj(wt, bwt, ncols):
                        ps, bps = psum()
                        for k in range(8):
                            op("pe", lambda e: e.matmul(ps[0:npt, 0:ncols], lhsT=hT[:, k, c0:c0 + npt], rhs=wt[:, k, 0:ncols],
                                                        start=(k == 0), stop=(k == 7)),
                               reads=[bwt, bhT[tt]], writes=[bps] if k == 0 else (), part=() if k == 0 else [bps], inc=(k == 7))
                        return ps, bps

                    def rms(ps_ap, bps, nh, dst, bdst, gsel, first):
                        n = nh * 64
                        v3 = lambda ap: ap.rearrange("p (h d) -> p h d", d=64)
                        op("act", lambda e: e.activation(out=scr[0:npt, 0:n], in_=ps_ap, func=AF.Square), reads=[bps], writes=[bscr])
                        op("dve", lambda e: e.tensor_reduce(out=st8[0:npt, 8:8 + nh], in_=v3(scr[0:npt, 0:n]), axis=AX.X, op=ALU.add),
                           reads=[bscr], part=[bst8])
                        op("act", lambda e: e.activation(out=st8[0:npt, 16:16 + nh], in_=st8[0:npt, 8:8 + nh], func=AF.Sqrt,
                                                         bias=eps_t[0:npt, :], scale=1.0 / 64), reads=[bst8, bC], part=[bst8])
                        op("dve", lambda e: e.reciprocal(out=st8[0:npt, 24:24 + nh], in_=st8[0:npt, 16:16 + nh]), reads=[bst8], part=[bst8])
                        op("dve", lambda e: e.tensor_tensor(out=v3(dst), in0=v3(ps_ap),
                                                            in1=st8[0:npt, 24:24 + nh].unsqueeze(2).to_broadcast([npt, nh, 64]), op=ALU.mult),
                           reads=[bst8, bps], **(dict(writes=[bdst]) if first else dict(part=[bdst])))
                        if gsel is not None:
                            op("dve", lambda e: e.tensor_tensor(out=v3(dst), in0=v3(dst),
                                                                 in1=gsel[0:npt, l, :].unsqueeze(1).to_broadcast([npt, nh, 64]), op=ALU.mult),
                               reads=[bdst, bC], part=[bdst])

                    ps, bps = proj(w2, bw2, 512)
                    rms(ps[0:npt, 0:128], bps, 2, ktok[0:npt, 0:128], bktok, kg, True)
                    rope(ktok[0:npt, 0:128], bktok, 2, npt, cosT[0:npt], sinT[0:npt], ostage[0:npt, 0:128], bos, eng="dve")
                    op("act", lambda e: e.copy(out=ostage[0:npt, 128:256], in_=ps[0:npt, 128:256]), reads=[bps], part=[bos])
                    if not is_s:
                        op("act", lambda e: e.copy(out=ktok[0:npt, 192:448], in_=ps[0:npt, 256:512]), reads=[bps], part=[bktok])
                    ps3, bps3 = proj(w3, bw3, 68)
                    rms(ps3[0:npt, 0:64], bps3, 1, ktok[0:npt, 128:192], bktok, None, False)
                    rope(ktok[0:npt, 128:192], bktok, 1, npt, cosT[0:npt], sinT[0:npt], ostage[0:npt, 256:320], bos, part_out=True, eng="dve")
                    if not is_s:
                        op("act", lambda e: e.copy(out=wiS[:, tt, :], in_=ps3[:, 64:68]), reads=[bps3], part=[bqT[tt]])
                    if is_s:
                        dma("sp", ks[l], ostage[0:NS, 0:128], reads=[bos], own=bos, is_output=True)
                        dma("sp", vs[l], ostage[0:NS, 128:256], reads=[bos], own=bos, is_output=True)
                        dma("sp", iks[l], ostage[0:NS, 256:320], reads=[bos], own=bos, is_output=True)
                        op("act", lambda e: e.copy(out=sKV[:, :], in_=ostage[0:NS, :]), reads=[bos], writes=[bsQ])
                        op("act", lambda e: e.copy(out=sWI[:, :], in_=ps3[0:NS, 64:68]), reads=[bps3], part=[bsQ])
                        op("act", lambda e: e.copy(out=ktok[0:NS, 192:448], in_=ps[0:NS, 256:512]), reads=[bps], part=[bktok])
                        psq, bpsq = proj(w1, bw1, 512)
                        rms(psq[0:NS, :], bpsq, 8, qtok[0:NS, :], bqtok, qg, True)
                        rope(qtok[0:NS, :], bqtok, 8, NS, cosT, sinT, sQ[:, :], bsQ, part_out=True)
                        rope(ktok[0:NS, 192:448], bktok, 4, NS, cosT, sinT, sQI[:, :], bsQ, part_out=True)
                        continue
                    r0 = hf * TS + tt * 128
                    dma("sp", kp[l, r0:r0 + 128, :], ostage[:, 0:128], reads=[bos], own=bos, is_output=True)
                    dma("sp", vp[l, r0:r0 + 128, :], ostage[:, 128:256], reads=[bos], own=bos, is_output=True)
                    dma("sp", ikp[l, r0:r0 + 128, :], ostage[:, 256:320], reads=[bos], own=bos, is_output=True)
                    if stage <= 1:
                        continue
                    bk = bK(l, gkb)
                    op("pool", lambda e: e.tensor_copy(out=kdup[:, :, :, :],
                                                       in_=ostage[:, 0:128].rearrange("p (g d) -> p g d", d=64).unsqueeze(2).to_broadcast([128, 2, 2, 64])),
                       reads=[bos], writes=[bkdup])
                    op("pool", lambda e: e.tensor_copy(out=kidup[:, :, :], in_=ostage[:, 256:320].unsqueeze(1).to_broadcast([128, 2, 64])),
                       reads=[bos], part=[bkdup])
                    vdst = VpA[:, l, tt, :, :] if hf == 0 else VpB[:, tt, :, :]
                    op("pool", lambda e: e.tensor_copy(out=vdst.rearrange("p g (r d) -> p g r d", d=64),
                                                       in_=ostage[:, 128:256].rearrange("p (g d) -> p g d", d=64).unsqueeze(2).to_broadcast([128, 2, 2, 64])),
                       reads=[bos], writes=[bk])
                    psq, bpsq = proj(w1, bw1, 512)
                    rms(psq[:, :], bpsq, 8, qtok[:, :], bqtok, qg, True)
                    rope(qtok[:, :], bqtok, 8, 128, cosT, sinT, qrope[:, :], bqr, eng="dve")
                    rope(ktok[:, 192:448], bktok, 4, 128, cosT, sinT, qirope[:, :], bqir, eng="dve")
                    psT, bpsT = psum()
                    pb = psT[:].bitcast(BF16)
                    srcs = [qrope[:, c * 128:(c + 1) * 128] for c in range(4)] + [qirope[:, c * 128:(c + 1) * 128] for c in range(2)] \
                        + [kdup[:, g, :, :].rearrange("p r d -> p (r d)") for g in range(2)]
                    rds = [bqr] * 4 + [bqir] * 2 + [bkdup] * 2
                    for si, (sap, rb) in enumerate(zip(srcs, rds)):
                        op("pe", lambda e: e.transpose(out=pb[:, si * 128:(si + 1) * 128], in_=sap, identity=ident_b[:, :]),
                           reads=[rb, bC], writes=[bpsT] if si == 0 else (), part=() if si == 0 else [bpsT], inc=(si == 7))
                    psT2, bpsT2 = psum()
                    pb2 = psT2[:].bitcast(BF16)
                    op("pe", lambda e: e.transpose(out=pb2[:, 0:128], in_=kidup[:, :, :].rearrange("p r d -> p (r d)"), identity=ident_b[:, :]),
                       reads=[bkdup, bC], writes=[bpsT2])
                    tok = slice(tt * 128, (tt + 1) * 128)
                    op("act", lambda e: e.copy(out=qT[:, :, tok], in_=pb[:, 0:512].rearrange("p (c t) -> p c t", t=128)), reads=[bpsT], part=[bqT[tt]])
                    op("act", lambda e: e.copy(out=qiT[:, :, tok], in_=pb[:, 512:768].rearrange("p (c t) -> p c t", t=128)), reads=[bpsT], part=[bqT[tt]])
                    kdst = kdA[:, l, :, tok] if hf == 0 else kdB[:, :, tok]
                    op("act", lambda e: e.copy(out=kdst, in_=pb[:, 768:1024].rearrange("p (g t) -> p g t", t=128)), reads=[bpsT], part=[bk])
                    kidst = kidA[:, l, tok] if hf == 0 else kidB[:, tok]
                    op("act", lambda e: e.copy(out=kidst, in_=pb2[:, 0:128]), reads=[bpsT2], part=[bk])
                if stage <= 1:
                    Tk.barrier(); phB.close()
                    continue

                ntl = [(0, 0, 512, [bhT[i] for i in range(4)]), (1, 512, 512, [bhT[i] for i in range(4, 8)])] + ([(2, TS, NS, [bhT[NTT]])] if samp else [])
                wg, bwg = wload(w_in[l][:, C_BG:C_BG + 512], 8, 512)
                for c in range(4):
                    for (ni, n0, nn, bhs) in ntl:
                        ps, bps = psum()
                        for k in range(8):
                            op("pe", lambda e: e.matmul(ps[:, 0:nn], lhsT=wg[:, k, c * 128:(c + 1) * 128], rhs=hT[:, k, n0:n0 + nn], start=(k == 0), stop=(k == 7)),
                               reads=[bwg] + bhs, writes=[bps] if k == 0 else (), part=() if k == 0 else [bps], inc=(k == 7))
                        bg_dst = bgS[:, c, :] if ni == 2 else bgT[:, c, n0:n0 + nn]
                        op("act", lambda e: e.activation(out=bg_dst, in_=ps[:, 0:nn], func=AF.Silu), reads=[bps], part=[bbgS if ni == 2 else bbg[ni]])

                NIT = 12

                def blk(i):
                    gi = hf * NTT + i
                    return gi, gi + 1, (gi + 1) * 128, slice(i * 128, (i + 1) * 128)

                def stage1a(i):
                    gi, nkb, n, tq = blk(i)
                    if gi >= 2:
                        for kgp in range((nkb + 3) // 4):
                            nb = min(4, nkb - 4 * kgp)
                            ncol = nb * 128
                            kbufs = [bK(l, 4 * kgp + x) for x in range(nb)]
                            for h in range(4):
                                c, j = h // 2, h % 2
                                ps, bps = psum()
                                op("pe", lambda e: e.matmul(ps[:, 0:ncol], lhsT=qiT[j * 64:(j + 1) * 64, c, tq],
                                                            rhs=kid_ap(l, 4 * kgp, nb)[j * 64:(j + 1) * 64, :], start=True, stop=True),
                                   reads=[bqT[i]] + kbufs, writes=[bps])
                                r_, br_ = rl[h % 2], brl[h % 2]
                                op("act", lambda e: e.activation(out=r_[:, 0:ncol], in_=ps[:, 0:ncol], func=AF.Relu), reads=[bps], writes=[br_])
                                dstI = Isc[:, kgp * 512:kgp * 512 + ncol]
                                if h == 0:
                                    op("dve", lambda e: e.tensor_scalar(out=dstI, in0=r_[:, 0:ncol], scalar1=wiS[:, i, 0:1], scalar2=None, op0=ALU.mult),
                                       reads=[br_, bqT[i]], **(dict(writes=[bI]) if kgp == 0 else dict(part=[bI])))
                                else:
                                    op("dve", lambda e: e.scalar_tensor_tensor(out=dstI, in0=r_[:, 0:ncol], scalar=wiS[:, i, h:h + 1], in1=dstI,
                                                                               op0=ALU.mult, op1=ALU.add), reads=[br_, bqT[i]], part=[bI])
                        op("dve", lambda e: e.tensor_reduce(out=bis[:, 0:1], in_=Isc[:, 0:n], axis=AX.X, op=ALU.max, apply_absolute_value=True),
                           reads=[bI], writes=[bbis])
                        op("dve", lambda e: e.tensor_scalar(out=bis[:, 0:1], in0=bis[:, 0:1], scalar1=1.0, scalar2=None, op0=ALU.add), reads=[bbis], part=[bbis])
                        op("dve", lambda e: e.tensor_scalar(out=bis[:, 16:16 + NIT + 2], in0=pw2[:, 0:NIT + 2], scalar1=bis[:, 0:1], scalar2=None, op0=ALU.mult),
                           reads=[bC, bbis], part=[bbis])
                        op("dve", lambda e: e.memset(bis[:, 1:2], 0.0), part=[bbis])
                        op("dve", lambda e: e.tensor_tensor(out=Isc[:, gi * 128:(gi + 1) * 128], in0=Isc[:, gi * 128:(gi + 1) * 128], in1=caus_neg[:, :], op=ALU.add),
                           reads=[bC], part=[bI])
                        for it in range(NIT + 1):
                            if it < NIT:
                                op("dve", lambda e: e.tensor_scalar(out=maskb[:, 0:n], in0=Isc[:, 0:n], scalar1=bis[:, 1:2], scalar2=None, op0=ALU.is_ge,
                                                                    op1=ALU.add, accum_out=bis[:, 2:3]), reads=[bI, bbis], writes=[bmk], part=[bbis])
                                op("dve", lambda e: e.scalar_tensor_tensor(out=bis[:, 3:4], in0=bis[:, 2:3], scalar=256.0, in1=bis[:, 16 + it:17 + it],
                                                                           op0=ALU.is_ge, op1=ALU.mult), reads=[bbis], part=[bbis])
                                op("dve", lambda e: e.scalar_tensor_tensor(out=bis[:, 1:2], in0=bis[:, 3:4], scalar=bis[:, 17 + it:18 + it], in1=bis[:, 1:2],
                                                                           op0=ALU.subtract, op1=ALU.add), reads=[bbis], part=[bbis])
                            else:
                                op("dve", lambda e: e.tensor_tensor(out=bis[:, 4:5], in0=bis[:, 1:2], in1=bis[:, 16 + it:17 + it], op=ALU.subtract), reads=[bbis], part=[bbis])
                        op("dve", lambda e: e.tensor_scalar(out=maskb[:, 0:n], in0=Isc[:, 0:n], scalar1=bis[:, 4:5], scalar2=None, op0=ALU.is_ge),
                           reads=[bI, bbis], writes=[bmk])

                def stage1b(i):
                    gi, nkb, n, tq = blk(i)
                    if gi >= 2:
                        for b8 in range((nkb + 7) // 8):
                            nb = min(8, nkb - 8 * b8)
                            psT, bpsT = psum()
                            pb = psT[:].bitcast(BF16)
                            for x in range(nb):
                                kb = 8 * b8 + x
                                op("pe", lambda e: e.transpose(out=pb[:, x * 128:(x + 1) * 128], in_=maskb[:, kb * 128:(kb + 1) * 128], identity=ident_b[:, :]),
                                   reads=[bmk, bC], writes=[bpsT] if x == 0 else (), part=() if x == 0 else [bpsT], inc=(x == nb - 1))
                            op("act", lambda e: e.copy(out=maskT[:, 8 * b8:8 * b8 + nb, :], in_=pb[:, 0:nb * 128].rearrange("p (k t) -> p k t", t=128)),
                               reads=[bpsT], **(dict(writes=[bmT]) if b8 == 0 else dict(part=[bmT])))
                    else:
                        for kb in range(nkb):
                            src = triT if kb == gi else ones_b
                            op("pool", lambda e: e.tensor_copy(out=maskT[:, kb, :], in_=src[:, :]), reads=[bC],
                               **(dict(writes=[bmT]) if kb == 0 else dict(part=[bmT])))

                def stage2(i):
                    gi, nkb, n, tq = blk(i)
                    steps = [(g, kb) for g in range(2) for kb in range(nkb)]

                    def front(si):
                        g, kb = steps[si]
                        x = si % 2
                        for j in range(2):
                            psS, bpsS = psum()
                            for cc_ in range(2):
                                op("pe", lambda e: e.matmul(psS[:, cc_ * 128:(cc_ + 1) * 128], lhsT=kd_ap(l, kb, g)[j * 64:(j + 1) * 64, :],
                                                            rhs=qT[j * 64:(j + 1) * 64, 2 * g + cc_, tq], start=True, stop=True),
                                   reads=[bK(l, kb), bqT[i]], writes=[bpsS] if cc_ == 0 else (), part=() if cc_ == 0 else [bpsS], inc=(cc_ == 1))
                            op("act", lambda e: e.activation(out=Eb[x][:, j * 256:(j + 1) * 256], in_=psS[:, 0:256], func=AF.Exp, scale=0.125),
                               reads=[bpsS], **(dict(writes=[bEb[x]]) if j == 0 else dict(part=[bEb[x]])))
                        op("pool", lambda e: e.tensor_tensor(out=Pb[x][:, :].rearrange("p (h t) -> p h t", t=128),
                                                             in0=Eb[x][:, :].rearrange("p (h t) -> p h t", t=128),
                                                             in1=maskT[:, kb, :].unsqueeze(1).to_broadcast([128, 4, 128]), op=ALU.mult),
                           reads=[bEb[x], bmT], writes=[bPb[x]])

                    def back(si):
                        g, kb = steps[si]
                        x = si % 2
                        psO, bpsO = PS[4 + g], bPS[4 + g]
                        psD, bpsD = PS[6 + g], bPS[6 + g]
                        op("pe", lambda e: e.matmul(psO[:, :], lhsT=vp_ap(l, kb, g), rhs=Pb[x][:, :], start=(kb == 0), stop=(kb == nkb - 1)),
                           reads=[bK(l, kb), bPb[x]], writes=[bpsO] if kb == 0 else (), part=() if kb == 0 else [bpsO], inc=(kb == nkb - 1))
                        op("pe", lambda e: e.matmul(psD[:, :], lhsT=ones_b[:, :], rhs=Pb[x][:, :], start=(kb == 0), stop=(kb == nkb - 1)),
                           reads=[bC, bPb[x]], writes=[bpsD] if kb == 0 else (), part=() if kb == 0 else [bpsD], inc=(kb == nkb - 1))

                    front(0)
                    for si in range(len(steps)):
                        if si + 1 < len(steps):
                            front(si + 1)
                        back(si)
                    for g in range(2):
                        psO, bpsO = PS[4 + g], bPS[4 + g]
                        psD, bpsD = PS[6 + g], bPS[6 + g]
                        op("act", lambda e: e.activation(out=rden[:, :], in_=psD[:, :], func=AF.Ln), reads=[bpsD], writes=[brden])
                        op("act", lambda e: e.activation(out=rden[:, :], in_=rden[:, :], func=AF.Exp, scale=-1.0), reads=[brden], writes=[brden])
                        op("dve", lambda e: e.tensor_tensor(out=otmp[:, :], in0=psO[:, :], in1=rden[:, :], op=ALU.mult), reads=[bpsO, brden], writes=[botmp])
                        o4 = otmp[:, :].rearrange("p (j c t) -> p c j t", c=2, j=2)
                        for cc_ in range(2):
                            for j in range(2):
                                rows = slice(j * 64, (j + 1) * 64)
                                c = 2 * g + cc_
                                op("pool", lambda e: e.tensor_tensor(out=ybT[rows, c, tq], in0=o4[rows, cc_, j, :], in1=bgT[rows, c, tq], op=ALU.mult),
                                   reads=[botmp, bbg[i // 4]], part=[byb[i]])

                nq_ = cfg.get('nq', NTT)
                if nq_ > 0:
                    stage1a(0); stage1b(0)
                for i in range(nq_):
                    if i + 1 < nq_:
                        stage1a(i + 1)
                    stage2(i)
                    if i + 1 < nq_:
                        stage1b(i + 1)
                if dbg is not None and l == 0:
                    dma("pool", dbg[hf].rearrange("p (c t) -> p c t", c=4), ybT[:, :, :], reads=byb, own=byb[0], is_output=True)
                if stage <= 2:
                    Tk.barrier(); phB.close()
                    continue

                mrot = {"i": 0}

                def merge(yT, ybufs, wproj, gcol, first, sub=None, ysrc=None):
                    sub_ = ntl[0:2] if sub is None else sub
                    for b_ in bmgs + bmtmp:
                        b_.rd.update(bscr.rd); b_.rd.update(bscr.lw)
                    wp, bwp = wload(wproj[l], 4, 1024)
                    for hc in range(2):
                        wg_, bwg_ = wload(w_in[l][:, gcol + hc * 512:gcol + (hc + 1) * 512], 8, 512)
                        for c4 in range(4):
                            c = hc * 4 + c4
                            for (ni, n0, nn, bhs) in sub_:
                                psP, bpsP = psum()
                                for k in range(4):
                                    op("pe", lambda e: e.matmul(psP[:, 0:nn], lhsT=wp[:, k, c * 128:(c + 1) * 128],
                                                                rhs=(ysrc(k) if ysrc is not None else yT[:, k, n0:n0 + nn]),
                                                                start=(k == 0), stop=(k == 3)),
                                       reads=[bwp] + ybufs[ni], writes=[bpsP] if k == 0 else (), part=() if k == 0 else [bpsP], inc=(k == 3))
                                psG, bpsG = psum()
                                for k in range(8):
                                    op("pe", lambda e: e.matmul(psG[:, 0:nn], lhsT=wg_[:, k, c4 * 128:(c4 + 1) * 128], rhs=hT[:, k, n0:n0 + nn],
                                                                start=(k == 0), stop=(k == 7)),
                                       reads=[bwg_] + bhs, writes=[bpsG] if k == 0 else (), part=() if k == 0 else [bpsG], inc=(k == 7))
                                x = mrot["i"] % 2
                                mrot["i"] += 1
                                op("act", lambda e: e.activation(out=mgs[x][:, 0:nn], in_=psG[:, 0:nn], func=AF.Sigmoid), reads=[bpsG], writes=[bmgs[x]])
                                if first:
                                    op("dve", lambda e: e.tensor_tensor(out=mT[:, c, n0:n0 + nn], in0=psP[:, 0:nn], in1=mgs[x][:, 0:nn], op=ALU.mult),
                                       reads=[bpsP, bmgs[x]], part=[bmT_[ni]])
                                else:
                                    op("dve", lambda e: e.tensor_tensor(out=mtmp[x][:, 0:nn], in0=psP[:, 0:nn], in1=mgs[x][:, 0:nn], op=ALU.mult),
                                       reads=[bpsP, bmgs[x]], writes=[bmtmp[x]])
                                    op("pool", lambda e: e.tensor_tensor(out=mT[:, c, n0:n0 + nn], in0=mT[:, c, n0:n0 + nn], in1=mtmp[x][:, 0:nn], op=ALU.add),
                                       reads=[bmtmp[x]], part=[bmT_[ni]])

                ybufs = [byb[0:4], byb[4:8], [byb[NTT]]]
                merge(ybT, ybufs, w_pb, C_GB, True)
                wprefetch(("av", hf, l), w_in[l][:, C_A_VAL:C_A_VAL + 512], 8, 512)
                wprefetch(("agl", hf, l), w_in[l][:, C_A_GLU:C_A_GLU + 512], 8, 512)
                Tk.barrier(); phB.close()

                phA = ExitStack()

                def sa_(name, shape, dt=F32, _ph=phA):
                    return _ph.enter_context(nc.sbuf_tensor(f"{name}_{hf}{l}", list(shape), dt))

                uT = sa_("uT", [128, 4, 30 + TS], BF16); buH = B("uH"); buT = [B("uT0"), B("uT1")]
                agT = sa_("agT", [128, 4, TW], BF16); bag = [B(f"ag{i}") for i in range(3)]
                dg = sa_("dg", [128, 31, 128], BF16); bdg = B("dg")
                yconv = sa_("yconv", [128, 4, TS], BF16); byc = [[B(f"yc{c}{n}") for n in range(2)] for c in range(4)]
                yaT = sa_("yaT", [128, 4, TW], BF16); bya = [B(f"ya{i}") for i in range(3)]
                sgA = [sa_(f"sgA{i}", [128, 512]) for i in range(2)]; bsgA = [B(f"sgA{i}") for i in range(2)]
                sqb = [sa_(f"sqb{i}", [128, 512], BF16) for i in range(2)]; bsqb = [B(f"sqb{i}") for i in range(2)]
                lnm = sa_("lnm", [128, 512]); lnr = sa_("lnr", [128, 512]); lnt = sa_("lnt", [128, 512]); bln = B("ln")
                u32 = sa_("u32", [128, 4, 30]); bu32 = B("u32")
                cvo = sa_("cvo", [30, 512]); bcvo = B("cvo")
                if hf == 0:
                    op("pool", lambda e: e.memset(uT[:, :, 0:30], 0.0), writes=[buH])
                else:
                    op("pool", lambda e: e.tensor_copy(out=uT[:, :, 0:30], in_=uhalo[:, l, :, :]), reads=[buh], writes=[buH])
                wv, bwv = wload(w_in[l][:, C_A_VAL:C_A_VAL + 512], 8, 512, key=("av", hf, l))
                wgl, bwgl = wload(w_in[l][:, C_A_GLU:C_A_GLU + 512], 8, 512, key=("agl", hf, l))
                wag, bwag = wload(w_in[l][:, C_A_GATE:C_A_GATE + 512], 8, 512, key=("ag", hf, l))
                arot = 0
                for c in range(4):
                    for (ni, n0, nn, bhs) in ntl:
                        def mm8(wt, bwt):
                            ps, bps = psum()
                            for k in range(8):
                                op("pe", lambda e: e.matmul(ps[:, 0:nn], lhsT=wt[:, k, c * 128:(c + 1) * 128], rhs=hT[:, k, n0:n0 + nn],
                                                            start=(k == 0), stop=(k == 7)),
                                   reads=[bwt] + bhs, writes=[bps] if k == 0 else (), part=() if k == 0 else [bps], inc=(k == 7))
                            return ps, bps
                        psV, bpsV = mm8(wv, bwv)
                        psG, bpsG = mm8(wgl, bwgl)
                        x = arot % 2
                        arot += 1
                        op("act", lambda e: e.activation(out=sgA[x][:, 0:nn], in_=psG[:, 0:nn], func=AF.Sigmoid), reads=[bpsG], writes=[bsgA[x]])
                        u_dst = uS[:, c, :] if ni == 2 else uT[:, c, 30 + n0:30 + n0 + nn]
                        op("dve", lambda e: e.tensor_tensor(out=u_dst, in0=psV[:, 0:nn], in1=sgA[x][:, 0:nn], op=ALU.mult),
                           reads=[bpsV, bsgA[x]], part=[bsA if ni == 2 else buT[ni]])
                        if hf == 1 and ni == 1:
                            op("dve", lambda e: e.tensor_tensor(out=u32[:, c, :], in0=psV[:, 482:512], in1=sgA[x][:, 482:512], op=ALU.mult),
                               reads=[bpsV, bsgA[x]], part=[bu32])
                for c in range(4):
                    for (ni, n0, nn, bhs) in ntl:
                        def mm8(wt, bwt):
                            ps, bps = psum()
                            for k in range(8):
                                op("pe", lambda e: e.matmul(ps[:, 0:nn], lhsT=wt[:, k, c * 128:(c + 1) * 128], rhs=hT[:, k, n0:n0 + nn],
                                                            start=(k == 0), stop=(k == 7)),
                                   reads=[bwt] + bhs, writes=[bps] if k == 0 else (), part=() if k == 0 else [bps], inc=(k == 7))
                            return ps, bps
                        psA, bpsA = mm8(wag, bwag)
                        ag_dst = agS[:, c, :] if ni == 2 else agT[:, c, n0:n0 + nn]
                        op("act", lambda e: e.activation(out=ag_dst, in_=psA[:, 0:nn], func=AF.Silu), reads=[bpsA], part=[bsA if ni == 2 else bag[ni]])
                if hf == 0:
                    op("pool", lambda e: e.tensor_copy(out=uhalo[:, l, :, :], in_=uT[:, :, TS:TS + 30]), reads=[buT[1]], writes=[buh])
                for c in range(4):
                    for j in range(31):
                        if j % 2 == 0:
                            op("act", lambda e: e.activation(out=dg[:, j, :], in_=ident_b[:, :], func=AF.Copy, scale=wdw[:, l, c, j:j + 1]),
                               reads=[bC], **(dict(writes=[bdg]) if j == 0 else dict(part=[bdg])))
                        else:
                            op("dve", lambda e: e.tensor_scalar(out=dg[:, j, :], in0=ident_b[:, :], scalar1=wdw[:, l, c, j:j + 1], scalar2=None, op0=ALU.mult),
                               reads=[bC], part=[bdg])
                    for nt in range(2):
                        ps, bps = psum()
                        rd = [bdg, buT[nt], buT[nt - 1] if nt > 0 else buH]
                        for j in range(31):
                            op("pe", lambda e: e.matmul(ps[:, :], lhsT=dg[:, j, :], rhs=uT[:, c, nt * 512 + j:nt * 512 + j + 512], start=(j == 0), stop=(j == 30)),
                               reads=rd, writes=[bps] if j == 0 else (), part=() if j == 0 else [bps], inc=(j == 30))
                        op("act", lambda e: e.activation(out=yconv[:, c, nt * 512:(nt + 1) * 512], in_=ps[:, :], func=AF.Identity,
                                                         bias=bdw[:, l, c:c + 1], scale=1.0), reads=[bps, bC], writes=[byc[c][nt]])
                for nt in range(2):
                    tk = slice(nt * 512, (nt + 1) * 512)
                    psM, bpsM = psum()
                    for c in range(4):
                        op("pe", lambda e: e.matmul(psM[:, :], lhsT=onesm[:, :], rhs=yconv[:, c, tk], start=(c == 0), stop=(c == 3)),
                           reads=[bC, byc[c][nt]], writes=[bpsM] if c == 0 else (), part=() if c == 0 else [bpsM], inc=(c == 3))
                    psQ, bpsQ = psum()
                    for c in range(4):
                        x = c % 2
                        op("act", lambda e: e.activation(out=sqb[x][:, :], in_=yconv[:, c, tk], func=AF.Square), reads=[byc[c][nt]], writes=[bsqb[x]])
                        op("pe", lambda e: e.matmul(psQ[:, :], lhsT=onesm[:, :], rhs=sqb[x][:, :], start=(c == 0), stop=(c == 3)),
                           reads=[bC, bsqb[x]], writes=[bpsQ] if c == 0 else (), part=() if c == 0 else [bpsQ])
                    op("act", lambda e: e.copy(out=lnm[:, :], in_=psM[:, :]), reads=[bpsM], writes=[bln])
                    op("act", lambda e: e.activation(out=lnt[:, :], in_=psM[:, :], func=AF.Square), reads=[bpsM], part=[bln])
                    op("dve", lambda e: e.tensor_tensor(out=lnr[:, :], in0=psQ[:, :], in1=lnt[:, :], op=ALU.subtract), reads=[bpsQ, bln], part=[bln])
                    op("act", lambda e: e.activation(out=lnr[:, :], in_=lnr[:, :], func=AF.Sqrt, bias=eps_t[:, :], scale=1.0), reads=[bln, bC], part=[bln])
                    op("dve", lambda e: e.reciprocal(out=lnr[:, :], in_=lnr[:, :]), reads=[bln], part=[bln])
                    for c in range(4):
                        x = c % 2
                        op("dve", lambda e: e.tensor_tensor(out=sgA[x][:, :], in0=yconv[:, c, tk], in1=lnm[:, :], op=ALU.subtract),
                           reads=[byc[c][nt], bln], writes=[bsgA[x]])
                        op("pool", lambda e: e.tensor_tensor(out=sgA[x][:, :], in0=sgA[x][:, :], in1=lnr[:, :], op=ALU.mult), reads=[bln], writes=[bsgA[x]])
                        op("act", lambda e: e.activation(out=sqb[x][:, :], in_=sgA[x][:, :], func=AF.Silu, scale=lng[:, l, c:c + 1], bias=lnb[:, l, c:c + 1]),
                           reads=[bsgA[x], bC], writes=[bsqb[x]])
                        op("pool", lambda e: e.tensor_tensor(out=yaT[:, c, tk], in0=sqb[x][:, :], in1=agT[:, c, tk], op=ALU.mult),
                           reads=[bsqb[x], bag[nt]], part=[bya[nt]])
                if hf == 1:
                    ps, bps = psum()
                    for c in range(4):
                        op("pe", lambda e: e.transpose(out=ps[0:30, c * 128:(c + 1) * 128], in_=u32[:, c, :], identity=ident_f[:, :]),
                           reads=[bu32, bC], writes=[bps] if c == 0 else (), part=() if c == 0 else [bps], inc=(c == 3))
                    op("act", lambda e: e.copy(out=cvo[:, :], in_=ps[0:30, :]), reads=[bps], writes=[bcvo])
                    dma("sp", convp[l], cvo[:, :], reads=[bcvo], own=bcvo, is_output=True)
                if dbg is not None and l == 0 and cfg.get("dbgkey") == "p0_ya":
                    dma("pool", dbg[hf].rearrange("p (c t) -> p c t", c=4), yaT[:, :, :], reads=bya, own=bya[0], is_output=True)
                merge(yaT, [[bya[0]], [bya[1]], [bya[2]]], w_pa, C_GA, False)
                Tk.barrier(); phA.close()
                if stage <= 3:
                    continue

                phC = ExitStack()

                def sc_(name, shape, dt=F32, _ph=phC):
                    return _ph.enter_context(nc.sbuf_tensor(f"{name}_{hf}{l}", list(shape), dt))

                ycT = sc_("ycT", [128, 4, TW], BF16); byc_ = [B(f"ycT{i}") for i in range(3)]
                qcL = [sc_(f"qc{i}", [128, TW], BF16) for i in range(2)]; sigL = [sc_(f"sig{i}", [128, TW]) for i in range(2)]
                cghL = [sc_(f"cgh{i}", [128, TW], BF16) for i in range(2)]; ciTL = [sc_(f"ciT{i}", [128, NTT + 1, 128], BF16) for i in range(2)]
                bqcL = [B(f"qc{i}") for i in range(2)]; bsigL = [B(f"sig{i}") for i in range(2)]; bcgL = [B(f"cgh{i}") for i in range(2)]
                bciL = [[B(f"ci{i}_{t}") for t in range(NTT + 1)] for i in range(2)]
                tA = sc_("tA", [128, TS]); btA = B("tA"); tB = sc_("tB", [128, TS]); btB = B("tB")
                B128 = sc_("B128", [128, TS]); bB = B("B128")
                kkf = sc_("kkf", [128, TS], BF16); bkk = B("kkf")
                qt = sc_("qt", [128, TS], BF16); kt = sc_("kt", [128, TS], BF16); qe = sc_("qe", [128, TS], BF16); ke = sc_("ke", [128, TS], BF16)
                qh = sc_("qh", [128, TS], BF16); kh = sc_("kh", [128, TS], BF16)
                bqt, bkt, bqe, bke, bqh, bkh = [B(n_) for n_ in ("qt", "kt", "qe", "ke", "qh", "kh")]
                oT = tB; boT = [btB, btB]
                ebe = sc_("ebe", [128, NTT]); bebe = B("ebe")
                attS = [sc_(f"attS{i}", [128, 128], BF16) for i in range(2)]; battS = [B(f"attS{i}") for i in range(2)]
                khT = [sc_(f"khT{i}", [128, 128], BF16) for i in range(2)]; bkhT = [B(f"khT{i}") for i in range(2)]
                Sbf = [sc_(f"Sbf{i}", [128, 128], BF16) for i in range(2)]; bSbf = [B(f"Sbf{i}") for i in range(2)]
                osq = kt[:, 0:512]; bosq = bkt; orstd = tA[:, 0:512]; bors = btA
                for i_ in range(2):
                    op("pool", lambda e: e.memset(attS[i_][:, :], 0.0), writes=[battS[i_]])
                v3 = lambda ap, n_: ap.rearrange("p (a b) -> p a b", b=n_)
                triU = triT[:, :].bitcast(mybir.dt.uint16)
                srot = 0

                def proj_jobs(h):
                    S_ = h % 2
                    qc, sig, cgh, ciT = qcL[S_], sigL[S_], cghL[S_], ciTL[S_]
                    bqc, bsig, bcg, bci = bqcL[S_], bsigL[S_], bcgL[S_], bciL[S_]
                    kx = wrot["i"] % NWB
                    wrot["i"] += 1
                    wh = WB[kx][:, :].rearrange("p (k n) -> p k n", n=512)
                    bwh = bWB[kx]
                    jobs = []

                    def jw():
                        dma("pool", wh[:, :, :], w_c[l][:, h, :].rearrange("(k p) n -> p k n", p=128), writes=[bwh], nobar=True)
                    for gi_, func, dst, bdst in ((1, AF.Sigmoid, sig, bsig), (0, AF.Silu, qc, bqc), (3, AF.Silu, cgh, bcg)):
                        for (ni, n0, nn, bhs) in ntl:
                            def jf(ni=ni, n0=n0, nn=nn, bhs=bhs, gi_=gi_, func=func, dst=dst, bdst=bdst):
                                ps, bps = psum()
                                for k in range(8):
                                    op("pe", lambda e: e.matmul(ps[:, 0:nn], lhsT=wh[:, k, gi_ * 128:(gi_ + 1) * 128], rhs=hT[:, k, n0:n0 + nn], start=(k == 0), stop=(k == 7)),
                                       reads=[bwh] + bhs, writes=[bps] if k == 0 else (), part=() if k == 0 else [bps], inc=(k == 7))
                                if ni == 2:
                                    sdst = {0: qcS, 1: sigS, 3: cgS}[gi_]
                                    op("act", lambda e: e.activation(out=sdst[:, h, :], in_=ps[:, 0:nn], func=func), reads=[bps], part=[bsC])
                                else:
                                    op("act", lambda e: e.activation(out=dst[:, n0:n0 + nn], in_=ps[:, 0:nn], func=func), reads=[bps],
                                       **(dict(writes=[bdst]) if ni == 0 else dict(part=[bdst])))
                            jobs.append(jf)
                    for (tt, npt, c0) in tiles:
                        def jc(tt=tt, npt=npt, c0=c0):
                            ps, bps = psum()
                            for k in range(8):
                                op("pe", lambda e: e.matmul(ps[0:npt, 0:128], lhsT=hT[:, k, c0:c0 + npt], rhs=wh[:, k, 256:384], start=(k == 0), stop=(k == 7)),
                                   reads=[bwh, bhT[tt]], writes=[bps] if k == 0 else (), part=() if k == 0 else [bps], inc=(k == 7))
                            if tt == NTT:
                                op("act", lambda e: e.copy(out=ciS[:, h, :], in_=ps[0:npt, 0:128]), reads=[bps], part=[bsC])
                            else:
                                op("act", lambda e: e.copy(out=ciT[0:npt, tt, :], in_=ps[0:npt, 0:128]), reads=[bps], writes=[bci[tt]])
                        jobs.append(jc)

                    def jg():
                        op("pool", lambda e: e.tensor_scalar(out=cgh[:, 0:TS], in0=cgh[:, 0:TS], scalar1=cng[:, l:l + 1], scalar2=None, op0=ALU.mult), reads=[blb], writes=[bcg])
                        if samp:
                            op("pool", lambda e: e.tensor_scalar(out=cgS[:, h, :], in0=cgS[:, h, :], scalar1=cng[:, l:l + 1], scalar2=None, op0=ALU.mult), reads=[blb], part=[bsC])
                    jobs.append(jg)
                    return jw, jobs

                PJ = [proj_jobs(h_) for h_ in range(4)]
                PJ[0][0](); PJ[1][0]()
                pending = PJ[0][1]
                for h in range(4):
                    if hf == 0:
                        op("pool", lambda e: e.memset(Sst[:, l, h, :], 0.0), writes=[bS[l][h]])
                    for jb in pending:
                        jb()
                    if h + 2 < 4:
                        PJ[h + 2][0]()
                    pending = list(PJ[h + 1][1]) if h < 3 else []
                    S_ = h % 2
                    qc, sig, cgh, ciT = qcL[S_], sigL[S_], cghL[S_], ciTL[S_]
                    bqc, bsig, bcg, bci = bqcL[S_], bsigL[S_], bcgL[S_], bciL[S_]
                    npre = (len(pending) * 2) // 5
                    for jb in pending[:npre]:
                        jb()
                    pending = pending[npre:]
                    P_ = slice(0, TS)
                    op("dve", lambda e: e.tensor_scalar(out=tA[:, :], in0=sig[:, P_], scalar1=oml[:, l, h:h + 1], scalar2=lbm[:, l, h:h + 1], op0=ALU.mult, op1=ALU.add),
                       reads=[bsig, blb], writes=[btA])
                    op("act", lambda e: e.activation(out=tA[:, :], in_=tA[:, :], func=AF.Ln), reads=[btA], writes=[btA])
                    op("dve", lambda e: e.tensor_scalar(out=kkf[:, :], in0=sig[:, P_], scalar1=noml[:, l, h:h + 1], scalar2=oml[:, l, h:h + 1], op0=ALU.mult, op1=ALU.add),
                       reads=[bsig, blb], writes=[bkk])
                    for tt in range(NTT):
                        tk = slice(tt * 128, (tt + 1) * 128)
                        op("dve", lambda e: e.tensor_tensor_scan(out=B128[:, tk], data0=ones_f[:, :], data1=tA[:, tk], initial=0.0, op0=ALU.mult, op1=ALU.add),
                           reads=[btA, bC], **(dict(writes=[bB]) if tt == 0 else dict(part=[bB])))
                    B3 = v3(B128[:, :], 128)
                    op("act", lambda e: e.activation(out=tB[:, :], in_=B128[:, :], func=AF.Exp), reads=[bB], writes=[btB])
                    op("pool", lambda e: e.tensor_tensor(out=qh[:, :], in0=qc[:, P_], in1=tB[:, :], op=ALU.mult), reads=[bqc, btB], writes=[bqh])
                    op("dve", lambda e: e.tensor_copy(out=ebe[:, :], in_=v3(tB[:, :], 128)[:, :, 127]), reads=[btB], writes=[bebe])
                    op("dve", lambda e: e.tensor_tensor(out=v3(tA[:, :], 128), in0=B3[:, :, 127:128].to_broadcast([128, NTT, 128]), in1=B3, op=ALU.subtract),
                       reads=[bB], writes=[btA])
                    op("act", lambda e: e.activation(out=tA[:, :], in_=tA[:, :], func=AF.Exp), reads=[btA], writes=[btA])
                    op("pool", lambda e: e.tensor_tensor(out=kh[:, :], in0=kkf[:, :], in1=tA[:, :], op=ALU.mult), reads=[bkk, btA], writes=[bkh])
                    B64 = v3(B128[:, :], 64)
                    op("dve", lambda e: e.tensor_tensor(out=v3(tB[:, :], 64), in0=B64, in1=B64[:, :, 31:32].to_broadcast([128, 2 * NTT, 64]), op=ALU.subtract),
                       reads=[bB], writes=[btB])
                    op("act", lambda e: e.activation(out=tA[:, :], in_=tB[:, :], func=AF.Exp), reads=[btB], writes=[btA])
                    op("pool", lambda e: e.tensor_tensor(out=qt[:, :], in0=qc[:, P_], in1=tA[:, :], op=ALU.mult), reads=[bqc, btA], writes=[bqt])
                    op("act", lambda e: e.activation(out=tA[:, :], in_=tB[:, :], func=AF.Exp, scale=-1.0), reads=[btB], writes=[btA])
                    op("pool", lambda e: e.tensor_tensor(out=kt[:, :], in0=kkf[:, :], in1=tA[:, :], op=ALU.mult), reads=[bkk, btA], writes=[bkt])
                    op("dve", lambda e: e.tensor_tensor(out=v3(tB[:, :], 128), in0=B3, in1=B3[:, :, 63:64].to_broadcast([128, NTT, 128]), op=ALU.subtract),
                       reads=[bB], writes=[btB])
                    op("act", lambda e: e.activation(out=tA[:, :], in_=tB[:, :], func=AF.Exp), reads=[btB], writes=[btA])
                    op("pool", lambda e: e.tensor_tensor(out=qe[:, :], in0=qc[:, P_], in1=tA[:, :], op=ALU.mult), reads=[bqc, btA], writes=[bqe])
                    op("act", lambda e: e.activation(out=tA[:, :], in_=tB[:, :], func=AF.Exp, scale=-1.0), reads=[btB], writes=[btA])
                    op("pool", lambda e: e.tensor_tensor(out=ke[:, :], in0=kkf[:, :], in1=tA[:, :], op=ALU.mult), reads=[bkk, btA], writes=[bke])
                    if dbg is not None and cfg.get("dbgkey") == "B128" and h == 0 and l == 0:
                        dma("sp", dbg[hf][:, 0:TS], B128[:, :], reads=[bB], own=bB, is_output=True)
                        dma("sp", dbg[hf][:, TS:2 * TS], sig[:, 0:TS], reads=[bsig], own=bsig, is_output=True)
                    for tt in range(NTT):
                        a0, a1, b1 = tt * 128, tt * 128 + 64, tt * 128 + 128
                        x = srot % 2
                        srot += 1
                        ps, bps = psum()
                        op("pe", lambda e: e.matmul(ps[0:64, 0:64], lhsT=kt[:, a0:a1], rhs=qt[:, a0:a1], start=True, stop=True), reads=[bkt, bqt], writes=[bps], inc=False)
                        op("pe", lambda e: e.matmul(ps[0:64, 64:128], lhsT=ke[:, a0:a1], rhs=qe[:, a1:b1], start=True, stop=True), reads=[bke, bqe], part=[bps], inc=False)
                        op("pe", lambda e: e.matmul(ps[64:128, 64:128], lhsT=kt[:, a1:b1], rhs=qt[:, a1:b1], start=True, stop=True), reads=[bkt, bqt], part=[bps])
                        op("dve", lambda e: e.copy_predicated(out=attS[x][0:64, :], mask=triU[0:64, :], data=ps[0:64, 0:128]), reads=[bps, bC], part=[battS[x]])
                        op("dve", lambda e: e.copy_predicated(out=attS[x][64:128, 64:128], mask=triU[64:128, 64:128], data=ps[64:128, 64:128]),
                           reads=[bps, bC], part=[battS[x]])
                        if tt == 0 and hf == 0:
                            op("act", lambda e: e.copy(out=Sbf[x][:, :], in_=Sst[:, l, h, :]), reads=[bS[l][h]], writes=[bSbf[x]])
                        elif tt == 0:
                            op("act", lambda e: e.copy(out=Sbf[x][:, :], in_=Sst[:, l, h, :]), reads=[bS[l][h]], writes=[bSbf[x]])
                        ps2, bps2 = psum()
                        op("pe", lambda e: e.matmul(ps2[:, 0:128], lhsT=ciT[:, tt, :], rhs=attS[x][:, :], start=True, stop=False), reads=[bci[tt], battS[x]], writes=[bps2], inc=False)
                        op("pe", lambda e: e.matmul(ps2[:, 0:128], lhsT=Sbf[x][:, :], rhs=qh[:, a0:b1], start=False, stop=True), reads=[bSbf[x], bqh], part=[bps2])
                        op("act", lambda e: e.copy(out=oT[:, a0:b1], in_=ps2[:, 0:128]), reads=[bps2], part=[boT[tt // 4]])
                        psT, bpsT = psum()
                        pbT = psT[:].bitcast(BF16)
                        op("pe", lambda e: e.transpose(out=pbT[:, 0:128], in_=kh[:, a0:b1], identity=ident_b[:, :]), reads=[bkh, bC], writes=[bpsT])
                        op("act", lambda e: e.copy(out=khT[x][:, :], in_=pbT[:, 0:128]), reads=[bpsT], writes=[bkhT[x]])
                        ps3, bps3 = psum()
                        op("pe", lambda e: e.matmul(ps3[:, 0:128], lhsT=khT[x][:, :], rhs=ciT[:, tt, :], start=True, stop=True), reads=[bkhT[x], bci[tt]], writes=[bps3])
                        op("dve", lambda e: e.scalar_tensor_tensor(out=Sst[:, l, h, :], in0=Sst[:, l, h, :], scalar=ebe[:, tt:tt + 1], in1=ps3[:, 0:128],
                                                                   op0=ALU.mult, op1=ALU.add), reads=[bps3, bebe], writes=[bS[l][h]])
                        if tt < NTT - 1:
                            y_ = srot % 2
                            op("act", lambda e: e.copy(out=Sbf[y_][:, :], in_=Sst[:, l, h, :]), reads=[bS[l][h]], writes=[bSbf[y_]])
                        for _ in range(2):
                            if pending:
                                pending.pop(0)()
                    for nt in range(2):
                        tk = slice(nt * 512, (nt + 1) * 512)
                        op("act", lambda e: e.activation(out=osq[:, :], in_=oT[:, tk], func=AF.Square), reads=[boT[nt]], writes=[bosq])
                        ps, bps = psum()
                        op("pe", lambda e: e.matmul(ps[:, :], lhsT=ones_b[:, :], rhs=osq[:, :], start=True, stop=True), reads=[bosq, bC], writes=[bps])
                        op("act", lambda e: e.activation(out=orstd[:, :], in_=ps[:, :], func=AF.Sqrt, bias=eps_t[:, :], scale=1.0 / 128), reads=[bps, bC], writes=[bors])
                        op("dve", lambda e: e.reciprocal(out=orstd[:, :], in_=orstd[:, :]), reads=[bors], writes=[bors])
                        op("dve", lambda e: e.tensor_tensor(out=orstd[:, :], in0=oT[:, tk], in1=orstd[:, :], op=ALU.mult), reads=[boT[nt]], writes=[bors])
                        op("pool", lambda e: e.tensor_tensor(out=ycT[:, h, tk], in0=orstd[:, :], in1=cgh[:, tk], op=ALU.mult), reads=[bors, bcg], part=[byc_[nt]])
                if hf == 1:
                    dma("sp", hgp[l].rearrange("h d v -> d h v"), Sst[:, l, :, :], reads=bS[l], own=bS[l][0], is_output=True)
                if dbg is not None and l == 0 and cfg.get("dbgkey") == "p0_yc":
                    dma("pool", dbg[hf].rearrange("p (c t) -> p c t", c=4), ycT[:, :, :], reads=byc_, own=byc_[0], is_output=True)
                merge(ycT, [[byc_[0]], [byc_[1]], [byc_[2]]], w_pc, C_GC, False)
                wprefetch(("wo", 0, hf, l), w_out[l][:, 0:512], 8, 512)
                wprefetch(("wo", 1, hf, l), w_out[l][:, 512:1024], 8, 512)
                Tk.barrier(); phC.close()

                phF = ExitStack()

                def sf_(name, shape, dt=F32, _ph=phF):
                    return _ph.enter_context(nc.sbuf_tensor(f"{name}_{hf}{l}", list(shape), dt))

                gbc = sf_("gbc", [128, D]); bgbc = B("gbc")
                dgf = [sf_(f"dgf{i}", [128, 128]) for i in range(2)]; bdgf = [B(f"dgf{i}") for i in range(2)]
                ftmp = [sf_(f"ftmp{i}", [128, 512]) for i in range(2)]; bft = [B(f"ftmp{i}") for i in range(2)]
                for c in range(8):
                    x = c % 2
                    op("pool", lambda e: e.tensor_scalar(out=dgf[x][:, :], in0=ident_f[:, :], scalar1=modT[:, l, 16 + c, 0:1], scalar2=None, op0=ALU.mult),
                       reads=[bC, bmod], writes=[bdgf[x]])
                    ps, bps = psum()
                    op("pe", lambda e: e.matmul(ps[:, 0:128], lhsT=ones_f[:, :], rhs=dgf[x][:, :], start=True, stop=True), reads=[bC, bdgf[x]], writes=[bps])
                    op("act", lambda e: e.copy(out=gbc[:, c * 128:(c + 1) * 128], in_=ps[:, 0:128]), reads=[bps], part=[bgbc])
                frot = 0
                for hc in range(2):
                    wo, bwo = wload(w_out[l][:, hc * 512:(hc + 1) * 512], 8, 512, key=("wo", hc, hf, l))
                    for tt in range(NTT):
                        ps, bps = psum()
                        for k in range(8):
                            op("pe", lambda e: e.matmul(ps[:, :], lhsT=mT[:, k, tt * 128:(tt + 1) * 128], rhs=wo[:, k, :], start=(k == 0), stop=(k == 7)),
                               reads=[bwo, bmT_[tt // 4]], writes=[bps] if k == 0 else (), part=() if k == 0 else [bps], inc=(k == 7))
                        x = frot % 2
                        frot += 1
                        op("dve", lambda e: e.tensor_tensor(out=ftmp[x][:, :], in0=ps[:, :], in1=gbc[:, hc * 512:(hc + 1) * 512], op=ALU.mult),
                           reads=[bps, bgbc], writes=[bft[x]])
                        op("pool", lambda e: e.tensor_tensor(out=X[:, tt, hc * 512:(hc + 1) * 512], in0=X[:, tt, hc * 512:(hc + 1) * 512], in1=ftmp[x][:, :], op=ALU.add),
                           reads=[bft[x]], writes=[bX[tt]])
                if l == NL - 1:
                    for tt in range(NTT):
                        r0 = hf * TS + tt * 128
                        dma("sp", yp[r0:r0 + 128, :], X[:, tt, :], reads=[bX[tt]], own=bX[tt], is_output=True)
                if not samp:
                    nxt = {(0, 0): (0, 1), (0, 1): (1, 0), (1, 0): (1, 1)}.get((hf, l)) if NL == 2 else None
                    if nxt is not None and nxt[0] in cfg.get('halves', [0, 1]):
                        wprefetch(("b2",) + nxt, w_in[nxt[1]][:, C_ZK:C_ZK + 512], 8, 512)
                        wprefetch(("b3",) + nxt, w_in[nxt[1]][:, C_ZKI:C_ZKI + 68], 8, 68)
                Tk.barrier(); phF.close()
                if samp:
                    sample_dsa(l)
                    if dbg is not None and cfg.get("dbgkey") == "s_yb" and l == 0:
                        dma("pool", dbg[0][:, 0:64].rearrange("p (c t) -> p c t", c=4), yS[:, 1, :, :], reads=[byS[1]], own=byS[1], is_output=True)
                    if cfg.get("sstage", 9) <= 1:
                        continue
                    sample_conv(l)
                    sample_hgrn(l)
                    stl = [ntl[2]]
                    merge(None, [None, None, [byS[1]]], w_pb, C_GB, True, sub=stl, ysrc=lambda k: yS[:, 1, k, :])
                    merge(None, [None, None, [byS[0]]], w_pa, C_GA, False, sub=stl, ysrc=lambda k: yS[:, 0, k, :])
                    merge(None, [None, None, [byS[2]]], w_pc, C_GC, False, sub=stl, ysrc=lambda k: yS[:, 2, k, :])
                    phG = ExitStack()
                    ag_ = lambda n_, shp, dt=F32: phG.enter_context(nc.sbuf_tensor(f"{n_}_g{l}", list(shp), dt))
                    sel16 = ag_("sel16", [128, 4, NS]); gbs = ag_("gbs", [NS, D]); bgbs = B("gbs"); bsel = B("sel16")
                    dgs = [ag_(f"dgs{i}", [128, 128]) for i in range(2)]; bdgs = [B(f"dgs{i}") for i in range(2)]
                    fts = ag_("fts", [NS, 512]); bfts = B("fts")
                    dma("sp", sel16[:], cst["sel16"], writes=[bsel])
                    grot = 0
                    for c in range(8):
                        ps, bps = psum()
                        for b in range(4):
                            x = grot % 2
                            grot += 1
                            op("pool", lambda e: e.tensor_scalar(out=dgs[x][:, :], in0=ident_f[:, :], scalar1=modT[:, l, 16 + c, 1 + b:2 + b], scalar2=None, op0=ALU.mult),
                               reads=[bC, bmod], writes=[bdgs[x]])
                            op("pe", lambda e: e.matmul(ps[0:NS, 0:128], lhsT=sel16[:, b, :], rhs=dgs[x][:, :], start=(b == 0), stop=(b == 3)),
                               reads=[bsel, bdgs[x]], writes=[bps] if b == 0 else (), part=() if b == 0 else [bps], inc=True)
                        op("act", lambda e: e.copy(out=gbs[:, c * 128:(c + 1) * 128], in_=ps[0:NS, 0:128]), reads=[bps], part=[bgbs])
                    for hc in range(2):
                        wo, bwo = wload(w_out[l][:, hc * 512:(hc + 1) * 512], 8, 512)
                        ps, bps = psum()
                        for k in range(8):
                            op("pe", lambda e: e.matmul(ps[0:NS, :], lhsT=mT[:, k, TS:TS + NS], rhs=wo[:, k, :], start=(k == 0), stop=(k == 7)),
                               reads=[bwo, bmT_[2]], writes=[bps] if k == 0 else (), part=() if k == 0 else [bps], inc=(k == 7))
                        op("dve", lambda e: e.tensor_tensor(out=fts[:, :], in0=ps[0:NS, :], in1=gbs[:, hc * 512:(hc + 1) * 512], op=ALU.mult), reads=[bps, bgbs], writes=[bfts])
                        op("dve", lambda e: e.tensor_tensor(out=XS[:, hc * 512:(hc + 1) * 512], in0=XS[:, hc * 512:(hc + 1) * 512], in1=fts[:, :], op=ALU.add),
                           reads=[bfts], writes=[bXS])
                    if l == NL - 1:
                        dma("sp", ys, XS[:, :], reads=[bXS], own=bXS, is_output=True)
                    Tk.barrier(); phG.close()
        Tk.finish()
        print("instructions:", Tk.ninstr, "semaphores:", Tk.nsem)
    return nc


def make_in_maps(inp, cores, cfg):
    consts = _consts()
    f = lambda a: np.ascontiguousarray(np.asarray(a), dtype=np.float32)
    w_ada = f(inp["w_ada"]); b_ada = f(inp["b_ada"])
    shared = {
        "w_ada": w_ada,
        "b_adaT": np.ascontiguousarray(b_ada.reshape(DEPTH, 24, 128).transpose(0, 2, 1)),
        "b_ada_g": np.ascontiguousarray(b_ada[:, None, 2 * D:]),
        "norm_gT": np.ascontiguousarray(f(inp["norm_g"]).reshape(DEPTH, 8, 128).transpose(0, 2, 1)),
        "w_in": f(inp["w_in"]),
        "qg_bc": f(inp["q_norm_g"])[:, None, :],
        "kg_bc": f(inp["k_norm_g"])[:, None, :],
        "w_dwT": np.ascontiguousarray(f(inp["w_dw"]).reshape(DEPTH, 31, 4, 128).transpose(0, 3, 2, 1)),
        "b_dwT": np.ascontiguousarray(f(inp["b_dw"]).reshape(DEPTH, 4, 128).transpose(0, 2, 1)),
        "ln_gT": np.ascontiguousarray(f(inp["ln_g"]).reshape(DEPTH, 4, 128).transpose(0, 2, 1)),
        "ln_bT": np.ascontiguousarray(f(inp["ln_b"]).reshape(DEPTH, 4, 128).transpose(0, 2, 1)),
        "lbT": np.ascontiguousarray(f(inp["lb_logits"]).reshape(DEPTH, 4, 128).transpose(0, 2, 1)),
        "cng": np.ascontiguousarray(f(inp["c_norm_g"])[:, :, None]),
        "w_c": np.ascontiguousarray(np.stack([f(inp["w_in"])[:, :, c0:c0 + 512].reshape(DEPTH, D, 4, 128) for c0 in (C_CQ, C_CF, C_CI, C_CG)], axis=3)
                                    .reshape(DEPTH, D, 4, 512)),
        "w_pa": f(inp["w_proj_a"]), "w_pb": f(inp["w_proj_b"]), "w_pc": f(inp["w_proj_c"]), "w_out": f(inp["w_out"]),
    }
    for k, v in consts.items():
        shared["c_" + k] = v
    maps = []
    xp = np.asarray(inp["x_prompt"]); xs = np.asarray(inp["x_sample"])
    cp = np.asarray(inp["c_prompt"]); cs = np.asarray(inp["c_sample"])
    do_sample = cfg.get("sample", True)
    if do_sample:
        for i in range(DEPTH):
            shared[f"ck{i}"] = f(inp["cache_k"][i]).reshape(NPOOL * 128, 128)
            shared[f"cv{i}"] = f(inp["cache_v"][i]).reshape(NPOOL * 128, 128)
            shared[f"cik{i}"] = f(inp["cache_idx_k"][i]).reshape(NPOOL * 128, 64)
        pt = np.asarray(inp["page_table"]).astype(np.int32)
        sconv = f(inp["state_conv"]); shg = f(inp["state_hgrn"])
    for c in cores:
        m = dict(shared)
        if do_sample:
            ptc = pt[4 * c:4 * c + 4].reshape(4, 8, 8)
            m["ptx"] = np.ascontiguousarray(np.repeat(ptc.transpose(2, 0, 1), 16, axis=0)).astype(np.int32)
            m["sconv"] = np.ascontiguousarray(sconv[:, 4 * c:4 * c + 4])
            m["shg"] = np.ascontiguousarray(shg[:, 4 * c:4 * c + 4])
        m["xp"] = f(xp[c])
        m["xs"] = f(xs[4 * c:4 * c + 4].reshape(NS, D))
        m["cc"] = f(np.concatenate([cp[c:c + 1], cs[4 * c:4 * c + 4]], axis=0))
        maps.append(m)
    return maps


def kernel(**inp):
    cfg = {}
    nc = build(cfg)
    cores = list(range(8))
    res = run_bass_kernel_spmd(nc, make_in_maps(inp, cores, cfg), core_ids=cores)
    R = res.results
    g = lambda k: [np.asarray(r[k], dtype=np.float32) for r in R]
    z = lambda k, shp: [np.asarray(r[k], dtype=np.float32) if k in r else np.zeros(shp, np.float32) for r in R]
    y_prompt = np.stack(g("yp"), 0)
    y_sample = np.concatenate([a.reshape(4, 4, D) for a in z("ys", (NS, D))], 0)
    k_prompt = np.stack(g("kp"), 1).reshape(DEPTH, 8, T, 2, 64)
    v_prompt = np.stack(g("vp"), 1).reshape(DEPTH, 8, T, 2, 64)
    idxk_prompt = np.stack(g("ikp"), 1)
    conv_prompt = np.stack(g("convp"), 1)
    hgrn_prompt = np.stack(g("hgp"), 1)
    k_sample = np.concatenate([a.reshape(DEPTH, 4, 4, 2, 64) for a in g("ks")], 1)
    v_sample = np.concatenate([a.reshape(DEPTH, 4, 4, 2, 64) for a in g("vs")], 1)
    idxk_sample = np.concatenate([a.reshape(DEPTH, 4, 4, 64) for a in g("iks")], 1)
    conv_sample = np.concatenate(z("convs", (DEPTH, 4, 30, 512)), 1)
    hgrn_sample = np.concatenate(z("hgs", (DEPTH, 4, 4, 128, 128)), 1)
    return (y_prompt, y_sample, k_prompt, v_prompt, idxk_prompt, conv_prompt, hgrn_prompt,
            k_sample, v_sample, idxk_sample, conv_sample, hgrn_sample)
```

```python
import math
from contextlib import ExitStack

import numpy as np
import ml_dtypes

import concourse.bass as bass
import concourse.mybir as mybir
from concourse.bass_utils import run_bass_kernel_spmd

F32 = mybir.dt.float32
BF16 = mybir.dt.bfloat16
I32 = mybir.dt.int32
AF = mybir.ActivationFunctionType
ALU = mybir.AluOpType
AX = mybir.AxisListType

D = 1024
T = 2048
TS = 1024
NTT = 8
NS = 16
TW = TS + NS
DEPTH = 2
N_IN = 8260
EPS = 1e-6
PAST = 8192
NPG = 64
NPOOL = 2560
C_A_VAL, C_A_GLU, C_A_GATE = 0, 512, 1024
C_ZQ, C_ZK, C_ZV, C_ZQI, C_ZKI, C_ZWI, C_BG = 1536, 2048, 2176, 2304, 2560, 2624, 2628
C_CQ, C_CF, C_CI, C_CG = 3140, 3652, 4164, 4676
C_GA, C_GB, C_GC = 5188, 6212, 7236
NEGM = -1.0e30


class Buf:
    __slots__ = ("name", "lw", "rd", "dsem", "dcnt")

    def __init__(self, name):
        self.name = name
        self.lw = {}
        self.rd = {}
        self.dsem = None
        self.dcnt = 0


class Eng:
    def __init__(self, name, h, sem):
        self.name, self.h, self.sem = name, h, sem
        self.count = 0
        self.waited = {}


def _put(d, tok):
    k = id(tok[0])
    if k not in d or d[k][1] < tok[1]:
        d[k] = tok


class Trk:
    def __init__(self, nc, stack):
        self.nc, self.stack = nc, stack
        self.engs = {}
        for name, h in (("pe", nc.tensor), ("act", nc.scalar), ("dve", nc.vector),
                        ("pool", nc.gpsimd), ("sp", nc.sync)):
            self.engs[name] = Eng(name, h, stack.enter_context(nc.semaphore("s_" + name)))
        self.out_tokens = {}
        self.pend_dma = {}
        self.ninstr = 0
        self.nsem = 5

    def _need(self, e, toks):
        need = {}
        for s, v in toks:
            k = id(s)
            if e.waited.get(k, 0) >= v:
                continue
            if k not in need or need[k][1] < v:
                need[k] = (s, v)
        for k, (s, v) in need.items():
            e.h.wait_ge(s, v)
            e.waited[k] = v
            self.ninstr += 1

    def _collect(self, e, reads, writes, part):
        toks = []
        for b in reads:
            toks.extend(b.lw.values())
        for b in writes:
            toks.extend(b.rd.values())
            toks.extend(b.lw.values())
        for b in part:
            toks.extend(b.rd.values())
            for k, t in b.lw.items():
                if k != id(e.sem):
                    toks.append(t)
        return toks

    def _update(self, tok, reads, writes, part):
        for b in writes:
            b.lw = {id(tok[0]): tok}
            b.rd = {}
        for b in part:
            _put(b.lw, tok)
        for b in reads:
            _put(b.rd, tok)

    def op(self, eng, fn, reads=(), writes=(), part=(), inc=True):
        e = self.engs[eng]
        self._need(e, self._collect(e, reads, writes, part))
        ins = fn(e.h)
        self.ninstr += 1
        if inc:
            e.count += 1
            ins.then_inc(e.sem, 1)
            tok = (e.sem, e.count)
        else:
            tok = (e.sem, e.count + 1)
        self._update(tok, reads, writes, part)
        return ins

    def dma(self, eng, out, in_, reads=(), writes=(), part=(), own=None, is_output=False, fn=None, nobar=False, **kw):
        e = self.engs[eng]
        if own is None:
            own = (list(writes) + list(part) + list(reads))[0]
        if own.dsem is None:
            own.dsem = self.stack.enter_context(self.nc.semaphore(f"d{self.nsem}_" + own.name))
            self.nsem += 1
        self._need(e, self._collect(e, reads, writes, part))
        if fn is not None:
            ins = fn(e.h)
        else:
            ins = e.h.dma_start(out=out, in_=in_, **kw)
        own.dcnt += 1
        ins.then_inc(own.dsem, 16)
        self.ninstr += 1
        tok = (own.dsem, 16 * own.dcnt)
        self._update(tok, reads, writes, part)
        if not nobar:
            _put(self.pend_dma, tok)
        if is_output:
            _put(self.out_tokens, tok)
        return ins

    def barrier(self):
        sp = self.engs["sp"]
        toks = [(e.sem, e.count) for e in self.engs.values() if e is not sp and e.count > 0]
        toks += list(self.pend_dma.values())
        self._need(sp, toks)
        self.pend_dma = {}
        ins = sp.h.nop()
        sp.count += 1
        ins.then_inc(sp.sem, 1)
        for e in self.engs.values():
            if e is not sp:
                self._need(e, [(sp.sem, sp.count)])

    def finish(self):
        self._need(self.engs["sp"], list(self.out_tokens.values()))


def _consts():
    c = {}
    c["ident_f"] = np.eye(128, dtype=np.float32)
    inv = (10000.0 ** (-np.arange(32, dtype=np.float32) / 32)).astype(np.float32)
    pos = np.concatenate([np.arange(T), PAST + np.arange(4)]).astype(np.float32)
    ang = pos[:, None] * inv[None, :]
    cos = np.cos(ang).astype(np.float32)
    sin = np.sin(ang).astype(np.float32)
    c["cos_p"] = np.ascontiguousarray(cos[:T].reshape(16, 128, 32).transpose(1, 0, 2))
    c["sin_p"] = np.ascontiguousarray(sin[:T].reshape(16, 128, 32).transpose(1, 0, 2))
    c["cos_s"] = np.ascontiguousarray(np.tile(cos[T:], (4, 1)))
    c["sin_s"] = np.ascontiguousarray(np.tile(sin[T:], (4, 1)))
    tt = np.arange(128)
    c["caus_neg"] = np.where(tt[None, :] <= tt[:, None], 0.0, NEGM).astype(np.float32)
    c["triT"] = (tt[:, None] <= tt[None, :]).astype(np.float32)
    selp = np.zeros((5, 128), np.float32); selp[0, :] = 1
    sels = np.zeros((5, 128), np.float32)
    for m in range(16):
        sels[1 + m // 4, m] = 1
    c["selp"] = selp
    c["pw2"] = np.tile((2.0 ** -np.arange(32, dtype=np.float32))[None, :], (128, 1)).astype(np.float32)
    c["sels"] = sels
    p = np.arange(128)
    c["roff"] = (p % 16).astype(np.float32)[:, None]
    nm = np.full((128, 4, 16), NEGM, np.float32)
    for b in range(4):
        for t in range(4):
            for s_ in range(t + 1):
                nm[t, b, 4 * b + s_] = 0.0
    c["newmask"] = nm
    c["grp4"] = (p[:, None] % 4 == p[None, :] % 4).astype(np.float32)
    rep = np.zeros((16, 4, 128), np.float32)
    for b in range(4):
        for m in range(128):
            rep[4 * b + m % 4, b, m] = 1.0
    c["rep4"] = rep
    sel16 = np.zeros((128, 4, 16), np.float32)
    for b in range(4):
        sel16[:, b, 4 * b:4 * b + 4] = 1.0
    c["sel16"] = sel16
    bdm = np.zeros((16, 16), np.float32)
    for b in range(4):
        for t in range(4):
            for s_ in range(t + 1):
                bdm[4 * b + s_, 4 * b + t] = 1.0
    c["bdm"] = bdm
    rms = np.ones((128, 16), np.float32); rms[:, 0::4] = 0.0
    c["rmS"] = rms
    rmk = np.zeros((16, 4), np.float32)
    for b in range(4):
        rmk[4 * b:4 * b + 4, b] = 1.0
    c["rowmask"] = rmk
    return c


def build(cfg):
    nc = bass.Bass("TRN2", target_bir_lowering=False)
    NL = cfg.get("layers", DEPTH)
    do_sample = cfg.get("sample", True)
    stage = cfg.get("stage", 99)

    def din(name, shape, dt=F32):
        return nc.dram_tensor(name, list(shape), dt, kind="ExternalInput").ap()

    def dout(name, shape, dt=F32):
        return nc.dram_tensor(name, list(shape), dt, kind="ExternalOutput").ap()

    xp = din("xp", [T, D]); xs = din("xs", [NS, D]); cc = din("cc", [5, D])
    w_ada = din("w_ada", [DEPTH, D, 3 * D]); b_adaT = din("b_adaT", [DEPTH, 128, 24]); b_ada_g = din("b_ada_g", [DEPTH, 1, D])
    norm_gT = din("norm_gT", [DEPTH, 128, 8])
    w_in = din("w_in", [DEPTH, D, N_IN])
    qg_bc = din("qg_bc", [DEPTH, 1, 64]); kg_bc = din("kg_bc", [DEPTH, 1, 64])
    w_dwT = din("w_dwT", [DEPTH, 128, 4, 31]); b_dwT = din("b_dwT", [DEPTH, 128, 4])
    ln_gT = din("ln_gT", [DEPTH, 128, 4]); ln_bT = din("ln_bT", [DEPTH, 128, 4])
    w_pa = din("w_pa", [DEPTH, 512, D]); w_pb = din("w_pb", [DEPTH, 512, D]); w_pc = din("w_pc", [DEPTH, 512, D])
    w_out = din("w_out", [DEPTH, D, D])
    lbT_in = din("lbT", [DEPTH, 128, 4]); cng_in = din("cng", [DEPTH, 128, 1])
    w_c = din("w_c", [DEPTH, D, 4, 512])
    if do_sample:
        ptx_in = din("ptx", [128, 4, 8], I32)
        ck_in = [din(f"ck{i}", [NPOOL * 128, 128]) for i in range(DEPTH)]; cv_in = [din(f"cv{i}", [NPOOL * 128, 128]) for i in range(DEPTH)]
        cik_in = [din(f"cik{i}", [NPOOL * 128, 64]) for i in range(DEPTH)]
        sconv_in = din("sconv", [DEPTH, 4, 30, 512]); shg_in = din("shg", [DEPTH, 4, 4, 128, 128])
    cst = {k: din("c_" + k, v.shape) for k, v in _consts().items()}
    kp = dout("kp", [DEPTH, T, 128]); vp = dout("vp", [DEPTH, T, 128]); ikp = dout("ikp", [DEPTH, T, 64])
    ks = dout("ks", [DEPTH, NS, 128]); vs = dout("vs", [DEPTH, NS, 128]); iks = dout("iks", [DEPTH, NS, 64])
    yp = dout("yp", [T, D]); ys = dout("ys", [NS, D])
    convp = dout("convp", [DEPTH, 30, 512])
    hgp = dout("hgp", [DEPTH, 4, 128, 128])
    convs = dout("convs", [DEPTH, 4, 30, 512]); hgs = dout("hgs", [DEPTH, 4, 4, 128, 128])
    dbg = dout("dbg", [2, 128, 4 * TW]) if cfg.get("dbg") else None

    with ExitStack() as st:
        Tk = Trk(nc, st)
        op, dma = Tk.op, Tk.dma

        def sb(name, shape, dt=F32):
            return st.enter_context(nc.sbuf_tensor(name, list(shape), dt))

        def B(name):
            return Buf(name)

        PS = [st.enter_context(nc.psum_tensor(f"ps{i}", [128, 512], F32)) for i in range(8)]
        bPS = [B(f"ps{i}") for i in range(8)]
        rot = {"i": 0, "hi": 8}

        def psum(lo=0, hi=None):
            if hi is None:
                hi = rot["hi"]
            k = lo + rot["i"] % (hi - lo)
            rot["i"] += 1
            return PS[k], bPS[k]

        ident_f = sb("ident_f", [128, 128]); ident_b = sb("ident_b", [128, 128], BF16)
        ones_b = sb("ones_b", [128, 128], BF16)
        eps_t = sb("eps_t", [128, 1]); bC = B("consts")
        cos_p = sb("cos_p", [128, 16, 32]); sin_p = sb("sin_p", [128, 16, 32])
        cos_s = sb("cos_s", [NS, 32]); sin_s = sb("sin_s", [NS, 32])
        caus_neg = sb("caus_neg", [128, 128]); triT = sb("triT", [128, 128], BF16)
        selp = sb("selp", [5, 128]); sels = sb("sels", [5, 128])
        dma("sp", ident_f[:], cst["ident_f"], part=[bC])
        dma("pool", ident_b[:], cst["ident_f"], part=[bC])
        dma("pool", triT[:], cst["triT"], part=[bC])
        dma("sp", cos_p[:], cst["cos_p"], part=[bC]); dma("sp", sin_p[:], cst["sin_p"], part=[bC])
        dma("sp", cos_s[:], cst["cos_s"], part=[bC]); dma("sp", sin_s[:], cst["sin_s"], part=[bC])
        dma("sp", caus_neg[:], cst["caus_neg"], part=[bC])
        dma("sp", selp[:], cst["selp"], part=[bC]); dma("sp", sels[:], cst["sels"], part=[bC])
        op("pool", lambda e: e.memset(eps_t[:], EPS), part=[bC])
        op("pool", lambda e: e.memset(ones_b[:], 1.0), part=[bC])
        qg = sb("qg", [128, DEPTH, 64]); kg = sb("kg", [128, DEPTH, 64])
        for l in range(DEPTH):
            dma("sp", qg[:, l, :], qg_bc[l].to_broadcast([128, 64]), part=[bC])
            dma("sp", kg[:, l, :], kg_bc[l].to_broadcast([128, 64]), part=[bC])
        ngT = sb("ngT", [128, DEPTH, 8]); badT = sb("badT", [128, DEPTH, 24])
        for l in range(DEPTH):
            dma("sp", ngT[:, l, :], norm_gT[l], part=[bC]); dma("sp", badT[:, l, :], b_adaT[l], part=[bC])

        NWB = 3
        WB = [sb(f"wb{i}", [128, 4096], BF16) for i in range(NWB)]
        bWB = [B(f"wb{i}") for i in range(NWB)]
        wrot = {"i": 0}

        wcache = {}

        def wload(src, kc, ncols, key=None):
            if key is not None and key in wcache:
                return wcache.pop(key)
            k = wrot["i"] % NWB
            wrot["i"] += 1
            view = WB[k][:, 0:kc * ncols].rearrange("p (k n) -> p k n", n=ncols)
            dma("pool", view, src.rearrange("(k p) n -> p k n", p=128), writes=[bWB[k]], nobar=True)
            return view, bWB[k]

        def wprefetch(key, src, kc, ncols):
            wcache[key] = wload(src, kc, ncols)

        X = sb("X", [128, NTT, D]); bX = [B(f"X{i}") for i in range(NTT)]
        XS = sb("XS", [NS, D]); bXS = B("XS")
        hT = sb("hT", [128, 8, TW], BF16); bhT = [B(f"hT{i}") for i in range(NTT + 1)]
        cT = sb("cT", [128, 8, 5], BF16); bcT = B("cT")
        modT = sb("modT", [128, DEPTH, 24, 5]); bmod = B("modT")
        modA = sb("modA", [128, DEPTH, 8, 5]); modB_ = modT
        mT = sb("mT", [128, 8, TW], BF16); bmT_ = [B(f"mT{i}") for i in range(3)]
        wdw = sb("wdw", [128, DEPTH, 4, 31]); bdw = sb("bdw", [128, DEPTH, 4]); lng = sb("lng", [128, DEPTH, 4]); lnb = sb("lnb", [128, DEPTH, 4])
        for l_ in range(DEPTH):
            dma("sp", wdw[:, l_], w_dwT[l_], part=[bC]); dma("sp", bdw[:, l_], b_dwT[l_], part=[bC])
            dma("sp", lng[:, l_], ln_gT[l_], part=[bC]); dma("sp", lnb[:, l_], ln_bT[l_], part=[bC])
        lbl = sb("lbl", [128, DEPTH, 4]); lbm = sb("lbm", [128, DEPTH, 4]); oml = sb("oml", [128, DEPTH, 4]); noml = sb("noml", [128, DEPTH, 4])
        cng = sb("cng_sb", [128, DEPTH]); blb = B("lb")
        for l_ in range(DEPTH):
            dma("sp", lbl[:, l_, :], lbT_in[l_], part=[blb]); dma("sp", cng[:, l_:l_ + 1], cng_in[l_], part=[blb])
        op("dve", lambda e: e.tensor_tensor(out=lbm[:, 1, :], in0=lbl[:, 1, :], in1=lbl[:, 0, :], op=ALU.subtract), reads=[blb], part=[blb])
        op("act", lambda e: e.activation(out=lbm[:, 1, :], in_=lbm[:, 1, :], func=AF.Sigmoid), reads=[blb], part=[blb])
        op("dve", lambda e: e.memset(lbm[:, 0, :], 0.0), part=[blb])
        op("dve", lambda e: e.tensor_scalar(out=oml[:, :, :], in0=lbm[:, :, :], scalar1=-1.0, scalar2=1.0, op0=ALU.mult, op1=ALU.add), reads=[blb], part=[blb])
        op("dve", lambda e: e.tensor_scalar(out=noml[:, :, :], in0=oml[:, :, :], scalar1=-1.0, scalar2=None, op0=ALU.mult), reads=[blb], part=[blb])
        op("dve", lambda e: e.tensor_scalar(out=lbm[:, :, :], in0=lbm[:, :, :], scalar1=1e-30, scalar2=None, op0=ALU.max), reads=[blb], part=[blb])
        Sst = sb("Sst", [128, DEPTH, 4, 128]); bS = [[B(f"S{l_}{h}") for h in range(4)] for l_ in range(DEPTH)]
        sQ = sb("sQ", [NS, 512], BF16); sQI = sb("sQI", [NS, 256], BF16); sKV = sb("sKV", [NS, 320]); sWI = sb("sWI", [NS, 4]); bsQ = B("sQ")
        bgS = sb("bgS", [128, 4, NS], BF16); bbgS = B("bgS"); uS = sb("uS", [128, 4, NS]); agS = sb("agS", [128, 4, NS], BF16); bsA = B("sA")
        qcS = sb("qcS", [128, 4, NS], BF16); sigS = sb("sigS", [128, 4, NS]); cgS = sb("cgS", [128, 4, NS], BF16); ciS = sb("ciS", [NS, 4, 128], BF16); bsC = B("sC")
        yS = sb("yS", [128, 3, 4, NS], BF16); byS = [B(f"yS{i}") for i in range(3)]
        uhalo = sb("uhalo", [128, DEPTH, 4, 30], BF16); buh = B("uhalo")
        ones_f = sb("ones_f", [128, 128])
        op("pool", lambda e: e.memset(ones_f[:], 1.0), part=[bC])
        onesm = sb("onesm", [128, 128], BF16)
        op("pool", lambda e: e.memset(onesm[:], 1.0 / 512), part=[bC])
        kdA = sb("kdA", [128, DEPTH, 2, TS], BF16); kdB = sb("kdB", [128, 2, TS], BF16)
        VpA = sb("VpA", [128, DEPTH, NTT, 2, 128], BF16); VpB = sb("VpB", [128, NTT, 2, 128], BF16)
        kidA = sb("kidA", [128, DEPTH, TS], BF16); kidB = sb("kidB", [128, TS], BF16)
        bKA = [[B(f"KA{l}_{i}") for i in range(NTT)] for l in range(DEPTH)]
        bKB = [B(f"KB{i}") for i in range(NTT)]

        def kd_ap(l, kb, g, nblk=1):
            if kb < NTT:
                return kdA[:, l, g, kb * 128:(kb + nblk) * 128]
            return kdB[:, g, (kb - NTT) * 128:(kb - NTT + nblk) * 128]

        def kid_ap(l, kb, nblk=1):
            if kb < NTT:
                return kidA[:, l, kb * 128:(kb + nblk) * 128]
            return kidB[:, (kb - NTT) * 128:(kb - NTT + nblk) * 128]

        def vp_ap(l, kb, g):
            return VpA[:, l, kb, g, :] if kb < NTT else VpB[:, kb - NTT, g, :]

        def bK(l, kb):
            return bKA[l][kb] if kb < NTT else bKB[kb - NTT]

        bis = sb("bis", [128, 64]); bbis = B("bis")
        pw2 = sb("pw2", [128, 32]);
        dma("sp", pw2[:], cst["pw2"], part=[bC])
        scr = sb("scr", [128, D]); bscr = B("scr")
        scr_b = scr[:, :].bitcast(BF16)
        mgs = [scr_b[:, i * 512:(i + 1) * 512] for i in range(2)]; bmgs = [B(f"mgs{i}") for i in range(2)]
        mtmp = [scr_b[:, 1024 + i * 512:1024 + (i + 1) * 512] for i in range(2)]; bmtmp = [B(f"mtmp{i}") for i in range(2)]
        xn = sb("xn", [128, D], BF16); bxn = B("xn")
        st8 = sb("st8", [128, 64]); bst8 = B("st8")

        ada_st = ExitStack()
        cin = ada_st.enter_context(nc.sbuf_tensor("cin", [5, D], F32)); bcin = B("cin")
        dma("sp", cin[:], cc, writes=[bcin])
        csil = ada_st.enter_context(nc.sbuf_tensor("csil", [5, D], F32)); bcs = B("csil")
        op("act", lambda e: e.activation(out=csil[:], in_=cin[:], func=AF.Silu), reads=[bcin], writes=[bcs])
        for k in range(8):
            ps, bps = psum()
            op("pe", lambda e: e.transpose(out=ps[:, 0:5], in_=csil[:, k * 128:(k + 1) * 128], identity=ident_f[0:5, 0:5]),
               reads=[bcs, bC], writes=[bps])
            op("act", lambda e: e.copy(out=cT[:, k, :], in_=ps[:, 0:5]), reads=[bps], part=[bcT])
        for l in range(NL):
            for gi in range(6):
                w, bw = wload(w_ada[l][:, gi * 512:(gi + 1) * 512], 8, 512)
                for c4 in range(4):
                    ch = gi * 4 + c4
                    ps, bps = psum()
                    for k in range(8):
                        op("pe", lambda e: e.matmul(ps[:, 0:5], lhsT=w[:, k, c4 * 128:(c4 + 1) * 128], rhs=cT[:, k, :],
                                                    start=(k == 0), stop=(k == 7)),
                           reads=[bw, bcT], writes=[bps] if k == 0 else (), part=() if k == 0 else [bps], inc=(k == 7))
                    op("act", lambda e: e.activation(out=modT[:, l, ch, :], in_=ps[:, 0:5], func=AF.Identity,
                                                     bias=badT[:, l, ch:ch + 1], scale=1.0),
                       reads=[bps, bC], part=[bmod])
            op("dve", lambda e: e.tensor_scalar(out=modA[:, l, :, :], in0=modT[:, l, 8:16, :], scalar1=1.0, scalar2=None, op0=ALU.add),
               reads=[bmod], part=[bmod])
            op("dve", lambda e: e.tensor_tensor(out=modA[:, l, :, :], in0=modA[:, l, :, :],
                                                in1=ngT[:, l, :].unsqueeze(2).to_broadcast([128, 8, 5]), op=ALU.mult),
               reads=[bmod, bC], part=[bmod])

        Tk.barrier()
        ada_st.close()

        def make_hT(l, src, npart, bsrc, col0, bh, rows):
            op("act", lambda e: e.activation(out=scr[0:npart, :], in_=src, func=AF.Square, accum_out=st8[0:npart, 0:1]),
               reads=[bsrc], writes=[bscr, bst8])
            op("act", lambda e: e.activation(out=st8[0:npart, 1:2], in_=st8[0:npart, 0:1], func=AF.Sqrt,
                                             bias=eps_t[0:npart, :], scale=1.0 / D), reads=[bst8, bC], part=[bst8])
            op("dve", lambda e: e.reciprocal(out=st8[0:npart, 2:3], in_=st8[0:npart, 1:2]), reads=[bst8], part=[bst8])
            op("act", lambda e: e.activation(out=xn[0:npart, :], in_=src, func=AF.Copy, scale=st8[0:npart, 2:3]),
               reads=[bsrc, bst8], writes=[bxn])
            for half in range(2):
                ps, bps = psum()
                psb = ps[:].bitcast(BF16)
                for k4 in range(4):
                    k = half * 4 + k4
                    op("pe", lambda e: e.transpose(out=psb[:, k4 * 128:k4 * 128 + npart], in_=xn[0:npart, k * 128:(k + 1) * 128],
                                                   identity=ident_b[0:npart, 0:npart]),
                       reads=[bxn, bC], writes=[bps] if k4 == 0 else (), part=() if k4 == 0 else [bps], inc=(k4 == 3))
                for k4 in range(4):
                    k = half * 4 + k4
                    for (lo, hi, r) in rows:
                        op("act", lambda e: e.activation(out=hT[:, k, col0 + lo:col0 + hi], in_=psb[:, k4 * 128 + lo:k4 * 128 + hi],
                                                         func=AF.Identity, scale=modA[:, l, k, r:r + 1], bias=modT[:, l, k, r:r + 1]),
                           reads=[bps, bmod], part=[bh])

        rtmp = scr[:, :].rearrange("p (a b) -> p a b", a=4); brt = bscr

        def rms_heads(ps_ap, nh, npart, gsel, l):
            n = nh * 64
            op("act", lambda e: e.activation(out=scr[0:npart, 0:n], in_=ps_ap, func=AF.Square), reads=[], writes=[bscr])
            op("dve", lambda e: e.tensor_reduce(out=st8[0:npart, 8:8 + nh], in_=scr[0:npart, 0:n].rearrange("p (h d) -> p h d", d=64),
                                                axis=AX.X, op=ALU.add), reads=[bscr], part=[bst8])
            op("act", lambda e: e.activation(out=st8[0:npart, 16:16 + nh], in_=st8[0:npart, 8:8 + nh], func=AF.Sqrt,
                                             bias=eps_t[0:npart, :], scale=1.0 / 64), reads=[bst8, bC], part=[bst8])
            op("dve", lambda e: e.reciprocal(out=st8[0:npart, 24:24 + nh], in_=st8[0:npart, 16:16 + nh]), reads=[bst8], part=[bst8])
            op("dve", lambda e: e.tensor_tensor(out=qtok[0:npart, 0:n].rearrange("p (h d) -> p h d", d=64),
                                                in0=ps_ap.rearrange("p (h d) -> p h d", d=64),
                                                in1=st8[0:npart, 24:24 + nh].unsqueeze(2).to_broadcast([npart, nh, 64]), op=ALU.mult),
               reads=[bst8], writes=[bqtok])
            if gsel is not None:
                op("pool", lambda e: e.tensor_tensor(out=qtok[0:npart, 0:n].rearrange("p (h d) -> p h d", d=64),
                                                     in0=qtok[0:npart, 0:n].rearrange("p (h d) -> p h d", d=64),
                                                     in1=gsel[0:npart, l, :].unsqueeze(1).to_broadcast([npart, nh, 64]), op=ALU.mult),
                   reads=[bqtok, bC], writes=[bqtok])

        def rope(src_ap, bsrc, nh, npart, cos_ap, sin_ap, out_ap, bout, eng="pool", part_out=False):
            s3 = src_ap.rearrange("p (h two d) -> p h two d", two=2, d=32)
            o3 = out_ap.rearrange("p (h two d) -> p h two d", two=2, d=32)
            x1, x2 = s3[:, :, 0, :], s3[:, :, 1, :]
            cb = cos_ap.unsqueeze(1).to_broadcast([npart, nh, 32])
            sbb = sin_ap.unsqueeze(1).to_broadcast([npart, nh, 32])
            t = [rtmp[0:npart, i, 0:nh * 32].rearrange("p (h d) -> p h d", d=32) for i in range(4)]
            op(eng, lambda e: e.tensor_tensor(out=t[0], in0=x1, in1=cb, op=ALU.mult), reads=[bsrc, bC], writes=[brt])
            op(eng, lambda e: e.tensor_tensor(out=t[1], in0=x2, in1=sbb, op=ALU.mult), reads=[bsrc, bC], part=[brt])
            op(eng, lambda e: e.tensor_tensor(out=t[2], in0=x2, in1=cb, op=ALU.mult), reads=[bsrc, bC], part=[brt])
            op(eng, lambda e: e.tensor_tensor(out=t[3], in0=x1, in1=sbb, op=ALU.mult), reads=[bsrc, bC], part=[brt])
            w_ = dict(part=[bout]) if part_out else dict(writes=[bout])
            op(eng, lambda e: e.tensor_tensor(out=o3[:, :, 0, :], in0=t[0], in1=t[1], op=ALU.subtract), reads=[brt], **w_)
            op(eng, lambda e: e.tensor_tensor(out=o3[:, :, 1, :], in0=t[2], in1=t[3], op=ALU.add), reads=[brt], part=[bout])

        def sample_dsa(l):
            NITS = 20
            U16 = mybir.dt.uint16
            ph0 = ExitStack()
            a0 = lambda n_, shp, dt=F32: ph0.enter_context(nc.sbuf_tensor(f"{n_}_s{l}", list(shp), dt))
            ptx = a0("ptx", [128, 4, 8], I32); idx2 = a0("idx2", [128, 4, 8], I32); roff = a0("roff", [128, 1]); bidx = B("idx")
            newmask = a0("newmask", [128, 4, 16]); grp4 = a0("grp4", [128, 128]); rep4 = a0("rep4", [16, 4, 128]); bsc = B("sconst")
            dma("sp", ptx[:], ptx_in, part=[bidx]); dma("sp", roff[:], cst["roff"], part=[bidx])
            dma("sp", newmask[:], cst["newmask"], part=[bsc]); dma("sp", grp4[:], cst["grp4"], part=[bsc]); dma("sp", rep4[:], cst["rep4"], part=[bsc])
            op("dve", lambda e: e.tensor_scalar(out=idx2[:], in0=ptx[:], scalar1=16.0, scalar2=roff[:, 0:1], op0=ALU.mult, op1=ALU.add), reads=[bidx], writes=[bidx])
            qsTz = a0("qsTz", [128, 2, 4, NS], BF16); sQg = a0("sQg", [NS, 4, 128], BF16); qiT4 = a0("qiT4", [64, 4, NS], BF16); kinT = a0("kinT", [64, NS], BF16); knT = a0("knT", [128, NS], BF16)
            vnew = a0("vnew", [NS, 2, 65], BF16); kvb = a0("kvb", [NS, 192], BF16); wrepS = a0("wrepS", [128, 4, 4]); bq = B("sq")
            Wh = a0("Wh", [64, 4, 252], BF16); bWh = B("Wh")
            op("pool", lambda e: e.memset(qsTz[:], 0.0), writes=[bq])
            op("pool", lambda e: e.memset(Wh[:], 0.0), writes=[bWh])
            op("pool", lambda e: e.memset(vnew[:], 1.0), part=[bq])
            op("act", lambda e: e.copy(out=kvb[:, 0:128], in_=sKV[:, 0:128]), reads=[bsQ], part=[bq])
            op("act", lambda e: e.copy(out=kvb[:, 128:192], in_=sKV[:, 256:320]), reads=[bsQ], part=[bq])
            op("act", lambda e: e.copy(out=vnew[:, :, 0:64], in_=sKV[:, 128:256].rearrange("p (g d) -> p g d", d=64)), reads=[bsQ], part=[bq])
            ps, bps = psum(0, 3)
            pb = ps[:].bitcast(BF16)
            op("act", lambda e: e.copy(out=sQg[:, :, :].rearrange("p h (g d) -> p h g d", g=2), in_=sQ[:, :].rearrange("p (g h d) -> p h g d", g=2, h=4)),
               reads=[bsQ], part=[bq])
            for hh in range(4):
                op("pe", lambda e: e.transpose(out=pb[:, hh * 16:(hh + 1) * 16], in_=sQg[:, hh, :], identity=ident_b[0:NS, 0:NS]),
                   reads=[bq, bC], writes=[bps] if hh == 0 else (), part=() if hh == 0 else [bps], inc=False)
            for h in range(4):
                op("pe", lambda e: e.transpose(out=pb[0:64, 64 + h * 16:64 + (h + 1) * 16], in_=sQI[:, h * 64:(h + 1) * 64], identity=ident_b[0:NS, 0:NS]),
                   reads=[bsQ, bC], part=[bps], inc=False)
            op("pe", lambda e: e.transpose(out=pb[0:64, 128:144], in_=kvb[:, 128:192], identity=ident_b[0:NS, 0:NS]), reads=[bq, bC], part=[bps], inc=False)
            op("pe", lambda e: e.transpose(out=pb[:, 144:160], in_=kvb[:, 0:128], identity=ident_b[0:NS, 0:NS]), reads=[bq, bC], part=[bps])
            for g in range(2):
                rows = slice(g * 64, (g + 1) * 64)
                op("act", lambda e: e.copy(out=qsTz[rows, g, :, :].rearrange("p b (h t) -> p b h t", h=4),
                                           in_=pb[rows, 0:64].rearrange("p (h b t) -> p b h t", h=4, b=4)), reads=[bps], part=[bq])
            op("act", lambda e: e.copy(out=qiT4[:, :, :], in_=pb[0:64, 64:128].rearrange("p (h t) -> p h t", t=NS)), reads=[bps], part=[bq])
            op("act", lambda e: e.copy(out=kinT[:, :], in_=pb[0:64, 128:144]), reads=[bps], part=[bq])
            op("act", lambda e: e.copy(out=knT[:, :], in_=pb[:, 144:160]), reads=[bps], part=[bq])
            for b in range(4):
                ps, bps = psum(0, 3)
                op("pe", lambda e: e.matmul(ps[:, 0:4], lhsT=rep4[:, b, :], rhs=sWI[:, :], start=True, stop=True), reads=[bsc, bsQ], writes=[bps])
                op("act", lambda e: e.copy(out=wrepS[:, b, :], in_=ps[:, 0:4]), reads=[bps], part=[bq])

            bgI = [B(f"gI{j}") for j in range(8)]; bgK = [B(f"gK{j}") for j in range(4)]
            mT2 = a0("mT2", [128, 2, 128], BF16); mTn = a0("mTn", [NS, 128], BF16); bmT2 = B("mT2")
            for b in range(4):
                p1 = ExitStack()
                a1 = lambda n_, shp, dt=F32: p1.enter_context(nc.sbuf_tensor(f"{n_}_s{l}{b}", list(shp), dt))
                gI = a1("gI", [128, 8, 512])
                kiTs = a1("kiTs", [64, PAST], BF16); bkiT = [B(f"kiT{j}") for j in range(8)]
                Ib = a1("Ib", [128, 272]); bIb = B("Ib"); rlS = a1("rlS", [128, 272]); brlS = B("rlS")
                mkS = a1("mkS", [128, 272], BF16); bmkS = B("mkS"); bs = a1("bs", [128, 64]); bbs = B("bs")
                op("pool", lambda e: e.tensor_copy(out=Wh[:, :, 124:128], in_=qiT4[:, :, 4 * b:4 * b + 4]), reads=[bq], writes=[bWh])
                for jhi in range(8):
                    dma("pool", None, None, reads=[bidx], writes=[bgI[jhi]],
                        fn=lambda e: e.indirect_dma_start(out=gI[:, jhi, :], out_offset=None, in_=cik_in[l].rearrange("(r e) d -> r (e d)", e=8),
                                                          in_offset=bass.IndirectOffsetOnAxis(ap=idx2[:, b, jhi:jhi + 1], axis=0)))
                for jhi in range(8):
                    for r4 in range(2):
                        ps, bps = psum(0, 3)
                        for x in range(4):
                            rr = r4 * 4 + x
                            op("pe", lambda e: e.transpose(out=ps[0:64, x * 128:(x + 1) * 128], in_=gI[:, jhi, rr * 64:(rr + 1) * 64], identity=ident_f[:, :]),
                               reads=[bgI[jhi], bC], writes=[bps] if x == 0 else (), part=() if x == 0 else [bps], inc=(x == 3))
                        k0 = (jhi * 8 + r4 * 4) * 128
                        op("act" if r4 == 0 else "dve", (lambda e: e.copy(out=kiTs[:, k0:k0 + 512], in_=ps[0:64, :])) if r4 == 0 else
                           (lambda e: e.tensor_copy(out=kiTs[:, k0:k0 + 512], in_=ps[0:64, :])), reads=[bps], part=[bkiT[jhi]])
                for h in range(4):
                    ps, bps = psum(0, 3)
                    for k in range(32):
                        op("pe", lambda e: e.matmul(ps[:, 0:256], lhsT=Wh[:, h, 124 - 4 * k:252 - 4 * k], rhs=kiTs[:, k * 256:(k + 1) * 256], start=(k == 0), stop=(k == 31)),
                           reads=[bWh, bkiT[k // 4]], writes=[bps] if k == 0 else (), part=() if k == 0 else [bps], inc=(k == 31))
                    op("pe", lambda e: e.matmul(ps[:, 256:272], lhsT=Wh[:, h, 124:252], rhs=kinT[:, :], start=True, stop=True), reads=[bWh, bq], part=[bps])
                    op("act", lambda e: e.activation(out=rlS[:, :], in_=ps[:, 0:272], func=AF.Relu), reads=[bps], writes=[brlS])
                    if h == 0:
                        op("dve", lambda e: e.tensor_scalar(out=Ib[:, :], in0=rlS[:, :], scalar1=wrepS[:, b, 0:1], scalar2=None, op0=ALU.mult), reads=[brlS, bq], writes=[bIb])
                    else:
                        op("dve", lambda e: e.scalar_tensor_tensor(out=Ib[:, :], in0=rlS[:, :], scalar=wrepS[:, b, h:h + 1], in1=Ib[:, :], op0=ALU.mult, op1=ALU.add),
                           reads=[brlS, bq], writes=[bIb])
                op("dve", lambda e: e.tensor_reduce(out=bs[:, 0:1], in_=Ib[:, :], axis=AX.X, op=ALU.max, apply_absolute_value=True), reads=[bIb], writes=[bbs])
                ps, bps = psum(0, 3)
                op("pe", lambda e: e.matmul(ps[:, 0:1], lhsT=ones_f[:, :], rhs=bs[:, 0:1], start=True, stop=True), reads=[bbs, bC], writes=[bps])
                op("dve", lambda e: e.tensor_scalar(out=bs[:, 1:2], in0=ps[:, 0:1], scalar1=1.0, scalar2=None, op0=ALU.add), reads=[bps], part=[bbs])
                op("dve", lambda e: e.tensor_scalar(out=bs[:, 16:16 + NITS + 2], in0=pw2[:, 0:NITS + 2], scalar1=bs[:, 1:2], scalar2=None, op0=ALU.mult), reads=[bC, bbs], part=[bbs])
                op("dve", lambda e: e.memset(bs[:, 2:3], 0.0), part=[bbs])
                op("dve", lambda e: e.tensor_tensor(out=Ib[:, 256:272], in0=Ib[:, 256:272], in1=newmask[:, b, :], op=ALU.add), reads=[bsc], writes=[bIb])
                for it in range(NITS + 1):
                    if it < NITS:
                        op("dve", lambda e: e.tensor_scalar(out=mkS[:, :], in0=Ib[:, :], scalar1=bs[:, 2:3], scalar2=None, op0=ALU.is_ge, op1=ALU.add, accum_out=bs[:, 3:4]),
                           reads=[bIb, bbs], writes=[bmkS], part=[bbs])
                        ps, bps = psum(0, 3)
                        op("pe", lambda e: e.matmul(ps[:, 0:1], lhsT=grp4[:, :], rhs=bs[:, 3:4], start=True, stop=True), reads=[bbs, bsc], writes=[bps])
                        op("dve", lambda e: e.scalar_tensor_tensor(out=bs[:, 4:5], in0=ps[:, 0:1], scalar=256.0, in1=bs[:, 16 + it:17 + it], op0=ALU.is_ge, op1=ALU.mult),
                           reads=[bps, bbs], part=[bbs])
                        op("dve", lambda e: e.scalar_tensor_tensor(out=bs[:, 2:3], in0=bs[:, 4:5], scalar=bs[:, 17 + it:18 + it], in1=bs[:, 2:3], op0=ALU.subtract, op1=ALU.add),
                           reads=[bbs], part=[bbs])
                    else:
                        op("dve", lambda e: e.tensor_tensor(out=bs[:, 5:6], in0=bs[:, 2:3], in1=bs[:, 16 + it:17 + it], op=ALU.subtract), reads=[bbs], part=[bbs])
                op("dve", lambda e: e.tensor_scalar(out=mkS[:, :], in0=Ib[:, :], scalar1=bs[:, 5:6], scalar2=None, op0=ALU.is_ge), reads=[bIb, bbs], writes=[bmkS])
                ps, bps = psum(0, 3)
                pb = ps[:].bitcast(BF16)
                for cb in range(2):
                    op("pe", lambda e: e.transpose(out=pb[:, cb * 128:(cb + 1) * 128], in_=mkS[:, cb * 128:(cb + 1) * 128], identity=ident_b[:, :]),
                       reads=[bmkS, bC], writes=[bps] if cb == 0 else (), part=() if cb == 0 else [bps], inc=False)
                op("pe", lambda e: e.transpose(out=pb[0:NS, 256:384], in_=mkS[:, 256:272], identity=ident_b[:, :]), reads=[bmkS, bC], part=[bps])
                op("act", lambda e: e.copy(out=mT2[:, :, :], in_=pb[:, 0:256].rearrange("p (c t) -> p c t", c=2)), reads=[bps], writes=[bmT2])
                op("act", lambda e: e.copy(out=mTn[:, :], in_=pb[0:NS, 256:384]), reads=[bps], part=[bmT2])
                if dbg is not None and cfg.get("dbgkey") == "s_dbg" and l == 0 and b == 0:
                    dma("sp", dbg[1][:, 0:272], Ib[:, :], reads=[bIb], own=bIb, is_output=True)
                    dma("pool", dbg[1][:, 272:544], mkS[:, :], reads=[bmkS], own=bmkS, is_output=True)
                    dma("sp", dbg[1][:, 544:552], bs[:, 0:8], reads=[bbs], own=bbs, is_output=True)
                    dma("pool", dbg[1][0:64, 600:600 + 2048], kiTs[:, 0:2048], reads=bkiT, own=bkiT[0], is_output=True)
                Tk.barrier(); p1.close()
                p2 = ExitStack()
                a2 = lambda n_, shp, dt=F32: p2.enter_context(nc.sbuf_tensor(f"{n_}_s{l}{b}", list(shp), dt))
                gK = a2("gK", [128, 4, 1024])
                KTs = a2("KTs", [128, PAST], BF16); bKT = [B(f"KT{j}") for j in range(8)]
                Vs = a2("Vs", [128, 32, 2, 65], BF16); bVs = B("Vs")
                Ee = a2("Ee", [128, 64, 32], BF16); bEe = [B(f"Ee{j}") for j in range(4)]; En = a2("En", [NS, 32], BF16); bEn = B("En")
                osb = a2("osb", [NS, 2, 2, 64], BF16); bosb = B("osb"); ork = a2("ork", [NS, 4]); bork = B("ork")
                op("pool", lambda e: e.memset(Vs[:, :, :, 64:65], 1.0), writes=[bVs])
                ck_r = ck_in[l].rearrange("(r e) d -> r (e d)", e=8); cv_r = cv_in[l].rearrange("(r e) d -> r (e d)", e=8)
                for half in range(2):
                    for j4 in range(4):
                        jhi = half * 4 + j4
                        dma("pool", None, None, reads=[bidx], writes=[bgK[j4]],
                            fn=lambda e: e.indirect_dma_start(out=gK[:, j4, :], out_offset=None, in_=ck_r,
                                                              in_offset=bass.IndirectOffsetOnAxis(ap=idx2[:, b, jhi:jhi + 1], axis=0)))
                    for j4 in range(4):
                        jhi = half * 4 + j4
                        for r4 in range(2):
                            ps, bps = psum(0, 3)
                            for x in range(4):
                                rr = r4 * 4 + x
                                op("pe", lambda e: e.transpose(out=ps[:, x * 128:(x + 1) * 128], in_=gK[:, j4, rr * 128:(rr + 1) * 128], identity=ident_f[:, :]),
                                   reads=[bgK[j4], bC], writes=[bps] if x == 0 else (), part=() if x == 0 else [bps], inc=(x == 3))
                            k0 = (jhi * 8 + r4 * 4) * 128
                            if r4 == 0:
                                op("act", lambda e: e.copy(out=KTs[:, k0:k0 + 512], in_=ps[:, :]), reads=[bps], part=[bKT[jhi]])
                            else:
                                op("dve", lambda e: e.tensor_copy(out=KTs[:, k0:k0 + 512], in_=ps[:, :]), reads=[bps], part=[bKT[jhi]])
                for bk in range(4):
                    psS, bpsS = PS[3 + bk], bPS[3 + bk]
                    for k16 in range(16):
                        kt = bk * 16 + k16
                        for g in range(2):
                            op("pe", lambda e: e.matmul(psS[:, k16 * 32 + g * 16:k16 * 32 + (g + 1) * 16], lhsT=KTs[:, kt * 128:(kt + 1) * 128],
                                                        rhs=qsTz[:, g, b, :], start=True, stop=True),
                               reads=[bKT[kt // 8], bq], writes=[bpsS] if (k16 == 0 and g == 0) else (), part=() if (k16 == 0 and g == 0) else [bpsS],
                               inc=(k16 == 15 and g == 1))
                    op("act", lambda e: e.activation(out=Ee[:, bk * 16:(bk + 1) * 16, :], in_=psS[:, :].rearrange("p (k c) -> p k c", c=32), func=AF.Exp, scale=0.125),
                       reads=[bpsS], writes=[bEe[bk]])
                    for cb in range(2):
                        e4 = Ee[:, bk * 16:(bk + 1) * 16, :].rearrange("p (s c) (gh t) -> p s c gh t", c=2, t=4)[:, :, cb, :, :]
                        m4 = mT2[:, cb, bk * 32:(bk + 1) * 32].rearrange("p (s t) -> p s t", t=4).unsqueeze(2).to_broadcast([128, 8, 8, 4])
                        op("pool", lambda e: e.tensor_tensor(out=e4, in0=e4, in1=m4, op=ALU.mult), reads=[bmT2], part=[bEe[bk]])
                psN, bpsN = PS[7], bPS[7]
                for g in range(2):
                    op("pe", lambda e: e.matmul(psN[0:NS, g * 16:(g + 1) * 16], lhsT=knT[:, :], rhs=qsTz[:, g, b, :], start=True, stop=True),
                       reads=[bq], writes=[bpsN] if g == 0 else (), part=() if g == 0 else [bpsN], inc=(g == 1))
                op("act", lambda e: e.activation(out=En[:, :], in_=psN[0:NS, 0:32], func=AF.Exp, scale=0.125), reads=[bpsN], writes=[bEn])
                op("pool", lambda e: e.tensor_tensor(out=En[:, :].rearrange("p (gh t) -> p gh t", t=4), in0=En[:, :].rearrange("p (gh t) -> p gh t", t=4),
                                                     in1=mTn[:, 0:4].unsqueeze(1).to_broadcast([NS, 8, 4]), op=ALU.mult), reads=[bmT2], writes=[bEn])
                psOg = [PS[3], PS[4]]; bpsOg = [bPS[3], bPS[4]]
                for half in range(2):
                    for j4 in range(4):
                        jhi = half * 4 + j4
                        dma("pool", None, None, reads=[bidx], writes=[bgK[j4]],
                            fn=lambda e: e.indirect_dma_start(out=gK[:, j4, :], out_offset=None, in_=cv_r,
                                                              in_offset=bass.IndirectOffsetOnAxis(ap=idx2[:, b, jhi:jhi + 1], axis=0)))
                    for j4 in range(4):
                        vsrc = gK[:, j4, :].rearrange("p (r g d) -> p r g d", g=2, d=64)
                        vdst = Vs[:, j4 * 8:(j4 + 1) * 8, :, 0:64]
                        if j4 % 2 == 0:
                            op("act", lambda e: e.copy(out=vdst, in_=vsrc), reads=[bgK[j4]], part=[bVs])
                        else:
                            op("pool", lambda e: e.tensor_copy(out=vdst, in_=vsrc), reads=[bgK[j4]], part=[bVs])
                    for g in range(2):
                        for k32 in range(32):
                            kt = half * 32 + k32
                            first = (half == 0 and k32 == 0)
                            op("pe", lambda e: e.matmul(psOg[g][0:NS, 0:65], lhsT=Ee[:, kt, g * 16:(g + 1) * 16], rhs=Vs[:, k32, g, :], start=first, stop=False),
                               reads=[bEe[kt // 16], bVs], writes=[bpsOg[g]] if first else (), part=() if first else [bpsOg[g]], inc=(k32 == 31))
                for g in range(2):
                    op("pe", lambda e: e.matmul(psOg[g][0:NS, 0:65], lhsT=En[:, g * 16:(g + 1) * 16], rhs=vnew[:, g, :], start=False, stop=True),
                       reads=[bEn, bq], part=[bpsOg[g]])
                for g in range(2):
                    op("dve", lambda e: e.reciprocal(out=ork[:, g:g + 1], in_=psOg[g][0:NS, 64:65]), reads=[bpsOg[g]], part=[bork])
                    op("dve", lambda e: e.tensor_scalar(out=osb[:, g, :, :], in0=psOg[g][0:NS, 0:64].unsqueeze(1).to_broadcast([NS, 2, 64]),
                                                        scalar1=ork[:, g:g + 1], scalar2=None, op0=ALU.mult), reads=[bpsOg[g], bork], part=[bosb])
                ps, bps = psum(0, 3)
                pb = ps[:].bitcast(BF16)
                for g in range(2):
                    op("pe", lambda e: e.transpose(out=pb[:, g * 16:(g + 1) * 16], in_=osb[:, g, :, :].rearrange("p r d -> p (r d)"), identity=ident_b[0:NS, 0:NS]),
                       reads=[bosb, bC], writes=[bps] if g == 0 else (), part=() if g == 0 else [bps], inc=(g == 1))
                for g in range(2):
                    for hh in range(4):
                        j, c = hh % 2, 2 * g + hh // 2
                        rows = slice(j * 64, (j + 1) * 64)
                        op("dve", lambda e: e.tensor_tensor(out=yS[rows, 1, c, 4 * b:4 * b + 4], in0=pb[rows, g * 16 + hh * 4:g * 16 + hh * 4 + 4],
                                                            in1=bgS[rows, c, 4 * b:4 * b + 4], op=ALU.mult), reads=[bps, bbgS], part=[byS[1]])
                if dbg is not None and cfg.get("dbgkey") == "s_dbg" and l == 0 and b == 0:
                    dma("pool", dbg[0][:, 0:4096], KTs[:, 0:4096], reads=bKT, own=bKT[0], is_output=True)
                Tk.barrier(); p2.close()
            Tk.barrier(); ph0.close()

        def sample_conv(l):
            ph = ExitStack()
            a = lambda n_, shp, dt=F32: ph.enter_context(nc.sbuf_tensor(f"{n_}_c{l}", list(shp), dt))
            sct = a("sct", [30, 4, 512]); bsct = B("sct")
            ufS = a("ufS", [128, 4, 4, 34]); bufS = B("ufS")
            prod = a("prod", [128, 4, 4, 31]); bprod = B("prod")
            ycv = a("ycv", [128, 4, NS]); bycv = B("ycv"); ycb = a("ycb", [128, 4, NS], BF16); ysq = a("ysq", [128, 4, NS], BF16)
            lnS = a("lnS", [128, 3, NS]); blnS = B("lnS"); t1 = a("t1", [128, 4, NS]); bt1 = B("t1"); t2 = a("t2", [128, 4, NS], BF16)
            utok = a("utok", [NS, 512]); butok = B("utok")
            dma("sp", sct[:, :, :], sconv_in[l].rearrange("b r c -> r b c"), writes=[bsct])
            for b in range(4):
                dma("sp", convs[l, b, 0:26, :], sct[4:30, b, :], reads=[bsct], own=bsct, is_output=True)
            for b in range(4):
                ps, bps = psum()
                for c in range(4):
                    op("pe", lambda e: e.transpose(out=ps[:, c * 32:c * 32 + 30], in_=sct[:, b, c * 128:(c + 1) * 128], identity=ident_f[0:30, 0:30]),
                       reads=[bsct, bC], writes=[bps] if c == 0 else (), part=() if c == 0 else [bps], inc=(c == 3))
                op("act", lambda e: e.copy(out=ufS[:, b, :, 0:30], in_=ps[:, 0:128].rearrange("p (c r) -> p c r", r=32)[:, :, 0:30]), reads=[bps], part=[bufS])
            op("act", lambda e: e.copy(out=ufS[:, :, :, 30:34], in_=uS[:, :, :].rearrange("p c (b t) -> p b c t", t=4)), reads=[bsA], part=[bufS])
            for c in range(4):
                win = ufS[:, :, c, 0:31].unsqueeze(2).to_broadcast([128, 4, 4, 31])
                import copy as _cp
                base = ufS[:, :, c, :]
                for t in range(4):
                    op("dve", lambda e: e.tensor_tensor(out=prod[:, :, t, :], in0=ufS[:, :, c, t:t + 31],
                                                        in1=wdw[:, l, c, :].unsqueeze(1).to_broadcast([128, 4, 31]), op=ALU.mult),
                       reads=[bufS, bC], **(dict(writes=[bprod]) if t == 0 else dict(part=[bprod])))
                op("dve", lambda e: e.tensor_reduce(out=ycv[:, c, :].rearrange("p (b t) -> p b t", t=4), in_=prod[:, :, :, :], axis=AX.X, op=ALU.add),
                   reads=[bprod], part=[bycv])
                op("dve", lambda e: e.tensor_scalar(out=ycv[:, c, :], in0=ycv[:, c, :], scalar1=bdw[:, l, c:c + 1], scalar2=None, op0=ALU.add),
                   reads=[bycv, bC], part=[bycv])
            op("act", lambda e: e.copy(out=ycb[:, :, :], in_=ycv[:, :, :]), reads=[bycv], part=[bycv])
            op("act", lambda e: e.activation(out=ysq[:, :, :], in_=ycv[:, :, :], func=AF.Square), reads=[bycv], part=[bycv])
            psM, bpsM = psum()
            for c in range(4):
                op("pe", lambda e: e.matmul(psM[:, 0:NS], lhsT=onesm[:, :], rhs=ycb[:, c, :], start=(c == 0), stop=(c == 3)),
                   reads=[bycv, bC], writes=[bpsM] if c == 0 else (), part=() if c == 0 else [bpsM], inc=(c == 3))
            psQ, bpsQ = psum()
            for c in range(4):
                op("pe", lambda e: e.matmul(psQ[:, 0:NS], lhsT=onesm[:, :], rhs=ysq[:, c, :], start=(c == 0), stop=(c == 3)),
                   reads=[bycv, bC], writes=[bpsQ] if c == 0 else (), part=() if c == 0 else [bpsQ], inc=(c == 3))
            op("act", lambda e: e.copy(out=lnS[:, 0, :], in_=psM[:, 0:NS]), reads=[bpsM], writes=[blnS])
            op("act", lambda e: e.activation(out=lnS[:, 1, :], in_=psM[:, 0:NS], func=AF.Square), reads=[bpsM], part=[blnS])
            op("dve", lambda e: e.tensor_tensor(out=lnS[:, 2, :], in0=psQ[:, 0:NS], in1=lnS[:, 1, :], op=ALU.subtract), reads=[bpsQ, blnS], part=[blnS])
            op("act", lambda e: e.activation(out=lnS[:, 2, :], in_=lnS[:, 2, :], func=AF.Sqrt, bias=eps_t[:, :], scale=1.0), reads=[blnS, bC], part=[blnS])
            op("dve", lambda e: e.reciprocal(out=lnS[:, 2, :], in_=lnS[:, 2, :]), reads=[blnS], part=[blnS])
            op("dve", lambda e: e.tensor_tensor(out=t1[:, :, :], in0=ycv[:, :, :], in1=lnS[:, 0, :].unsqueeze(1).to_broadcast([128, 4, NS]), op=ALU.subtract),
               reads=[bycv, blnS], writes=[bt1])
            op("dve", lambda e: e.tensor_tensor(out=t1[:, :, :], in0=t1[:, :, :], in1=lnS[:, 2, :].unsqueeze(1).to_broadcast([128, 4, NS]), op=ALU.mult),
               reads=[bt1, blnS], writes=[bt1])
            for c in range(4):
                op("act", lambda e: e.activation(out=t2[:, c, :], in_=t1[:, c, :], func=AF.Silu, scale=lng[:, l, c:c + 1], bias=lnb[:, l, c:c + 1]),
                   reads=[bt1, bC], part=[bt1])
            op("dve", lambda e: e.tensor_tensor(out=yS[:, 0, :, :], in0=t2[:, :, :], in1=agS[:, :, :], op=ALU.mult), reads=[bt1, bsA], writes=[byS[0]])
            ps, bps = psum()
            for c in range(4):
                op("pe", lambda e: e.transpose(out=ps[0:NS, c * 128:(c + 1) * 128], in_=uS[:, c, :], identity=ident_f[:, :]),
                   reads=[bsA, bC], writes=[bps] if c == 0 else (), part=() if c == 0 else [bps], inc=(c == 3))
            op("act", lambda e: e.copy(out=utok[:, :], in_=ps[0:NS, :]), reads=[bps], writes=[butok])
            for b in range(4):
                dma("sp", convs[l, b, 26:30, :], utok[4 * b:4 * b + 4, :], reads=[butok], own=butok, is_output=True)
            Tk.barrier(); ph.close()

        def sample_hgrn(l):
            U16 = mybir.dt.uint16
            ph = ExitStack()
            a = lambda n_, shp, dt=F32: ph.enter_context(nc.sbuf_tensor(f"{n_}_h{l}", list(shp), dt))
            S0 = a("S0", [128, 4, 4, 128]); bS0 = [B(f"S0{b}") for b in range(4)]
            S0b = [a(f"S0b{i}", [128, 128], BF16) for i in range(2)]; bS0b = [B(f"S0b{i}") for i in range(2)]
            rmS = a("rmS", [128, NS]); bdmk = a("bdmk", [NS, NS], BF16); rowm = a("rowm", [NS, 4]); bhc = B("hconst")
            w = a("w", [128, 8, NS]); bw_ = B("w")
            qh_ = a("qh_", [128, NS], BF16); kt_ = a("kt_", [128, NS], BF16); kh_ = a("kh_", [128, NS], BF16); bqk = B("qk")
            attS_ = a("attS_", [NS, NS], BF16); battS_ = B("attS_")
            khT_ = a("khT_", [NS, 128], BF16); khm = a("khm", [NS, 4, 128], BF16); bkhm = B("khm")
            oS = a("oS", [128, NS]); boS = B("oS"); osq_ = a("osq_", [128, NS], BF16); ors = a("ors", [128, NS]); bors_ = B("ors")
            ebe_ = a("ebe_", [128, 4]); bebe_ = B("ebe_")
            for b in range(4):
                dma("sp", S0[:, b, :, :], shg_in[l, b].rearrange("h d v -> d h v"), writes=[bS0[b]])
            dma("sp", rmS[:, :], cst["rmS"], part=[bhc]); dma("pool", bdmk[:, :], cst["bdm"], part=[bhc]); dma("sp", rowm[:, :], cst["rowmask"], part=[bhc])
            op("pool", lambda e: e.memset(attS_[:, :], 0.0), writes=[battS_])
            srot = 0
            for h in range(4):
                op("dve", lambda e: e.tensor_scalar(out=w[:, 0, :], in0=sigS[:, h, :], scalar1=oml[:, l, h:h + 1], scalar2=lbm[:, l, h:h + 1], op0=ALU.mult, op1=ALU.add),
                   reads=[bsC, blb, bw_], writes=[bw_])
                op("act", lambda e: e.activation(out=w[:, 0, :], in_=w[:, 0, :], func=AF.Ln), reads=[bw_], part=[bw_])
                op("dve", lambda e: e.tensor_scalar(out=w[:, 1, :], in0=sigS[:, h, :], scalar1=noml[:, l, h:h + 1], scalar2=oml[:, l, h:h + 1], op0=ALU.mult, op1=ALU.add),
                   reads=[bsC, blb, bw_], part=[bw_])
                op("dve", lambda e: e.tensor_tensor_scan(out=w[:, 2, :], data0=rmS[:, :], data1=w[:, 0, :], initial=0.0, op0=ALU.mult, op1=ALU.add),
                   reads=[bw_, bhc], part=[bw_])
                op("act", lambda e: e.activation(out=w[:, 3, :], in_=w[:, 2, :], func=AF.Exp), reads=[bw_], part=[bw_])
                op("act", lambda e: e.activation(out=w[:, 4, :], in_=w[:, 2, :], func=AF.Exp, scale=-1.0), reads=[bw_], part=[bw_])
                B3 = w[:, 2, :].rearrange("p (b t) -> p b t", t=4)
                op("dve", lambda e: e.tensor_tensor(out=w[:, 5, :].rearrange("p (b t) -> p b t", t=4), in0=B3[:, :, 3:4].to_broadcast([128, 4, 4]), in1=B3, op=ALU.subtract),
                   reads=[bw_], part=[bw_])
                op("act", lambda e: e.activation(out=w[:, 5, :], in_=w[:, 5, :], func=AF.Exp), reads=[bw_], part=[bw_])
                op("dve", lambda e: e.tensor_copy(out=ebe_[:, :], in_=w[:, 3, :].rearrange("p (b t) -> p b t", t=4)[:, :, 3]), reads=[bw_], writes=[bebe_])
                op("dve", lambda e: e.tensor_tensor(out=qh_[:, :], in0=qcS[:, h, :], in1=w[:, 3, :], op=ALU.mult), reads=[bsC, bw_], writes=[bqk])
                op("dve", lambda e: e.tensor_tensor(out=kt_[:, :], in0=w[:, 1, :], in1=w[:, 4, :], op=ALU.mult), reads=[bw_], part=[bqk])
                op("dve", lambda e: e.tensor_tensor(out=kh_[:, :], in0=w[:, 1, :], in1=w[:, 5, :], op=ALU.mult), reads=[bw_], part=[bqk])
                ps, bps = psum()
                op("pe", lambda e: e.matmul(ps[0:NS, 0:NS], lhsT=kt_[:, :], rhs=qh_[:, :], start=True, stop=True), reads=[bqk], writes=[bps])
                op("dve", lambda e: e.copy_predicated(out=attS_[:, :], mask=bdmk[:, :].bitcast(U16), data=ps[0:NS, 0:NS]), reads=[bps, bhc], part=[battS_])
                ps2, bps2 = psum()
                op("pe", lambda e: e.matmul(ps2[:, 0:NS], lhsT=ciS[:, h, :], rhs=attS_[:, :], start=True, stop=False), reads=[bsC, battS_], writes=[bps2], inc=False)
                for b in range(4):
                    x = srot % 2
                    srot += 1
                    op("act", lambda e: e.copy(out=S0b[x][:, :], in_=S0[:, b, h, :]), reads=[bS0[b]], writes=[bS0b[x]])
                    op("pe", lambda e: e.matmul(ps2[:, 4 * b:4 * b + 4], lhsT=S0b[x][:, :], rhs=qh_[:, 4 * b:4 * b + 4], start=False, stop=(b == 3)),
                       reads=[bS0b[x], bqk], part=[bps2], inc=True)
                op("act", lambda e: e.copy(out=oS[:, :], in_=ps2[:, 0:NS]), reads=[bps2], writes=[boS])
                psT, bpsT = psum()
                pbT = psT[:].bitcast(BF16)
                op("pe", lambda e: e.transpose(out=pbT[0:NS, 0:128], in_=kh_[:, :], identity=ident_b[:, :]), reads=[bqk, bC], writes=[bpsT])
                op("act", lambda e: e.copy(out=khT_[:, :], in_=pbT[0:NS, 0:128]), reads=[bpsT], writes=[bkhm])
                for b in range(4):
                    op("dve", lambda e: e.tensor_scalar(out=khm[:, b, :], in0=khT_[:, :], scalar1=rowm[:, b:b + 1], scalar2=None, op0=ALU.mult), reads=[bkhm, bhc], part=[bkhm])
                for b in range(4):
                    ps3, bps3 = psum()
                    op("pe", lambda e: e.matmul(ps3[:, 0:128], lhsT=khm[:, b, :], rhs=ciS[:, h, :], start=True, stop=True), reads=[bkhm, bsC], writes=[bps3])
                    op("dve", lambda e: e.scalar_tensor_tensor(out=S0[:, b, h, :], in0=S0[:, b, h, :], scalar=ebe_[:, b:b + 1], in1=ps3[:, 0:128], op0=ALU.mult, op1=ALU.add),
                       reads=[bps3, bebe_], part=[bS0[b]])
                op("act", lambda e: e.activation(out=osq_[:, :], in_=oS[:, :], func=AF.Square), reads=[boS], writes=[bors_])
                ps, bps = psum()
                op("pe", lambda e: e.matmul(ps[:, 0:NS], lhsT=ones_b[:, :], rhs=osq_[:, :], start=True, stop=True), reads=[bors_, bC], writes=[bps])
                op("act", lambda e: e.activation(out=ors[:, :], in_=ps[:, 0:NS], func=AF.Sqrt, bias=eps_t[:, :], scale=1.0 / 128), reads=[bps, bC], part=[bors_])
                op("dve", lambda e: e.reciprocal(out=ors[:, :], in_=ors[:, :]), reads=[bors_], part=[bors_])
                op("dve", lambda e: e.tensor_tensor(out=ors[:, :], in0=oS[:, :], in1=ors[:, :], op=ALU.mult), reads=[boS, bors_], part=[bors_])
                op("dve", lambda e: e.tensor_tensor(out=yS[:, 2, h, :], in0=ors[:, :], in1=cgS[:, h, :], op=ALU.mult), reads=[bors_, bsC], part=[byS[2]])
            for b in range(4):
                dma("sp", hgs[l, b].rearrange("h d v -> d h v"), S0[:, b, :, :], reads=[bS0[b]], own=bS0[b], is_output=True)
            Tk.barrier(); ph.close()

        for hf in cfg.get('halves', [0, 1]):
            for l in range(NL):
                if l == 0:
                    for tt in range(NTT):
                        r0 = hf * TS + tt * 128
                        dma("sp", X[:, tt, :], xp[r0:r0 + 128, :], writes=[bX[tt]])
                    if hf == 0 and do_sample:
                        dma("sp", XS[:], xs, writes=[bXS])
                for tt in range(NTT):
                    make_hT(l, X[:, tt, :], 128, bX[tt], tt * 128, bhT[tt], [(0, 128, 0)])
                samp = (hf == 0 and do_sample)
                if samp:
                    make_hT(l, XS[:], NS, bXS, TS, bhT[NTT], [(4 * b, 4 * b + 4, 1 + b) for b in range(4)])
                tiles = [(tt, 128, tt * 128) for tt in range(NTT)] + ([(NTT, NS, TS)] if samp else [])

                phB = ExitStack()
                rot["hi"] = 4

                def sp_(name, shape, dt=F32, _ph=phB):
                    return _ph.enter_context(nc.sbuf_tensor(f"{name}_{hf}{l}", list(shape), dt))

                qtok = sp_("qtok", [128, 512]); bqtok = B("qtok")
                qrope = sp_("qrope", [128, 512], BF16); bqr = B("qrope")
                ostage = sp_("ostage", [128, 320]); bos = B("ostage")
                ktok = sp_("ktok", [128, 448]); bktok = B("ktok")
                kdup = sp_("kdup", [128, 2, 2, 64], BF16); kidup = sp_("kidup", [128, 2, 64], BF16); bkdup = B("kdup")
                qirope = sp_("qirope", [128, 256], BF16); bqir = B("qirope")
                qT = sp_("qT", [128, 4, TS], BF16); bqT = [B(f"qT{i}") for i in range(NTT)]
                qiT = sp_("qiT", [128, 2, TS], BF16)
                wiS = sp_("wiS", [128, NTT, 4])
                bgT = sp_("bgT", [128, 4, TW], BF16); bbg = [B(f"bg{i}") for i in range(3)]
                ybT = sp_("ybT", [128, 4, TW], BF16); byb = [B(f"yb{i}") for i in range(NTT + 1)]
                Isc = sp_("Isc", [128, T]); bI = B("Isc")
                maskb = sp_("maskb", [128, T], BF16); bmk = B("maskb")
                maskT = sp_("maskT", [128, 16, 128], BF16); bmT = B("maskT")
                rl = [qtok] * 2; brl = [bqtok] * 2
                Eb = [sp_(f"Eb{i}", [128, 512], BF16) for i in range(2)]; bEb = [B(f"Eb{i}") for i in range(2)]
                Pb = [sp_(f"Pb{i}", [128, 512], BF16) for i in range(2)]; bPb = [B(f"Pb{i}") for i in range(2)]
                rden = scr[:, 0:512]; brden = bscr
                otmp, botmp = rden, brden
                w2, bw2 = wload(w_in[l][:, C_ZK:C_ZK + 512], 8, 512, key=("b2", hf, l))
                w3, bw3 = wload(w_in[l][:, C_ZKI:C_ZKI + 68], 8, 68, key=("b3", hf, l))
                w1, bw1 = wload(w_in[l][:, C_ZQ:C_ZQ + 512], 8, 512, key=("b1", hf, l))
                for (tt, npt, c0) in tiles:
                    is_s = (tt == NTT)
                    gkb = hf * NTT + tt
                    cosT = cos_s[:, :] if is_s else cos_p[:, gkb, :]
                    sinT = sin_s[:, :] if is_s else sin_p[:, gkb, :]

                    def proj(wt, bwt, ncols):
                        ps, bps = psum()
                        for k in range(8):
                            op("pe", lambda e: e.matmul(ps[0:npt, 0:ncols], lhsT=hT[:, k, c0:c0 + npt], rhs=wt[:, k, 0:ncols],
                                                        start=(k == 0), stop=(k == 7)),
                               reads=[bwt, bhT[tt]], writes=[bps] if k == 0 else (), part=() if k == 0 else [bps], inc=(k == 7))
                        return ps, bps

                    def rms(ps_ap, bps, nh, dst, bdst, gsel, first):
                        n = nh * 64
                        v3 = lambda ap: ap.rearrange("p (h d) -> p h d", d=64)
                        op("act", lambda e: e.activation(out=scr[0:npt, 0:n], in_=ps_ap, func=AF.Square), reads=[bps], writes=[bscr])
                        op("dve", lambda e: e.tensor_reduce(out=st8[0:npt, 8:8 + nh], in_=v3(scr[0:npt, 0:n]), axis=AX.X, op=ALU.add),
                           reads=[bscr], part=[bst8])
                        op("act", lambda e: e.activation(out=st8[0:npt, 16:16 + nh], in_=st8[0:npt, 8:8 + nh], func=AF.Sqrt,
                                                         bias=eps_t[0:npt, :], scale=1.0 / 64), reads=[bst8, bC], part=[bst8])
                        op("dve", lambda e: e.reciprocal(out=st8[0:npt, 24:24 + nh], in_=st8[0:npt, 16:16 + nh]), reads=[bst8], part=[bst8])
                        op("dve", lambda e: e.tensor_tensor(out=v3(dst), in0=v3(ps_ap),
                                                            in1=st8[0:npt, 24:24 + nh].unsqueeze(2).to_broadcast([npt, nh, 64]), op=ALU.mult),
                           reads=[bst8, bps], **(dict(writes=[bdst]) if first else dict(part=[bdst])))
                        if gsel is not None:
                            op("dve", lambda e: e.tensor_tensor(out=v3(dst), in0=v3(dst),
                                                                 in1=gsel[0:npt, l, :].unsqueeze(1).to_broadcast([npt, nh, 64]), op=ALU.mult),
                               reads=[bdst, bC], part=[bdst])

                    ps, bps = proj(w2, bw2, 512)
                    rms(ps[0:npt, 0:128], bps, 2, ktok[0:npt, 0:128], bktok, kg, True)
                    rope(ktok[0:npt, 0:128], bktok, 2, npt, cosT[0:npt], sinT[0:npt], ostage[0:npt, 0:128], bos, eng="dve")
                    op("act", lambda e: e.copy(out=ostage[0:npt, 128:256], in_=ps[0:npt, 128:256]), reads=[bps], part=[bos])
                    if not is_s:
                        op("act", lambda e: e.copy(out=ktok[0:npt, 192:448], in_=ps[0:npt, 256:512]), reads=[bps], part=[bktok])
                    ps3, bps3 = proj(w3, bw3, 68)
                    rms(ps3[0:npt, 0:64], bps3, 1, ktok[0:npt, 128:192], bktok, None, False)
                    rope(ktok[0:npt, 128:192], bktok, 1, npt, cosT[0:npt], sinT[0:npt], ostage[0:npt, 256:320], bos, part_out=True, eng="dve")
                    if not is_s:
                        op("act", lambda e: e.copy(out=wiS[:, tt, :], in_=ps3[:, 64:68]), reads=[bps3], part=[bqT[tt]])
                    if is_s:
                        dma("sp", ks[l], ostage[0:NS, 0:128], reads=[bos], own=bos, is_output=True)
                        dma("sp", vs[l], ostage[0:NS, 128:256], reads=[bos], own=bos, is_output=True)
                        dma("sp", iks[l], ostage[0:NS, 256:320], reads=[bos], own=bos, is_output=True)
                        op("act", lambda e: e.copy(out=sKV[:, :], in_=ostage[0:NS, :]), reads=[bos], writes=[bsQ])
                        op("act", lambda e: e.copy(out=sWI[:, :], in_=ps3[0:NS, 64:68]), reads=[bps3], part=[bsQ])
                        op("act", lambda e: e.copy(out=ktok[0:NS, 192:448], in_=ps[0:NS, 256:512]), reads=[bps], part=[bktok])
                        psq, bpsq = proj(w1, bw1, 512)
                        rms(psq[0:NS, :], bpsq, 8, qtok[0:NS, :], bqtok, qg, True)
                        rope(qtok[0:NS, :], bqtok, 8, NS, cosT, sinT, sQ[:, :], bsQ, part_out=True)
                        rope(ktok[0:NS, 192:448], bktok, 4, NS, cosT, sinT, sQI[:, :], bsQ, part_out=True)
                        continue
                    r0 = hf * TS + tt * 128
                    dma("sp", kp[l, r0:r0 + 128, :], ostage[:, 0:128], reads=[bos], own=bos, is_output=True)
                    dma("sp", vp[l, r0:r0 + 128, :], ostage[:, 128:256], reads=[bos], own=bos, is_output=True)
                    dma("sp", ikp[l, r0:r0 + 128, :], ostage[:, 256:320], reads=[bos], own=bos, is_output=True)
                    if stage <= 1:
                        continue
                    bk = bK(l, gkb)
                    op("pool", lambda e: e.tensor_copy(out=kdup[:, :, :, :],
                                                       in_=ostage[:, 0:128].rearrange("p (g d) -> p g d", d=64).unsqueeze(2).to_broadcast([128, 2, 2, 64])),
                       reads=[bos], writes=[bkdup])
                    op("pool", lambda e: e.tensor_copy(out=kidup[:, :, :], in_=ostage[:, 256:320].unsqueeze(1).to_broadcast([128, 2, 64])),
                       reads=[bos], part=[bkdup])
                    vdst = VpA[:, l, tt, :, :] if hf == 0 else VpB[:, tt, :, :]
                    op("pool", lambda e: e.tensor_copy(out=vdst.rearrange("p g (r d) -> p g r d", d=64),
                                                       in_=ostage[:, 128:256].rearrange("p (g d) -> p g d", d=64).unsqueeze(2).to_broadcast([128, 2, 2, 64])),
                       reads=[bos], writes=[bk])
                    psq, bpsq = proj(w1, bw1, 512)
                    rms(psq[:, :], bpsq, 8, qtok[:, :], bqtok, qg, True)
                    rope(qtok[:, :], bqtok, 8, 128, cosT, sinT, qrope[:, :], bqr, eng="dve")
                    rope(ktok[:, 192:448], bktok, 4, 128, cosT, sinT, qirope[:, :], bqir, eng="dve")
                    psT, bpsT = psum()
                    pb = psT[:].bitcast(BF16)
                    srcs = [qrope[:, c * 128:(c + 1) * 128] for c in range(4)] + [qirope[:, c * 128:(c + 1) * 128] for c in range(2)] \
                        + [kdup[:, g, :, :].rearrange("p r d -> p (r d)") for g in range(2)]
                    rds = [bqr] * 4 + [bqir] * 2 + [bkdup] * 2
                    for si, (sap, rb) in enumerate(zip(srcs, rds)):
                        op("pe", lambda e: e.transpose(out=pb[:, si * 128:(si + 1) * 128], in_=sap, identity=ident_b[:, :]),
                           reads=[rb, bC], writes=[bpsT] if si == 0 else (), part=() if si == 0 else [bpsT], inc=(si == 7))
                    psT2, bpsT2 = psum()
                    pb2 = psT2[:].bitcast(BF16)
                    op("pe", lambda e: e.transpose(out=pb2[:, 0:128], in_=kidup[:, :, :].rearrange("p r d -> p (r d)"), identity=ident_b[:, :]),
                       reads=[bkdup, bC], writes=[bpsT2])
                    tok = slice(tt * 128, (tt + 1) * 128)
                    op("act", lambda e: e.copy(out=qT[:, :, tok], in_=pb[:, 0:512].rearrange("p (c t) -> p c t", t=128)), reads=[bpsT], part=[bqT[tt]])
                    op("act", lambda e: e.copy(out=qiT[:, :, tok], in_=pb[:, 512:768].rearrange("p (c t) -> p c t", t=128)), reads=[bpsT], part=[bqT[tt]])
                    kdst = kdA[:, l, :, tok] if hf == 0 else kdB[:, :, tok]
                    op("act", lambda e: e.copy(out=kdst, in_=pb[:, 768:1024].rearrange("p (g t) -> p g t", t=128)), reads=[bpsT], part=[bk])
                    kidst = kidA[:, l, tok] if hf == 0 else kidB[:, tok]
                    op("act", lambda e: e.copy(out=kidst, in_=pb2[:, 0:128]), reads=[bpsT2], part=[bk])
                if stage <= 1:
                    Tk.barrier(); phB.close()
                    continue

                ntl = [(0, 0, 512, [bhT[i] for i in range(4)]), (1, 512, 512, [bhT[i] for i in range(4, 8)])] + ([(2, TS, NS, [bhT[NTT]])] if samp else [])
                wg, bwg = wload(w_in[l][:, C_BG:C_BG + 512], 8, 512)
                for c in range(4):
                    for (ni, n0, nn, bhs) in ntl:
                        ps, bps = psum()
                        for k in range(8):
                            op("pe", lambda e: e.matmul(ps[:, 0:nn], lhsT=wg[:, k, c * 128:(c + 1) * 128], rhs=hT[:, k, n0:n0 + nn], start=(k == 0), stop=(k == 7)),
                               reads=[bwg] + bhs, writes=[bps] if k == 0 else (), part=() if k == 0 else [bps], inc=(k == 7))
                        bg_dst = bgS[:, c, :] if ni == 2 else bgT[:, c, n0:n0 + nn]
                        op("act", lambda e: e.activation(out=bg_dst, in_=ps[:, 0:nn], func=AF.Silu), reads=[bps], part=[bbgS if ni == 2 else bbg[ni]])

                NIT = 12

                def blk(i):
                    gi = hf * NTT + i
                    return gi, gi + 1, (gi + 1) * 128, slice(i * 128, (i + 1) * 128)

                def stage1a(i):
                    gi, nkb, n, tq = blk(i)
                    if gi >= 2:
                        for kgp in range((nkb + 3) // 4):
                            nb = min(4, nkb - 4 * kgp)
                            ncol = nb * 128
                            kbufs = [bK(l, 4 * kgp + x) for x in range(nb)]
                            for h in range(4):
                                c, j = h // 2, h % 2
                                ps, bps = psum()
                                op("pe", lambda e: e.matmul(ps[:, 0:ncol], lhsT=qiT[j * 64:(j + 1) * 64, c, tq],
                                                            rhs=kid_ap(l, 4 * kgp, nb)[j * 64:(j + 1) * 64, :], start=True, stop=True),
                                   reads=[bqT[i]] + kbufs, writes=[bps])
                                r_, br_ = rl[h % 2], brl[h % 2]
                                op("act", lambda e: e.activation(out=r_[:, 0:ncol], in_=ps[:, 0:ncol], func=AF.Relu), reads=[bps], writes=[br_])
                                dstI = Isc[:, kgp * 512:kgp * 512 + ncol]
                                if h == 0:
                                    op("dve", lambda e: e.tensor_scalar(out=dstI, in0=r_[:, 0:ncol], scalar1=wiS[:, i, 0:1], scalar2=None, op0=ALU.mult),
                                       reads=[br_, bqT[i]], **(dict(writes=[bI]) if kgp == 0 else dict(part=[bI])))
                                else:
                                    op("dve", lambda e: e.scalar_tensor_tensor(out=dstI, in0=r_[:, 0:ncol], scalar=wiS[:, i, h:h + 1], in1=dstI,
                                                                               op0=ALU.mult, op1=ALU.add), reads=[br_, bqT[i]], part=[bI])
                        op("dve", lambda e: e.tensor_reduce(out=bis[:, 0:1], in_=Isc[:, 0:n], axis=AX.X, op=ALU.max, apply_absolute_value=True),
                           reads=[bI], writes=[bbis])
                        op("dve", lambda e: e.tensor_scalar(out=bis[:, 0:1], in0=bis[:, 0:1], scalar1=1.0, scalar2=None, op0=ALU.add), reads=[bbis], part=[bbis])
                        op("dve", lambda e: e.tensor_scalar(out=bis[:, 16:16 + NIT + 2], in0=pw2[:, 0:NIT + 2], scalar1=bis[:, 0:1], scalar2=None, op0=ALU.mult),
                           reads=[bC, bbis], part=[bbis])
                        op("dve", lambda e: e.memset(bis[:, 1:2], 0.0), part=[bbis])
                        op("dve", lambda e: e.tensor_tensor(out=Isc[:, gi * 128:(gi + 1) * 128], in0=Isc[:, gi * 128:(gi + 1) * 128], in1=caus_neg[:, :], op=ALU.add),
                           reads=[bC], part=[bI])
                        for it in range(NIT + 1):
                            if it < NIT:
                                op("dve", lambda e: e.tensor_scalar(out=maskb[:, 0:n], in0=Isc[:, 0:n], scalar1=bis[:, 1:2], scalar2=None, op0=ALU.is_ge,
                                                                    op1=ALU.add, accum_out=bis[:, 2:3]), reads=[bI, bbis], writes=[bmk], part=[bbis])
                                op("dve", lambda e: e.scalar_tensor_tensor(out=bis[:, 3:4], in0=bis[:, 2:3], scalar=256.0, in1=bis[:, 16 + it:17 + it],
                                                                           op0=ALU.is_ge, op1=ALU.mult), reads=[bbis], part=[bbis])
                                op("dve", lambda e: e.scalar_tensor_tensor(out=bis[:, 1:2], in0=bis[:, 3:4], scalar=bis[:, 17 + it:18 + it], in1=bis[:, 1:2],
                                                                           op0=ALU.subtract, op1=ALU.add), reads=[bbis], part=[bbis])
                            else:
                                op("dve", lambda e: e.tensor_tensor(out=bis[:, 4:5], in0=bis[:, 1:2], in1=bis[:, 16 + it:17 + it], op=ALU.subtract), reads=[bbis], part=[bbis])
                        op("dve", lambda e: e.tensor_scalar(out=maskb[:, 0:n], in0=Isc[:, 0:n], scalar1=bis[:, 4:5], scalar2=None, op0=ALU.is_ge),
                           reads=[bI, bbis], writes=[bmk])

                def stage1b(i):
                    gi, nkb, n, tq = blk(i)
                    if gi >= 2:
                        for b8 in range((nkb + 7) // 8):
                            nb = min(8, nkb - 8 * b8)
                            psT, bpsT = psum()
                            pb = psT[:].bitcast(BF16)
                            for x in range(nb):
                                kb = 8 * b8 + x
                                op("pe", lambda e: e.transpose(out=pb[:, x * 128:(x + 1) * 128], in_=maskb[:, kb * 128:(kb + 1) * 128], identity=ident_b[:, :]),
                                   reads=[bmk, bC], writes=[bpsT] if x == 0 else (), part=() if x == 0 else [bpsT], inc=(x == nb - 1))
                            op("act", lambda e: e.copy(out=maskT[:, 8 * b8:8 * b8 + nb, :], in_=pb[:, 0:nb * 128].rearrange("p (k t) -> p k t", t=128)),
                               reads=[bpsT], **(dict(writes=[bmT]) if b8 == 0 else dict(part=[bmT])))
                    else:
                        for kb in range(nkb):
                            src = triT if kb == gi else ones_b
                            op("pool", lambda e: e.tensor_copy(out=maskT[:, kb, :], in_=src[:, :]), reads=[bC],
                               **(dict(writes=[bmT]) if kb == 0 else dict(part=[bmT])))

                def stage2(i):
                    gi, nkb, n, tq = blk(i)
                    steps = [(g, kb) for g in range(2) for kb in range(nkb)]

                    def front(si):
                        g, kb = steps[si]
                        x = si % 2
                        for j in range(2):
                            psS, bpsS = psum()
                            for cc_ in range(2):
                                op("pe", lambda e: e.matmul(psS[:, cc_ * 128:(cc_ + 1) * 128], lhsT=kd_ap(l, kb, g)[j * 64:(j + 1) * 64, :],
                                                            rhs=qT[j * 64:(j + 1) * 64, 2 * g + cc_, tq], start=True, stop=True),
                                   reads=[bK(l, kb), bqT[i]], writes=[bpsS] if cc_ == 0 else (), part=() if cc_ == 0 else [bpsS], inc=(cc_ == 1))
                            op("act", lambda e: e.activation(out=Eb[x][:, j * 256:(j + 1) * 256], in_=psS[:, 0:256], func=AF.Exp, scale=0.125),
                               reads=[bpsS], **(dict(writes=[bEb[x]]) if j == 0 else dict(part=[bEb[x]])))
                        op("pool", lambda e: e.tensor_tensor(out=Pb[x][:, :].rearrange("p (h t) -> p h t", t=128),
                                                             in0=Eb[x][:, :].rearrange("p (h t) -> p h t", t=128),
                                                             in1=maskT[:, kb, :].unsqueeze(1).to_broadcast([128, 4, 128]), op=ALU.mult),
                           reads=[bEb[x], bmT], writes=[bPb[x]])

                    def back(si):
                        g, kb = steps[si]
                        x = si % 2
                        psO, bpsO = PS[4 + g], bPS[4 + g]
                        psD, bpsD = PS[6 + g], bPS[6 + g]
                        op("pe", lambda e: e.matmul(psO[:, :], lhsT=vp_ap(l, kb, g), rhs=Pb[x][:, :], start=(kb == 0), stop=(kb == nkb - 1)),
                           reads=[bK(l, kb), bPb[x]], writes=[bpsO] if kb == 0 else (), part=() if kb == 0 else [bpsO], inc=(kb == nkb - 1))
                        op("pe", lambda e: e.matmul(psD[:, :], lhsT=ones_b[:, :], rhs=Pb[x][:, :], start=(kb == 0), stop=(kb == nkb - 1)),
                           reads=[bC, bPb[x]], writes=[bpsD] if kb == 0 else (), part=() if kb == 0 else [bpsD], inc=(kb == nkb - 1))

                    front(0)
                    for si in range(len(steps)):
                        if si + 1 < len(steps):
                            front(si + 1)
                        back(si)
                    for g in range(2):
                        psO, bpsO = PS[4 + g], bPS[4 + g]
                        psD, bpsD = PS[6 + g], bPS[6 + g]
                        op("act", lambda e: e.activation(out=rden[:, :], in_=psD[:, :], func=AF.Ln), reads=[bpsD], writes=[brden])
                        op("act", lambda e: e.activation(out=rden[:, :], in_=rden[:, :], func=AF.Exp, scale=-1.0), reads=[brden], writes=[brden])
                        op("dve", lambda e: e.tensor_tensor(out=otmp[:, :], in0=psO[:, :], in1=rden[:, :], op=ALU.mult), reads=[bpsO, brden], writes=[botmp])
                        o4 = otmp[:, :].rearrange("p (j c t) -> p c j t", c=2, j=2)
                        for cc_ in range(2):
                            for j in range(2):
                                rows = slice(j * 64, (j + 1) * 64)
                                c = 2 * g + cc_
                                op("pool", lambda e: e.tensor_tensor(out=ybT[rows, c, tq], in0=o4[rows, cc_, j, :], in1=bgT[rows, c, tq], op=ALU.mult),
                                   reads=[botmp, bbg[i // 4]], part=[byb[i]])

                nq_ = cfg.get('nq', NTT)
                if nq_ > 0:
                    stage1a(0); stage1b(0)
                for i in range(nq_):
                    if i + 1 < nq_:
                        stage1a(i + 1)
                    stage2(i)
                    if i + 1 < nq_:
                        stage1b(i + 1)
                if dbg is not None and l == 0:
                    dma("pool", dbg[hf].rearrange("p (c t) -> p c t", c=4), ybT[:, :, :], reads=byb, own=byb[0], is_output=True)
                if stage <= 2:
                    Tk.barrier(); phB.close()
                    continue

                mrot = {"i": 0}

                def merge(yT, ybufs, wproj, gcol, first, sub=None, ysrc=None):
                    sub_ = ntl[0:2] if sub is None else sub
                    for b_ in bmgs + bmtmp:
                        b_.rd.update(bscr.rd); b_.rd.update(bscr.lw)
                    wp, bwp = wload(wproj[l], 4, 1024)
                    for hc in range(2):
                        wg_, bwg_ = wload(w_in[l][:, gcol + hc * 512:gcol + (hc + 1) * 512], 8, 512)
                        for c4 in range(4):
                            c = hc * 4 + c4
                            for (ni, n0, nn, bhs) in sub_:
                                psP, bpsP = psum()
                                for k in range(4):
                                    op("pe", lambda e: e.matmul(psP[:, 0:nn], lhsT=wp[:, k, c * 128:(c + 1) * 128],
                                                                rhs=(ysrc(k) if ysrc is not None else yT[:, k, n0:n0 + nn]),
                                                                start=(k == 0), stop=(k == 3)),
                                       reads=[bwp] + ybufs[ni], writes=[bpsP] if k == 0 else (), part=() if k == 0 else [bpsP], inc=(k == 3))
                                psG, bpsG = psum()
                                for k in range(8):
                                    op("pe", lambda e: e.matmul(psG[:, 0:nn], lhsT=wg_[:, k, c4 * 128:(c4 + 1) * 128], rhs=hT[:, k, n0:n0 + nn],
                                                                start=(k == 0), stop=(k == 7)),
                                       reads=[bwg_] + bhs, writes=[bpsG] if k == 0 else (), part=() if k == 0 else [bpsG], inc=(k == 7))
                                x = mrot["i"] % 2
                                mrot["i"] += 1
                                op("act", lambda e: e.activation(out=mgs[x][:, 0:nn], in_=psG[:, 0:nn], func=AF.Sigmoid), reads=[bpsG], writes=[bmgs[x]])
                                if first:
                                    op("dve", lambda e: e.tensor_tensor(out=mT[:, c, n0:n0 + nn], in0=psP[:, 0:nn], in1=mgs[x][:, 0:nn], op=ALU.mult),
                                       reads=[bpsP, bmgs[x]], part=[bmT_[ni]])
                                else:
                                    op("dve", lambda e: e.tensor_tensor(out=mtmp[x][:, 0:nn], in0=psP[:, 0:nn], in1=mgs[x][:, 0:nn], op=ALU.mult),
                                       reads=[bpsP, bmgs[x]], writes=[bmtmp[x]])
                                    op("pool", lambda e: e.tensor_tensor(out=mT[:, c, n0:n0 + nn], in0=mT[:, c, n0:n0 + nn], in1=mtmp[x][:, 0:nn], op=ALU.add),
                                       reads=[bmtmp[x]], part=[bmT_[ni]])

                ybufs = [byb[0:4], byb[4:8], [byb[NTT]]]
                rot["hi"] = 8
                merge(ybT, ybufs, w_pb, C_GB, True)
                wprefetch(("av", hf, l), w_in[l][:, C_A_VAL:C_A_VAL + 512], 8, 512)
                wprefetch(("agl", hf, l), w_in[l][:, C_A_GLU:C_A_GLU + 512], 8, 512)
                Tk.barrier(); phB.close()

                phA = ExitStack()

                def sa_(name, shape, dt=F32, _ph=phA):
                    return _ph.enter_context(nc.sbuf_tensor(f"{name}_{hf}{l}", list(shape), dt))

                uT = sa_("uT", [128, 4, 30 + TS], BF16); buH = B("uH"); buT = [B("uT0"), B("uT1")]
                agT = sa_("agT", [128, 4, TW], BF16); bag = [B(f"ag{i}") for i in range(3)]
                dg = sa_("dg", [128, 31, 128], BF16); bdg = B("dg")
                yconv = sa_("yconv", [128, 4, TS], BF16); byc = [[B(f"yc{c}{n}") for n in range(2)] for c in range(4)]
                yaT = sa_("yaT", [128, 4, TW], BF16); bya = [B(f"ya{i}") for i in range(3)]
                sgA = [sa_(f"sgA{i}", [128, 512]) for i in range(2)]; bsgA = [B(f"sgA{i}") for i in range(2)]
                sqb = [sa_(f"sqb{i}", [128, 512], BF16) for i in range(2)]; bsqb = [B(f"sqb{i}") for i in range(2)]
                lnm = sa_("lnm", [128, 512]); lnr = sa_("lnr", [128, 512]); lnt = sa_("lnt", [128, 512]); bln = B("ln")
                u32 = sa_("u32", [128, 4, 30]); bu32 = B("u32")
                cvo = sa_("cvo", [30, 512]); bcvo = B("cvo")
                if hf == 0:
                    op("pool", lambda e: e.memset(uT[:, :, 0:30], 0.0), writes=[buH])
                else:
                    op("pool", lambda e: e.tensor_copy(out=uT[:, :, 0:30], in_=uhalo[:, l, :, :]), reads=[buh], writes=[buH])
                wv, bwv = wload(w_in[l][:, C_A_VAL:C_A_VAL + 512], 8, 512, key=("av", hf, l))
                wgl, bwgl = wload(w_in[l][:, C_A_GLU:C_A_GLU + 512], 8, 512, key=("agl", hf, l))
                wag, bwag = wload(w_in[l][:, C_A_GATE:C_A_GATE + 512], 8, 512, key=("ag", hf, l))
                arot = 0
                for c in range(4):
                    for (ni, n0, nn, bhs) in ntl:
                        def mm8(wt, bwt):
                            ps, bps = psum()
                            for k in range(8):
                                op("pe", lambda e: e.matmul(ps[:, 0:nn], lhsT=wt[:, k, c * 128:(c + 1) * 128], rhs=hT[:, k, n0:n0 + nn],
                                                            start=(k == 0), stop=(k == 7)),
                                   reads=[bwt] + bhs, writes=[bps] if k == 0 else (), part=() if k == 0 else [bps], inc=(k == 7))
                            return ps, bps
                        psV, bpsV = mm8(wv, bwv)
                        psG, bpsG = mm8(wgl, bwgl)
                        x = arot % 2
                        arot += 1
                        op("act", lambda e: e.activation(out=sgA[x][:, 0:nn], in_=psG[:, 0:nn], func=AF.Sigmoid), reads=[bpsG], writes=[bsgA[x]])
                        u_dst = uS[:, c, :] if ni == 2 else uT[:, c, 30 + n0:30 + n0 + nn]
                        op("dve", lambda e: e.tensor_tensor(out=u_dst, in0=psV[:, 0:nn], in1=sgA[x][:, 0:nn], op=ALU.mult),
                           reads=[bpsV, bsgA[x]], part=[bsA if ni == 2 else buT[ni]])
                        if hf == 1 and ni == 1:
                            op("dve", lambda e: e.tensor_tensor(out=u32[:, c, :], in0=psV[:, 482:512], in1=sgA[x][:, 482:512], op=ALU.mult),
                               reads=[bpsV, bsgA[x]], part=[bu32])
                        psA, bpsA = mm8(wag, bwag)
                        ag_dst = agS[:, c, :] if ni == 2 else agT[:, c, n0:n0 + nn]
                        op("act", lambda e: e.activation(out=ag_dst, in_=psA[:, 0:nn], func=AF.Silu), reads=[bpsA], part=[bsA if ni == 2 else bag[ni]])
                if hf == 0:
                    op("pool", lambda e: e.tensor_copy(out=uhalo[:, l, :, :], in_=uT[:, :, TS:TS + 30]), reads=[buT[1]], writes=[buh])
                for c in range(4):
                    for j in range(31):
                        if j % 2 == 0:
                            op("act", lambda e: e.activation(out=dg[:, j, :], in_=ident_b[:, :], func=AF.Copy, scale=wdw[:, l, c, j:j + 1]),
                               reads=[bC], **(dict(writes=[bdg]) if j == 0 else dict(part=[bdg])))
                        else:
                            op("dve", lambda e: e.tensor_scalar(out=dg[:, j, :], in0=ident_b[:, :], scalar1=wdw[:, l, c, j:j + 1], scalar2=None, op0=ALU.mult),
                               reads=[bC], part=[bdg])
                    for nt in range(2):
                        ps, bps = psum()
                        rd = [bdg, buT[nt], buT[nt - 1] if nt > 0 else buH]
                        for j in range(31):
                            op("pe", lambda e: e.matmul(ps[:, :], lhsT=dg[:, j, :], rhs=uT[:, c, nt * 512 + j:nt * 512 + j + 512], start=(j == 0), stop=(j == 30)),
                               reads=rd, writes=[bps] if j == 0 else (), part=() if j == 0 else [bps], inc=(j == 30))
                        op("act", lambda e: e.activation(out=yconv[:, c, nt * 512:(nt + 1) * 512], in_=ps[:, :], func=AF.Identity,
                                                         bias=bdw[:, l, c:c + 1], scale=1.0), reads=[bps, bC], writes=[byc[c][nt]])
                for nt in range(2):
                    tk = slice(nt * 512, (nt + 1) * 512)
                    psM, bpsM = psum()
                    for c in range(4):
                        op("pe", lambda e: e.matmul(psM[:, :], lhsT=onesm[:, :], rhs=yconv[:, c, tk], start=(c == 0), stop=(c == 3)),
                           reads=[bC, byc[c][nt]], writes=[bpsM] if c == 0 else (), part=() if c == 0 else [bpsM], inc=(c == 3))
                    psQ, bpsQ = psum()
                    for c in range(4):
                        x = c % 2
                        op("act", lambda e: e.activation(out=sqb[x][:, :], in_=yconv[:, c, tk], func=AF.Square), reads=[byc[c][nt]], writes=[bsqb[x]])
                        op("pe", lambda e: e.matmul(psQ[:, :], lhsT=onesm[:, :], rhs=sqb[x][:, :], start=(c == 0), stop=(c == 3)),
                           reads=[bC, bsqb[x]], writes=[bpsQ] if c == 0 else (), part=() if c == 0 else [bpsQ])
                    op("act", lambda e: e.copy(out=lnm[:, :], in_=psM[:, :]), reads=[bpsM], writes=[bln])
                    op("act", lambda e: e.activation(out=lnt[:, :], in_=psM[:, :], func=AF.Square), reads=[bpsM], part=[bln])
                    op("dve", lambda e: e.tensor_tensor(out=lnr[:, :], in0=psQ[:, :], in1=lnt[:, :], op=ALU.subtract), reads=[bpsQ, bln], part=[bln])
                    op("act", lambda e: e.activation(out=lnr[:, :], in_=lnr[:, :], func=AF.Sqrt, bias=eps_t[:, :], scale=1.0), reads=[bln, bC], part=[bln])
                    op("dve", lambda e: e.reciprocal(out=lnr[:, :], in_=lnr[:, :]), reads=[bln], part=[bln])
                    for c in range(4):
                        x = c % 2
                        op("dve", lambda e: e.tensor_tensor(out=sgA[x][:, :], in0=yconv[:, c, tk], in1=lnm[:, :], op=ALU.subtract),
                           reads=[byc[c][nt], bln], writes=[bsgA[x]])
                        op("pool", lambda e: e.tensor_tensor(out=sgA[x][:, :], in0=sgA[x][:, :], in1=lnr[:, :], op=ALU.mult), reads=[bln], writes=[bsgA[x]])
                        op("act", lambda e: e.activation(out=sqb[x][:, :], in_=sgA[x][:, :], func=AF.Silu, scale=lng[:, l, c:c + 1], bias=lnb[:, l, c:c + 1]),
                           reads=[bsgA[x], bC], writes=[bsqb[x]])
                        op("pool", lambda e: e.tensor_tensor(out=yaT[:, c, tk], in0=sqb[x][:, :], in1=agT[:, c, tk], op=ALU.mult),
                           reads=[bsqb[x], bag[nt]], part=[bya[nt]])
                if hf == 1:
                    ps, bps = psum()
                    for c in range(4):
                        op("pe", lambda e: e.transpose(out=ps[0:30, c * 128:(c + 1) * 128], in_=u32[:, c, :], identity=ident_f[:, :]),
                           reads=[bu32, bC], writes=[bps] if c == 0 else (), part=() if c == 0 else [bps], inc=(c == 3))
                    op("act", lambda e: e.copy(out=cvo[:, :], in_=ps[0:30, :]), reads=[bps], writes=[bcvo])
                    dma("sp", convp[l], cvo[:, :], reads=[bcvo], own=bcvo, is_output=True)
                if dbg is not None and l == 0 and cfg.get("dbgkey") == "p0_ya":
                    dma("pool", dbg[hf].rearrange("p (c t) -> p c t", c=4), yaT[:, :, :], reads=bya, own=bya[0], is_output=True)
                merge(yaT, [[bya[0]], [bya[1]], [bya[2]]], w_pa, C_GA, False)
                Tk.barrier(); phA.close()
                if stage <= 3:
                    continue

                phC = ExitStack()

                def sc_(name, shape, dt=F32, _ph=phC):
                    return _ph.enter_context(nc.sbuf_tensor(f"{name}_{hf}{l}", list(shape), dt))

                ycT = sc_("ycT", [128, 4, TW], BF16); byc_ = [B(f"ycT{i}") for i in range(3)]
                qcL = [sc_(f"qc{i}", [128, TW], BF16) for i in range(2)]; sigL = [sc_(f"sig{i}", [128, TW]) for i in range(2)]
                cghL = [sc_(f"cgh{i}", [128, TW], BF16) for i in range(2)]; ciTL = [sc_(f"ciT{i}", [128, NTT + 1, 128], BF16) for i in range(2)]
                bqcL = [B(f"qc{i}") for i in range(2)]; bsigL = [B(f"sig{i}") for i in range(2)]; bcgL = [B(f"cgh{i}") for i in range(2)]
                bciL = [[B(f"ci{i}_{t}") for t in range(NTT + 1)] for i in range(2)]
                tA = sc_("tA", [128, TS]); btA = B("tA"); tB = sc_("tB", [128, TS]); btB = B("tB")
                B128 = sc_("B128", [128, TS]); bB = B("B128")
                kkf = sc_("kkf", [128, TS], BF16); bkk = B("kkf")
                qt = sc_("qt", [128, TS], BF16); kt = sc_("kt", [128, TS], BF16); qe = sc_("qe", [128, TS], BF16); ke = sc_("ke", [128, TS], BF16)
                qh = sc_("qh", [128, TS], BF16); kh = sc_("kh", [128, TS], BF16)
                bqt, bkt, bqe, bke, bqh, bkh = [B(n_) for n_ in ("qt", "kt", "qe", "ke", "qh", "kh")]
                oT = tB; boT = [btB, btB]
                ebe = sc_("ebe", [128, NTT]); bebe = B("ebe")
                attS = [sc_(f"attS{i}", [128, 128], BF16) for i in range(2)]; battS = [B(f"attS{i}") for i in range(2)]
                khT = [sc_(f"khT{i}", [128, 128], BF16) for i in range(2)]; bkhT = [B(f"khT{i}") for i in range(2)]
                Sbf = [sc_(f"Sbf{i}", [128, 128], BF16) for i in range(2)]; bSbf = [B(f"Sbf{i}") for i in range(2)]
                osq = kt[:, 0:512]; bosq = bkt; orstd = tA[:, 0:512]; bors = btA
                for i_ in range(2):
                    op("pool", lambda e: e.memset(attS[i_][:, :], 0.0), writes=[battS[i_]])
                v3 = lambda ap, n_: ap.rearrange("p (a b) -> p a b", b=n_)
                triU = triT[:, :].bitcast(mybir.dt.uint16)
                srot = 0

                def proj_jobs(h):
                    S_ = h % 2
                    qc, sig, cgh, ciT = qcL[S_], sigL[S_], cghL[S_], ciTL[S_]
                    bqc, bsig, bcg, bci = bqcL[S_], bsigL[S_], bcgL[S_], bciL[S_]
                    kx = wrot["i"] % NWB
                    wrot["i"] += 1
                    wh = WB[kx][:, :].rearrange("p (k n) -> p k n", n=512)
                    bwh = bWB[kx]
                    jobs = []

                    def jw():
                        dma("pool", wh[:, :, :], w_c[l][:, h, :].rearrange("(k p) n -> p k n", p=128), writes=[bwh], nobar=True)
                    for (ni, n0, nn, bhs) in ntl:
                        for gi_, func, dst, bdst in ((0, AF.Silu, qc, bqc), (1, AF.Sigmoid, sig, bsig), (3, AF.Silu, cgh, bcg)):
                            def jf(ni=ni, n0=n0, nn=nn, bhs=bhs, gi_=gi_, func=func, dst=dst, bdst=bdst):
                                ps, bps = psum()
                                for k in range(8):
                                    op("pe", lambda e: e.matmul(ps[:, 0:nn], lhsT=wh[:, k, gi_ * 128:(gi_ + 1) * 128], rhs=hT[:, k, n0:n0 + nn], start=(k == 0), stop=(k == 7)),
                                       reads=[bwh] + bhs, writes=[bps] if k == 0 else (), part=() if k == 0 else [bps], inc=(k == 7))
                                if ni == 2:
                                    sdst = {0: qcS, 1: sigS, 3: cgS}[gi_]
                                    op("act", lambda e: e.activation(out=sdst[:, h, :], in_=ps[:, 0:nn], func=func), reads=[bps], part=[bsC])
                                else:
                                    op("act", lambda e: e.activation(out=dst[:, n0:n0 + nn], in_=ps[:, 0:nn], func=func), reads=[bps],
                                       **(dict(writes=[bdst]) if ni == 0 else dict(part=[bdst])))
                            jobs.append(jf)
                    for (tt, npt, c0) in tiles:
                        def jc(tt=tt, npt=npt, c0=c0):
                            ps, bps = psum()
                            for k in range(8):
                                op("pe", lambda e: e.matmul(ps[0:npt, 0:128], lhsT=hT[:, k, c0:c0 + npt], rhs=wh[:, k, 256:384], start=(k == 0), stop=(k == 7)),
                                   reads=[bwh, bhT[tt]], writes=[bps] if k == 0 else (), part=() if k == 0 else [bps], inc=(k == 7))
                            if tt == NTT:
                                op("act", lambda e: e.copy(out=ciS[:, h, :], in_=ps[0:npt, 0:128]), reads=[bps], part=[bsC])
                            else:
                                op("act", lambda e: e.copy(out=ciT[0:npt, tt, :], in_=ps[0:npt, 0:128]), reads=[bps], writes=[bci[tt]])
                        jobs.append(jc)

                    def jg():
                        op("pool", lambda e: e.tensor_scalar(out=cgh[:, 0:TS], in0=cgh[:, 0:TS], scalar1=cng[:, l:l + 1], scalar2=None, op0=ALU.mult), reads=[blb], writes=[bcg])
                        if samp:
                            op("pool", lambda e: e.tensor_scalar(out=cgS[:, h, :], in0=cgS[:, h, :], scalar1=cng[:, l:l + 1], scalar2=None, op0=ALU.mult), reads=[blb], part=[bsC])
                    jobs.append(jg)
                    return jw, jobs

                PJ = [proj_jobs(h_) for h_ in range(4)]
                PJ[0][0](); PJ[1][0]()
                pending = PJ[0][1]
                for h in range(4):
                    if hf == 0:
                        op("pool", lambda e: e.memset(Sst[:, l, h, :], 0.0), writes=[bS[l][h]])
                    for jb in pending:
                        jb()
                    if h + 2 < 4:
                        PJ[h + 2][0]()
                    pending = list(PJ[h + 1][1]) if h < 3 else []
                    S_ = h % 2
                    qc, sig, cgh, ciT = qcL[S_], sigL[S_], cghL[S_], ciTL[S_]
                    bqc, bsig, bcg, bci = bqcL[S_], bsigL[S_], bcgL[S_], bciL[S_]
                    npre = (len(pending) * 2) // 5
                    for jb in pending[:npre]:
                        jb()
                    pending = pending[npre:]
                    P_ = slice(0, TS)
                    op("dve", lambda e: e.tensor_scalar(out=tA[:, :], in0=sig[:, P_], scalar1=oml[:, l, h:h + 1], scalar2=lbm[:, l, h:h + 1], op0=ALU.mult, op1=ALU.add),
                       reads=[bsig, blb], writes=[btA])
                    op("act", lambda e: e.activation(out=tA[:, :], in_=tA[:, :], func=AF.Ln), reads=[btA], writes=[btA])
                    op("dve", lambda e: e.tensor_scalar(out=kkf[:, :], in0=sig[:, P_], scalar1=noml[:, l, h:h + 1], scalar2=oml[:, l, h:h + 1], op0=ALU.mult, op1=ALU.add),
                       reads=[bsig, blb], writes=[bkk])
                    for tt in range(NTT):
                        tk = slice(tt * 128, (tt + 1) * 128)
                        op("dve", lambda e: e.tensor_tensor_scan(out=B128[:, tk], data0=ones_f[:, :], data1=tA[:, tk], initial=0.0, op0=ALU.mult, op1=ALU.add),
                           reads=[btA, bC], **(dict(writes=[bB]) if tt == 0 else dict(part=[bB])))
                    B3 = v3(B128[:, :], 128)
                    op("act", lambda e: e.activation(out=tB[:, :], in_=B128[:, :], func=AF.Exp), reads=[bB], writes=[btB])
                    op("pool", lambda e: e.tensor_tensor(out=qh[:, :], in0=qc[:, P_], in1=tB[:, :], op=ALU.mult), reads=[bqc, btB], writes=[bqh])
                    op("dve", lambda e: e.tensor_copy(out=ebe[:, :], in_=v3(tB[:, :], 128)[:, :, 127]), reads=[btB], writes=[bebe])
                    op("dve", lambda e: e.tensor_tensor(out=v3(tA[:, :], 128), in0=B3[:, :, 127:128].to_broadcast([128, NTT, 128]), in1=B3, op=ALU.subtract),
                       reads=[bB], writes=[btA])
                    op("act", lambda e: e.activation(out=tA[:, :], in_=tA[:, :], func=AF.Exp), reads=[btA], writes=[btA])
                    op("pool", lambda e: e.tensor_tensor(out=kh[:, :], in0=kkf[:, :], in1=tA[:, :], op=ALU.mult), reads=[bkk, btA], writes=[bkh])
                    B64 = v3(B128[:, :], 64)
                    op("dve", lambda e: e.tensor_tensor(out=v3(tB[:, :], 64), in0=B64, in1=B64[:, :, 31:32].to_broadcast([128, 2 * NTT, 64]), op=ALU.subtract),
                       reads=[bB], writes=[btB])
                    op("act", lambda e: e.activation(out=tA[:, :], in_=tB[:, :], func=AF.Exp), reads=[btB], writes=[btA])
                    op("pool", lambda e: e.tensor_tensor(out=qt[:, :], in0=qc[:, P_], in1=tA[:, :], op=ALU.mult), reads=[bqc, btA], writes=[bqt])
                    op("act", lambda e: e.activation(out=tA[:, :], in_=tB[:, :], func=AF.Exp, scale=-1.0), reads=[btB], writes=[btA])
                    op("pool", lambda e: e.tensor_tensor(out=kt[:, :], in0=kkf[:, :], in1=tA[:, :], op=ALU.mult), reads=[bkk, btA], writes=[bkt])
                    op("dve", lambda e: e.tensor_tensor(out=v3(tB[:, :], 128), in0=B3, in1=B3[:, :, 63:64].to_broadcast([128, NTT, 128]), op=ALU.subtract),
                       reads=[bB], writes=[btB])
                    op("act", lambda e: e.activation(out=tA[:, :], in_=tB[:, :], func=AF.Exp), reads=[btB], writes=[btA])
                    op("pool", lambda e: e.tensor_tensor(out=qe[:, :], in0=qc[:, P_], in1=tA[:, :], op=ALU.mult), reads=[bqc, btA], writes=[bqe])
                    op("act", lambda e: e.activation(out=tA[:, :], in_=tB[:, :], func=AF.Exp, scale=-1.0), reads=[btB], writes=[btA])
                    op("pool", lambda e: e.tensor_tensor(out=ke[:, :], in0=kkf[:, :], in1=tA[:, :], op=ALU.mult), reads=[bkk, btA], writes=[bke])
                    if dbg is not None and cfg.get("dbgkey") == "B128" and h == 0 and l == 0:
                        dma("sp", dbg[hf][:, 0:TS], B128[:, :], reads=[bB], own=bB, is_output=True)
                        dma("sp", dbg[hf][:, TS:2 * TS], sig[:, 0:TS], reads=[bsig], own=bsig, is_output=True)
                    for tt in range(NTT):
                        a0, a1, b1 = tt * 128, tt * 128 + 64, tt * 128 + 128
                        x = srot % 2
                        srot += 1
                        ps, bps = psum()
                        op("pe", lambda e: e.matmul(ps[0:64, 0:64], lhsT=kt[:, a0:a1], rhs=qt[:, a0:a1], start=True, stop=True), reads=[bkt, bqt], writes=[bps], inc=False)
                        op("pe", lambda e: e.matmul(ps[0:64, 64:128], lhsT=ke[:, a0:a1], rhs=qe[:, a1:b1], start=True, stop=True), reads=[bke, bqe], part=[bps], inc=False)
                        op("pe", lambda e: e.matmul(ps[64:128, 64:128], lhsT=kt[:, a1:b1], rhs=qt[:, a1:b1], start=True, stop=True), reads=[bkt, bqt], part=[bps])
                        op("dve", lambda e: e.copy_predicated(out=attS[x][0:64, :], mask=triU[0:64, :], data=ps[0:64, 0:128]), reads=[bps, bC], part=[battS[x]])
                        op("dve", lambda e: e.copy_predicated(out=attS[x][64:128, 64:128], mask=triU[64:128, 64:128], data=ps[64:128, 64:128]),
                           reads=[bps, bC], part=[battS[x]])
                        if tt == 0 and hf == 0:
                            op("act", lambda e: e.copy(out=Sbf[x][:, :], in_=Sst[:, l, h, :]), reads=[bS[l][h]], writes=[bSbf[x]])
                        elif tt == 0:
                            op("act", lambda e: e.copy(out=Sbf[x][:, :], in_=Sst[:, l, h, :]), reads=[bS[l][h]], writes=[bSbf[x]])
                        ps2, bps2 = psum()
                        op("pe", lambda e: e.matmul(ps2[:, 0:128], lhsT=ciT[:, tt, :], rhs=attS[x][:, :], start=True, stop=False), reads=[bci[tt], battS[x]], writes=[bps2], inc=False)
                        op("pe", lambda e: e.matmul(ps2[:, 0:128], lhsT=Sbf[x][:, :], rhs=qh[:, a0:b1], start=False, stop=True), reads=[bSbf[x], bqh], part=[bps2])
                        op("act", lambda e: e.copy(out=oT[:, a0:b1], in_=ps2[:, 0:128]), reads=[bps2], part=[boT[tt // 4]])
                        psT, bpsT = psum()
                        pbT = psT[:].bitcast(BF16)
                        op("pe", lambda e: e.transpose(out=pbT[:, 0:128], in_=kh[:, a0:b1], identity=ident_b[:, :]), reads=[bkh, bC], writes=[bpsT])
                        op("act", lambda e: e.copy(out=khT[x][:, :], in_=pbT[:, 0:128]), reads=[bpsT], writes=[bkhT[x]])
                        ps3, bps3 = psum()
                        op("pe", lambda e: e.matmul(ps3[:, 0:128], lhsT=khT[x][:, :], rhs=ciT[:, tt, :], start=True, stop=True), reads=[bkhT[x], bci[tt]], writes=[bps3])
                        op("dve", lambda e: e.scalar_tensor_tensor(out=Sst[:, l, h, :], in0=Sst[:, l, h, :], scalar=ebe[:, tt:tt + 1], in1=ps3[:, 0:128],
                                                                   op0=ALU.mult, op1=ALU.add), reads=[bps3, bebe], writes=[bS[l][h]])
                        if tt < NTT - 1:
                            y_ = srot % 2
                            op("act", lambda e: e.copy(out=Sbf[y_][:, :], in_=Sst[:, l, h, :]), reads=[bS[l][h]], writes=[bSbf[y_]])
                        for _ in range(2):
                            if pending:
                                pending.pop(0)()
                    for nt in range(2):
                        tk = slice(nt * 512, (nt + 1) * 512)
                        op("act", lambda e: e.activation(out=osq[:, :], in_=oT[:, tk], func=AF.Square), reads=[boT[nt]], writes=[bosq])
                        ps, bps = psum()
                        op("pe", lambda e: e.matmul(ps[:, :], lhsT=ones_b[:, :], rhs=osq[:, :], start=True, stop=True), reads=[bosq, bC], writes=[bps])
                        op("act", lambda e: e.activation(out=orstd[:, :], in_=ps[:, :], func=AF.Sqrt, bias=eps_t[:, :], scale=1.0 / 128), reads=[bps, bC], writes=[bors])
                        op("dve", lambda e: e.reciprocal(out=orstd[:, :], in_=orstd[:, :]), reads=[bors], writes=[bors])
                        op("dve", lambda e: e.tensor_tensor(out=orstd[:, :], in0=oT[:, tk], in1=orstd[:, :], op=ALU.mult), reads=[boT[nt]], writes=[bors])
                        op("pool", lambda e: e.tensor_tensor(out=ycT[:, h, tk], in0=orstd[:, :], in1=cgh[:, tk], op=ALU.mult), reads=[bors, bcg], part=[byc_[nt]])
                if hf == 1:
                    dma("sp", hgp[l].rearrange("h d v -> d h v"), Sst[:, l, :, :], reads=bS[l], own=bS[l][0], is_output=True)
                if dbg is not None and l == 0 and cfg.get("dbgkey") == "p0_yc":
                    dma("pool", dbg[hf].rearrange("p (c t) -> p c t", c=4), ycT[:, :, :], reads=byc_, own=byc_[0], is_output=True)
                merge(ycT, [[byc_[0]], [byc_[1]], [byc_[2]]], w_pc, C_GC, False)
                wprefetch(("wo", 0, hf, l), w_out[l][:, 0:512], 8, 512)
                wprefetch(("wo", 1, hf, l), w_out[l][:, 512:1024], 8, 512)
                Tk.barrier(); phC.close()

                phF = ExitStack()

                def sf_(name, shape, dt=F32, _ph=phF):
                    return _ph.enter_context(nc.sbuf_tensor(f"{name}_{hf}{l}", list(shape), dt))

                gbc = sf_("gbc", [128, D]); bgbc = B("gbc")
                dgf = [sf_(f"dgf{i}", [128, 128]) for i in range(2)]; bdgf = [B(f"dgf{i}") for i in range(2)]
                ftmp = [sf_(f"ftmp{i}", [128, 512]) for i in range(2)]; bft = [B(f"ftmp{i}") for i in range(2)]
                for c in range(8):
                    x = c % 2
                    op("pool", lambda e: e.tensor_scalar(out=dgf[x][:, :], in0=ident_f[:, :], scalar1=modT[:, l, 16 + c, 0:1], scalar2=None, op0=ALU.mult),
                       reads=[bC, bmod], writes=[bdgf[x]])
                    ps, bps = psum()
                    op("pe", lambda e: e.matmul(ps[:, 0:128], lhsT=ones_f[:, :], rhs=dgf[x][:, :], start=True, stop=True), reads=[bC, bdgf[x]], writes=[bps])
                    op("act", lambda e: e.copy(out=gbc[:, c * 128:(c + 1) * 128], in_=ps[:, 0:128]), reads=[bps], part=[bgbc])
                frot = 0
                for hc in range(2):
                    wo, bwo = wload(w_out[l][:, hc * 512:(hc + 1) * 512], 8, 512, key=("wo", hc, hf, l))
                    for tt in range(NTT):
                        ps, bps = psum()
                        for k in range(8):
                            op("pe", lambda e: e.matmul(ps[:, :], lhsT=mT[:, k, tt * 128:(tt + 1) * 128], rhs=wo[:, k, :], start=(k == 0), stop=(k == 7)),
                               reads=[bwo, bmT_[tt // 4]], writes=[bps] if k == 0 else (), part=() if k == 0 else [bps], inc=(k == 7))
                        x = frot % 2
                        frot += 1
                        op("dve", lambda e: e.tensor_tensor(out=ftmp[x][:, :], in0=ps[:, :], in1=gbc[:, hc * 512:(hc + 1) * 512], op=ALU.mult),
                           reads=[bps, bgbc], writes=[bft[x]])
                        op("pool", lambda e: e.tensor_tensor(out=X[:, tt, hc * 512:(hc + 1) * 512], in0=X[:, tt, hc * 512:(hc + 1) * 512], in1=ftmp[x][:, :], op=ALU.add),
                           reads=[bft[x]], writes=[bX[tt]])
                if l == NL - 1:
                    for tt in range(NTT):
                        r0 = hf * TS + tt * 128
                        dma("sp", yp[r0:r0 + 128, :], X[:, tt, :], reads=[bX[tt]], own=bX[tt], is_output=True)
                if not samp:
                    nxt = {(0, 0): (0, 1), (0, 1): (1, 0), (1, 0): (1, 1)}.get((hf, l)) if NL == 2 else None
                    if nxt is not None and nxt[0] in cfg.get('halves', [0, 1]):
                        wprefetch(("b2",) + nxt, w_in[nxt[1]][:, C_ZK:C_ZK + 512], 8, 512)
                        wprefetch(("b3",) + nxt, w_in[nxt[1]][:, C_ZKI:C_ZKI + 68], 8, 68)
                Tk.barrier(); phF.close()
                if samp:
                    sample_dsa(l)
                    if dbg is not None and cfg.get("dbgkey") == "s_yb" and l == 0:
                        dma("pool", dbg[0][:, 0:64].rearrange("p (c t) -> p c t", c=4), yS[:, 1, :, :], reads=[byS[1]], own=byS[1], is_output=True)
                    if cfg.get("sstage", 9) <= 1:
                        continue
                    sample_conv(l)
                    sample_hgrn(l)
                    stl = [ntl[2]]
                    merge(None, [None, None, [byS[1]]], w_pb, C_GB, True, sub=stl, ysrc=lambda k: yS[:, 1, k, :])
                    merge(None, [None, None, [byS[0]]], w_pa, C_GA, False, sub=stl, ysrc=lambda k: yS[:, 0, k, :])
                    merge(None, [None, None, [byS[2]]], w_pc, C_GC, False, sub=stl, ysrc=lambda k: yS[:, 2, k, :])
                    phG = ExitStack()
                    ag_ = lambda n_, shp, dt=F32: phG.enter_context(nc.sbuf_tensor(f"{n_}_g{l}", list(shp), dt))
                    sel16 = ag_("sel16", [128, 4, NS]); gbs = ag_("gbs", [NS, D]); bgbs = B("gbs"); bsel = B("sel16")
                    dgs = [ag_(f"dgs{i}", [128, 128]) for i in range(2)]; bdgs = [B(f"dgs{i}") for i in range(2)]
                    fts = ag_("fts", [NS, 512]); bfts = B("fts")
                    dma("sp", sel16[:], cst["sel16"], writes=[bsel])
                    grot = 0
                    for c in range(8):
                        ps, bps = psum()
                        for b in range(4):
                            x = grot % 2
                            grot += 1
                            op("pool", lambda e: e.tensor_scalar(out=dgs[x][:, :], in0=ident_f[:, :], scalar1=modT[:, l, 16 + c, 1 + b:2 + b], scalar2=None, op0=ALU.mult),
                               reads=[bC, bmod], writes=[bdgs[x]])
                            op("pe", lambda e: e.matmul(ps[0:NS, 0:128], lhsT=sel16[:, b, :], rhs=dgs[x][:, :], start=(b == 0), stop=(b == 3)),
                               reads=[bsel, bdgs[x]], writes=[bps] if b == 0 else (), part=() if b == 0 else [bps], inc=True)
                        op("act", lambda e: e.copy(out=gbs[:, c * 128:(c + 1) * 128], in_=ps[0:NS, 0:128]), reads=[bps], part=[bgbs])
                    for hc in range(2):
                        wo, bwo = wload(w_out[l][:, hc * 512:(hc + 1) * 512], 8, 512)
                        ps, bps = psum()
                        for k in range(8):
                            op("pe", lambda e: e.matmul(ps[0:NS, :], lhsT=mT[:, k, TS:TS + NS], rhs=wo[:, k, :], start=(k == 0), stop=(k == 7)),
                               reads=[bwo, bmT_[2]], writes=[bps] if k == 0 else (), part=() if k == 0 else [bps], inc=(k == 7))
                        op("dve", lambda e: e.tensor_tensor(out=fts[:, :], in0=ps[0:NS, :], in1=gbs[:, hc * 512:(hc + 1) * 512], op=ALU.mult), reads=[bps, bgbs], writes=[bfts])
                        op("dve", lambda e: e.tensor_tensor(out=XS[:, hc * 512:(hc + 1) * 512], in0=XS[:, hc * 512:(hc + 1) * 512], in1=fts[:, :], op=ALU.add),
                           reads=[bfts], writes=[bXS])
                    if l == NL - 1:
                        dma("sp", ys, XS[:, :], reads=[bXS], own=bXS, is_output=True)
                    Tk.barrier(); phG.close()
        Tk.finish()
        print("instructions:", Tk.ninstr, "semaphores:", Tk.nsem)
    return nc


def make_in_maps(inp, cores, cfg):
    consts = _consts()
    f = lambda a: np.ascontiguousarray(np.asarray(a), dtype=np.float32)
    w_ada = f(inp["w_ada"]); b_ada = f(inp["b_ada"])
    shared = {
        "w_ada": w_ada,
        "b_adaT": np.ascontiguousarray(b_ada.reshape(DEPTH, 24, 128).transpose(0, 2, 1)),
        "b_ada_g": np.ascontiguousarray(b_ada[:, None, 2 * D:]),
        "norm_gT": np.ascontiguousarray(f(inp["norm_g"]).reshape(DEPTH, 8, 128).transpose(0, 2, 1)),
        "w_in": f(inp["w_in"]),
        "qg_bc": f(inp["q_norm_g"])[:, None, :],
        "kg_bc": f(inp["k_norm_g"])[:, None, :],
        "w_dwT": np.ascontiguousarray(f(inp["w_dw"]).reshape(DEPTH, 31, 4, 128).transpose(0, 3, 2, 1)),
        "b_dwT": np.ascontiguousarray(f(inp["b_dw"]).reshape(DEPTH, 4, 128).transpose(0, 2, 1)),
        "ln_gT": np.ascontiguousarray(f(inp["ln_g"]).reshape(DEPTH, 4, 128).transpose(0, 2, 1)),
        "ln_bT": np.ascontiguousarray(f(inp["ln_b"]).reshape(DEPTH, 4, 128).transpose(0, 2, 1)),
        "lbT": np.ascontiguousarray(f(inp["lb_logits"]).reshape(DEPTH, 4, 128).transpose(0, 2, 1)),
        "cng": np.ascontiguousarray(f(inp["c_norm_g"])[:, :, None]),
        "w_c": np.ascontiguousarray(np.stack([f(inp["w_in"])[:, :, c0:c0 + 512].reshape(DEPTH, D, 4, 128) for c0 in (C_CQ, C_CF, C_CI, C_CG)], axis=3)
                                    .reshape(DEPTH, D, 4, 512)),
        "w_pa": f(inp["w_proj_a"]), "w_pb": f(inp["w_proj_b"]), "w_pc": f(inp["w_proj_c"]), "w_out": f(inp["w_out"]),
    }
    for k, v in consts.items():
        shared["c_" + k] = v
    maps = []
    xp = np.asarray(inp["x_prompt"]); xs = np.asarray(inp["x_sample"])
    cp = np.asarray(inp["c_prompt"]); cs = np.asarray(inp["c_sample"])
    do_sample = cfg.get("sample", True)
    if do_sample:
        for i in range(DEPTH):
            shared[f"ck{i}"] = f(inp["cache_k"][i]).reshape(NPOOL * 128, 128)
            shared[f"cv{i}"] = f(inp["cache_v"][i]).reshape(NPOOL * 128, 128)
            shared[f"cik{i}"] = f(inp["cache_idx_k"][i]).reshape(NPOOL * 128, 64)
        pt = np.asarray(inp["page_table"]).astype(np.int32)
        sconv = f(inp["state_conv"]); shg = f(inp["state_hgrn"])
    for c in cores:
        m = dict(shared)
        if do_sample:
            ptc = pt[4 * c:4 * c + 4].reshape(4, 8, 8)
            m["ptx"] = np.ascontiguousarray(np.repeat(ptc.transpose(2, 0, 1), 16, axis=0)).astype(np.int32)
            m["sconv"] = np.ascontiguousarray(sconv[:, 4 * c:4 * c + 4])
            m["shg"] = np.ascontiguousarray(shg[:, 4 * c:4 * c + 4])
        m["xp"] = f(xp[c])
        m["xs"] = f(xs[4 * c:4 * c + 4].reshape(NS, D))
        m["cc"] = f(np.concatenate([cp[c:c + 1], cs[4 * c:4 * c + 4]], axis=0))
        maps.append(m)
    return maps


def kernel(**inp):
    cfg = {}
    nc = build(cfg)
    cores = list(range(8))
    res = run_bass_kernel_spmd(nc, make_in_maps(inp, cores, cfg), core_ids=cores)
    R = res.results
    g = lambda k: [np.asarray(r[k], dtype=np.float32) for r in R]
    z = lambda k, shp: [np.asarray(r[k], dtype=np.float32) if k in r else np.zeros(shp, np.float32) for r in R]
    y_prompt = np.stack(g("yp"), 0)
    y_sample = np.concatenate([a.reshape(4, 4, D) for a in z("ys", (NS, D))], 0)
    k_prompt = np.stack(g("kp"), 1).reshape(DEPTH, 8, T, 2, 64)
    v_prompt = np.stack(g("vp"), 1).reshape(DEPTH, 8, T, 2, 64)
    idxk_prompt = np.stack(g("ikp"), 1)
    conv_prompt = np.stack(g("convp"), 1)
    hgrn_prompt = np.stack(g("hgp"), 1)
    k_sample = np.concatenate([a.reshape(DEPTH, 4, 4, 2, 64) for a in g("ks")], 1)
    v_sample = np.concatenate([a.reshape(DEPTH, 4, 4, 2, 64) for a in g("vs")], 1)
    idxk_sample = np.concatenate([a.reshape(DEPTH, 4, 4, 64) for a in g("iks")], 1)
    conv_sample = np.concatenate(z("convs", (DEPTH, 4, 30, 512)), 1)
    hgrn_sample = np.concatenate(z("hgs", (DEPTH, 4, 4, 128, 128)), 1)
    return (y_prompt, y_sample, k_prompt, v_prompt, idxk_prompt, conv_prompt, hgrn_prompt,
            k_sample, v_sample, idxk_sample, conv_sample, hgrn_sample)
```
